# Optimizing a Trainium2 kernel written in Bass

```python
import math
import jax, jax.numpy as jnp
from jax import lax
import numpy as np

D_MODEL = 1024
BATCH = 8
SEQ = 2048
DEPTH = 1
DEC_BATCH = 128
DEC_SEQ = 4
PAST_LEN = 16384
PAGE_SIZE = 128

RET_HEADS = 4
RET_DK = 128
RET_DV = 128
RET_W = RET_HEADS * RET_DV
RET_CHUNK = 64
HG_HEADS = 4
HG_DK = 128
HG_DV = 128
HG_W = HG_HEADS * HG_DV
HG_CHUNK = 32
MIX_W = RET_W + HG_W
IN_W = 2 * RET_HEADS * RET_DK + 2 * RET_W + 2 * HG_HEADS * HG_DK + 2 * HG_W
N_MEM = 256
XA_HEADS = 4
XA_HD = D_MODEL // XA_HEADS
D_FF = -(-8 * D_MODEL // (3 * 256)) * 256
ROPE_BASE = 10000.0
EPS = 1e-6

kernel_name = 'hymba_retention_hgrn2_memxattn_step'


def _rmsnorm(x, g):
    xf = x.astype(jnp.float32)
    y = xf * lax.rsqrt(jnp.mean(xf * xf, axis=-1, keepdims=True) + EPS)
    return (y * g.astype(jnp.float32)).astype(x.dtype)


def _split_heads(x, n_heads):
    b, t, _ = x.shape
    return x.reshape(b, t, n_heads, -1).transpose(0, 2, 1, 3).astype(jnp.float32)


def _merge_heads(x):
    b, h, t, d = x.shape
    return x.transpose(0, 2, 1, 3).reshape(b, t, h * d)


def _head_rmsnorm(o):
    return o * lax.rsqrt(jnp.mean(o * o, axis=-1, keepdims=True) + EPS)


def _rotary(x, pos):
    half = x.shape[-1] // 2
    inv_freq = ROPE_BASE ** (-jnp.arange(half, dtype=jnp.float32) / half)
    ang = pos[:, None] * inv_freq[None, :]
    cos, sin = jnp.cos(ang), jnp.sin(ang)
    x1, x2 = x[..., :half], x[..., half:]
    return jnp.concatenate([x1 * cos - x2 * sin, x1 * sin + x2 * cos], axis=-1)


def _to_chunks(x, L):
    b, h, t, d = x.shape
    return x.reshape(b, h, t // L, L, d).transpose(2, 0, 1, 3, 4)


def _from_chunks(x):
    n, b, h, L, d = x.shape
    return x.transpose(1, 2, 0, 3, 4).reshape(b, h, n * L, d)


def _retention(q, k, v, s0):
    t = q.shape[2]
    L = math.gcd(t, RET_CHUNK)
    log_g = jnp.log(1.0 - 2.0 ** (-5.0 - jnp.arange(RET_HEADS, dtype=jnp.float32)))
    idx = jnp.arange(L, dtype=jnp.float32)
    rel = idx[:, None] - idx[None, :]
    causal = rel >= 0
    decay = jnp.where(causal, jnp.exp(log_g[:, None, None] * jnp.where(causal, rel, 0.0)), 0.0)
    q_dec = jnp.exp(log_g[:, None] * (idx + 1.0))[..., None]
    k_dec = jnp.exp(log_g[:, None] * (L - 1.0 - idx))[..., None]
    s_dec = jnp.exp(log_g * L)[:, None, None]

    def step(s, chunk):
        qc, kc, vc = chunk
        att = jnp.einsum('bhld,bhmd->bhlm', qc, kc) * decay
        o = jnp.einsum('bhlm,bhme->bhle', att, vc) + jnp.einsum('bhld,bhde->bhle', qc * q_dec, s)
        s = s_dec * s + jnp.einsum('bhld,bhle->bhde', kc * k_dec, vc)
        return s, o

    s, o = lax.scan(step, s0, (_to_chunks(q, L), _to_chunks(k, L), _to_chunks(v, L)))
    return _from_chunks(o), s


def _hgrn2(q, k, v, log_f, s0):
    t = q.shape[2]
    L = math.gcd(t, HG_CHUNK)
    causal = (jnp.arange(L)[:, None] >= jnp.arange(L)[None, :])[:, :, None]

    def step(s, chunk):
        qc, kc, vc, lf = chunk
        b = jnp.cumsum(lf, axis=2)
        diff = b[:, :, :, None, :] - b[:, :, None, :, :]
        dec = jnp.where(causal, jnp.exp(jnp.where(causal, diff, 0.0)), 0.0)
        att = jnp.einsum('bhtc,bhsc,bhtsc->bhts', qc, kc, dec)
        o = jnp.einsum('bhts,bhse->bhte', att, vc) + jnp.einsum('bhtc,bhce->bhte', qc * jnp.exp(b), s)
        b_last = b[:, :, -1:, :]
        s = jnp.exp(b_last[:, :, 0, :])[..., None] * s + jnp.einsum('bhsc,bhse->bhce', kc * jnp.exp(b_last - b), vc)
        return s, o

    s, o = lax.scan(step, s0, (_to_chunks(q, L), _to_chunks(k, L), _to_chunks(v, L), _to_chunks(log_f, L)))
    return _from_chunks(o), s


def _mixer(h, offset, s_ret0, s_hg0, w_in, ret_gain, hg_gain, lower, w_out):
    b, t, _ = h.shape
    sizes = [RET_HEADS * RET_DK, RET_HEADS * RET_DK, RET_W, RET_W, HG_HEADS * HG_DK, HG_HEADS * HG_DK, HG_W, HG_W]
    points = [int(p) for p in np.cumsum(sizes)[:-1]]
    rq, rk, rv, rg, hq, hf, hi, hgate = jnp.split(h @ w_in, points, axis=-1)
    pos = jnp.arange(t, dtype=jnp.float32) + offset
    q_r = _rotary(_split_heads(rq, RET_HEADS), pos)
    k_r = _rotary(_split_heads(rk, RET_HEADS), pos) * (RET_DK ** -0.5)
    v_r = _split_heads(rv, RET_HEADS)
    o_r, s_r = _retention(q_r, k_r, v_r, s_ret0.astype(jnp.float32))
    o_r = _merge_heads(_head_rmsnorm(o_r)) * ret_gain.astype(jnp.float32) * jax.nn.silu(rg.astype(jnp.float32))
    f = lower + (1.0 - lower) * jax.nn.sigmoid(hf.astype(jnp.float32))
    q_h = _split_heads(jax.nn.silu(hq.astype(jnp.float32)), HG_HEADS)
    k_h = _split_heads(1.0 - f, HG_HEADS)
    v_h = _split_heads(hi, HG_HEADS)
    o_h, s_h = _hgrn2(q_h, k_h, v_h, _split_heads(jnp.log(f), HG_HEADS), s_hg0.astype(jnp.float32))
    o_h = _merge_heads(_head_rmsnorm(o_h)) * hg_gain.astype(jnp.float32) * jax.nn.silu(hgate.astype(jnp.float32))
    out = jnp.concatenate([o_r, o_h], axis=-1).astype(h.dtype) @ w_out
    return out, s_r, s_h


def _mem_kv(mem, g_mem, w_xk, w_xv):
    b = mem.shape[0]
    m = _rmsnorm(mem, g_mem)
    k = (m @ w_xk).reshape(b, N_MEM, XA_HEADS, XA_HD)
    v = (m @ w_xv).reshape(b, N_MEM, XA_HEADS, XA_HD)
    return k, v


def _cross_attn(h, mk, mv, w_xq, w_xo):
    b, t, _ = h.shape
    q = (h @ w_xq).reshape(b, t, XA_HEADS, XA_HD)
    s = jnp.einsum('bthd,bmhd->bhtm', q, mk).astype(jnp.float32) * (XA_HD ** -0.5)
    p = jax.nn.softmax(s, axis=-1).astype(mv.dtype)
    o = jnp.einsum('bhtm,bmhd->bthd', p, mv).reshape(b, t, XA_HEADS * XA_HD)
    return o @ w_xo


def _swiglu(h, w_gate, w_up, w_down):
    return (jax.nn.silu(h @ w_gate) * (h @ w_up)) @ w_down


def _forward(x, offset, s_ret, s_hg, mem_k, mem_v, g_mix, w_in, ret_gain, hg_gain, hg_lb, w_out,
             g_xa, w_xq, w_xo, g_ffn, w_gate, w_up, w_down, g_final):
    lower_all = jnp.cumsum(jax.nn.softmax(hg_lb.astype(jnp.float32), axis=0), axis=0)
    new_ret, new_hg = [], []
    for l in range(DEPTH):
        mix, s_r, s_h = _mixer(_rmsnorm(x, g_mix[l]), offset, s_ret[l], s_hg[l], w_in[l],
                               ret_gain[l], hg_gain[l], lower_all[l], w_out[l])
        x = x + mix
        x = x + _cross_attn(_rmsnorm(x, g_xa[l]), mem_k[l], mem_v[l], w_xq[l], w_xo[l])
        x = x + _swiglu(_rmsnorm(x, g_ffn[l]), w_gate[l], w_up[l], w_down[l])
        new_ret.append(s_r)
        new_hg.append(s_h)
    return _rmsnorm(x, g_final), jnp.stack(new_ret).astype(x.dtype), jnp.stack(new_hg).astype(x.dtype)


def setup_inputs(seed: int = 0) -> dict:
    key = jax.random.key(seed)
    ks = jax.random.split(key, 24)

    def nrm(k, shape, scale):
        return jax.random.normal(k, shape, jnp.float32) * scale

    def gain(k, shape):
        return 1.0 + 0.01 * jax.random.normal(k, shape, jnp.float32)

    d = D_MODEL
    return {
        'x_prompt': nrm(ks[0], (BATCH, SEQ, d), 1.0),
        'x_sample': nrm(ks[1], (DEC_BATCH, DEC_SEQ, d), 1.0),
        'mem_prompt': nrm(ks[2], (BATCH, N_MEM, d), 1.0),
        'state_ret': nrm(ks[3], (DEPTH, DEC_BATCH, RET_HEADS, RET_DK, RET_DV), 0.5),
        'state_hgrn': nrm(ks[4], (DEPTH, DEC_BATCH, HG_HEADS, HG_DK, HG_DV), 0.5),
        'cache_mem_k': nrm(ks[5], (DEPTH, DEC_BATCH, N_MEM, XA_HEADS, XA_HD), 1.0),
        'cache_mem_v': nrm(ks[6], (DEPTH, DEC_BATCH, N_MEM, XA_HEADS, XA_HD), 1.0),
        'g_mix': gain(ks[7], (DEPTH, d)),
        'w_in': nrm(ks[8], (DEPTH, d, IN_W), d ** -0.5),
        'ret_gain': gain(ks[9], (DEPTH, RET_W)),
        'hg_gain': gain(ks[10], (DEPTH, HG_W)),
        'hg_lb': nrm(ks[11], (DEPTH + 1, HG_HEADS * HG_DK), 0.1),
        'w_out': nrm(ks[12], (DEPTH, MIX_W, d), MIX_W ** -0.5),
        'g_xa': gain(ks[13], (DEPTH, d)),
        'g_mem': gain(ks[14], (DEPTH, d)),
        'w_xq': nrm(ks[15], (DEPTH, d, XA_HEADS * XA_HD), d ** -0.5),
        'w_xk': nrm(ks[16], (DEPTH, d, XA_HEADS * XA_HD), d ** -0.5),
        'w_xv': nrm(ks[17], (DEPTH, d, XA_HEADS * XA_HD), d ** -0.5),
        'w_xo': nrm(ks[18], (DEPTH, XA_HEADS * XA_HD, d), (XA_HEADS * XA_HD) ** -0.5),
        'g_ffn': gain(ks[19], (DEPTH, d)),
        'w_gate': nrm(ks[20], (DEPTH, d, D_FF), d ** -0.5),
        'w_up': nrm(ks[21], (DEPTH, d, D_FF), d ** -0.5),
        'w_down': nrm(ks[22], (DEPTH, D_FF, d), D_FF ** -0.5),
        'g_final': gain(ks[23], (d,)),
    }


def reference(x_prompt, x_sample, mem_prompt, state_ret, state_hgrn, cache_mem_k, cache_mem_v,
              g_mix, w_in, ret_gain, hg_gain, hg_lb, w_out, g_xa, g_mem, w_xq, w_xk, w_xv, w_xo,
              g_ffn, w_gate, w_up, w_down, g_final):
    b = x_prompt.shape[0]
    mk, mv = [], []
    for l in range(DEPTH):
        k_l, v_l = _mem_kv(mem_prompt, g_mem[l], w_xk[l], w_xv[l])
        mk.append(k_l)
        mv.append(v_l)
    mem_k_prompt = jnp.stack(mk)
    mem_v_prompt = jnp.stack(mv)
    zero_ret = jnp.zeros((DEPTH, b, RET_HEADS, RET_DK, RET_DV), jnp.float32)
    zero_hg = jnp.zeros((DEPTH, b, HG_HEADS, HG_DK, HG_DV), jnp.float32)
    y_prompt, ret_p, hg_p = _forward(x_prompt, 0, zero_ret, zero_hg, mem_k_prompt, mem_v_prompt,
                                     g_mix, w_in, ret_gain, hg_gain, hg_lb, w_out, g_xa, w_xq, w_xo,
                                     g_ffn, w_gate, w_up, w_down, g_final)
    y_sample, ret_s, hg_s = _forward(x_sample, PAST_LEN, state_ret, state_hgrn, cache_mem_k, cache_mem_v,
                                     g_mix, w_in, ret_gain, hg_gain, hg_lb, w_out, g_xa, w_xq, w_xo,
                                     g_ffn, w_gate, w_up, w_down, g_final)
    return (y_prompt, y_sample, ret_p, hg_p, mem_k_prompt, mem_v_prompt, ret_s, hg_s)
```

```python
import numpy as np
import concourse.bass as bass
import concourse.mybir as mybir
from concourse.bass_utils import run_bass_kernel_spmd

F32 = mybir.dt.float32
BF16 = mybir.dt.bfloat16
AF = mybir.ActivationFunctionType
ALU = mybir.AluOpType

D = 1024
SEQ = 2048
NREQ = 16
DEC = 4
NS = NREQ * DEC
PAST = 16384
DFF = 2816
NMEM = 256
EPS = 1e-6
NCORES = 8
GAM = [1.0 - 2.0 ** (-5.0 - h) for h in range(4)]
DEBUG = {"mixer": True, "xattn": True, "ffn": True}
STRICT_WAR = True


class Buf:
    __slots__ = ("name", "w", "r", "dsem", "dcnt")

    def __init__(self, name=""):
        self.name = name
        self.w = None
        self.r = {}
        self.dsem = {}
        self.dcnt = {}


class Prog:
    ENGS = ("pe", "act", "dve", "pool", "sp")

    def __init__(self, nc):
        self.nc = nc
        self.sem = {e: nc.alloc_semaphore(f"s_{e}") for e in self.ENGS}
        self.cnt = {e: 0 for e in self.ENGS}
        self.known = {e: {} for e in self.ENGS}
        self.q = {e: [] for e in self.ENGS}
        self.final = {}
        self.nsem = 0
        self.marks = []
        self.mark_instr = []

    def _need(self, eng, dep, waits):
        if dep is None:
            return
        sem, val = dep
        k = sem.name
        if self.known[eng].get(k, 0) >= val:
            return
        self.known[eng][k] = val
        waits.append((sem, val))

    def _deps(self, eng, reads, writes):
        waits = []
        for b in reads:
            self._need(eng, b.w, waits)
        for b in writes:
            self._need(eng, b.w, waits)
            for k, dep in b.r.items():
                if k == eng and not STRICT_WAR:
                    continue
                self._need(eng, dep, waits)
        return waits

    def op(self, eng, emit, reads=(), writes=()):
        waits = self._deps(eng, reads, writes)
        self.cnt[eng] += 1
        me = (self.sem[eng], self.cnt[eng])
        for b in reads:
            b.r[eng] = me
        for b in writes:
            b.w = me
            b.r = {}
        self.q[eng].append((waits, emit, (self.sem[eng], 1)))

    def _dsem(self, b, eng):
        kind = "sw" if eng == "pool" else "hw"
        if kind not in b.dsem:
            b.dsem[kind] = self.nc.alloc_semaphore(f"dq{self.nsem}")
            b.dcnt[kind] = 0
            self.nsem += 1
        b.dcnt[kind] += 16
        return b.dsem[kind], b.dcnt[kind]

    def dma_load(self, eng, buf, out, in_):
        waits = self._deps(eng, [], [buf])
        sem, cnt = self._dsem(buf, eng)
        buf.w = (sem, cnt)
        buf.r = {}
        self.q[eng].append((waits, lambda e: e.dma_start(out=out, in_=in_), (sem, 16)))

    def dma_store(self, eng, buf, out, in_):
        bufs = buf if isinstance(buf, (list, tuple)) else [buf]
        waits = self._deps(eng, bufs, [])
        own = bufs[0]
        sem, cnt = self._dsem(own, eng)
        for b in bufs:
            b.r["dma_" + sem.name] = (sem, cnt)
        self.final[sem.name] = (sem, cnt)
        self.q[eng].append((waits, lambda e: e.dma_start(out=out, in_=in_), (sem, 16)))

    def mark(self, name):
        self.marks.append((name, len(self.q["pe"])))

    def barrier(self):
        for e in ("pe", "act", "dve", "pool", "sp"):
            waits = []
            for o in ("pe", "act", "dve", "pool"):
                if o != e and self.cnt[o]:
                    self._need(e, (self.sem[o], self.cnt[o]), waits)
            if waits:
                self.q[e].append((waits, None, None))

    def build(self):
        nc = self.nc
        waits = list(self.final.values())
        for e in ("pe", "act", "dve", "pool"):
            if self.cnt[e]:
                waits.append((self.sem[e], self.cnt[e]))
        self.q["sp"].append((waits, None, None))

        prog = self

        class CountPE:
            def __init__(self, e):
                self.e = e
                self.n = 0

            def matmul(self, *a, **k):
                self.n += 1
                return self.e.matmul(*a, **k)

            def transpose(self, *a, **k):
                self.n += 1
                return self.e.transpose(*a, **k)

        def replay(items, e, count=False):
            ce = CountPE(e) if count else e
            mi = 0
            for idx, (waits, emit, inc) in enumerate(items):
                if count:
                    while mi < len(prog.marks) and prog.marks[mi][1] <= idx:
                        prog.mark_instr.append((prog.marks[mi][0], ce.n))
                        mi += 1
                for sem, val in waits:
                    e.wait_ge(sem, val)
                if emit is None:
                    continue
                ins = emit(ce)
                if inc is not None:
                    ins.then_inc(inc[0], inc[1])

        with nc.Block() as block:
            @block.tensor
            def _(e):
                replay(self.q["pe"], e, count=True)

            @block.scalar
            def _(e):
                replay(self.q["act"], e)

            @block.vector
            def _(e):
                replay(self.q["dve"], e)

            @block.gpsimd
            def _(e):
                replay(self.q["pool"], e)

            @block.sync
            def _(e):
                replay(self.q["sp"], e)


class Cols:
    def __init__(self):
        self.off = {}
        self.n = 0

    def add(self, name, n):
        self.off[name] = (self.n, n)
        self.n += n

    def sl(self, name):
        o, n = self.off[name]
        return slice(o, o + n)


CF = Cols()
for _n, _k in (("ident", 128), ("perm", 128), ("kdec", 16), ("kdecS", 4), ("decS", 4), ("pvec", 56), ("ones", 128)):
    CF.add(_n, _k)
CB = Cols()
for _n, _k in (("identb", 128), ("cmask", 128), ("maskSr", 256), ("maskSh", 64), ("memb", 16),
               ("qdec", 2048), ("qdecS", 256), ("maskr", 2048), ("onesb", 128)):
    CB.add(_n, _k)


def make_consts():
    cf = np.zeros((128, CF.n), np.float64)
    cb = np.zeros((128, CB.n), np.float64)
    p = np.arange(128)
    cf[:, CF.sl("ident")] = np.eye(128)
    perm = np.zeros((128, 128))
    for m in range(64):
        perm[m + 64, m] = -1.0
        perm[m, m + 64] = 1.0
    cf[:, CF.sl("perm")] = perm
    cf[:, CF.sl("ones")] = 1.0
    kdec = np.zeros((128, 4, 4))
    for h in range(4):
        for b in range(4):
            kdec[:, h, b] = GAM[h] ** (511 - (b * 128 + p))
    cf[:, CF.sl("kdec")] = kdec.reshape(128, 16)
    for h in range(4):
        cf[:, CF.off["kdecS"][0] + h] = GAM[h] ** (3 - (p % 4))
        cf[:, CF.off["decS"][0] + h] = GAM[h] ** 4
    cb[:, CB.sl("identb")] = np.eye(128)
    cb[:, CB.sl("onesb")] = 1.0
    cb[:, CB.sl("cmask")] = (p[:, None] <= p[None, :]).astype(np.float64)
    x = np.arange(512)
    mr = np.zeros((128, 4, 512))
    qd = np.zeros((128, 4, 512))
    for h in range(4):
        dlt = x[None, :] - p[:, None]
        mr[:, h, :] = np.where(dlt >= 0, GAM[h] ** np.maximum(dlt, 0), 0.0)
        qd[:, h, :] = GAM[h] ** (x[None, :] + 1.0)
    cb[:, CB.sl("maskr")] = mr.reshape(128, 2048)
    cb[:, CB.sl("qdec")] = qd.reshape(128, 2048)
    t = np.arange(64)
    same = (t[:, None] // 4) == (t[None, :] // 4)
    dl = (t[None, :] % 4) - (t[:, None] % 4)
    msr = np.zeros((128, 4, 64))
    qds = np.zeros((128, 4, 64))
    for h in range(4):
        msr[:64, h, :] = np.where(same & (dl >= 0), GAM[h] ** np.maximum(dl, 0), 0.0)
        qds[:, h, :] = GAM[h] ** ((t[None, :] % 4) + 1.0)
    cb[:, CB.sl("maskSr")] = msr.reshape(128, 256)
    cb[:, CB.sl("qdecS")] = qds.reshape(128, 256)
    cb[:64, CB.sl("maskSh")] = (same & (dl >= 0)).astype(np.float64)
    memb = np.zeros((128, 16))
    memb[:64] = ((t[:, None] // 4) == np.arange(16)[None, :]).astype(np.float64)
    cb[:, CB.sl("memb")] = memb
    inv_freq = (np.float32(10000.0) ** (-(np.arange(64, dtype=np.float32) / np.float32(64)))).astype(np.float32)
    pos = np.concatenate([np.arange(SEQ, dtype=np.float32),
                          (PAST + (np.arange(NS) % 4)).astype(np.float32)])
    ang = (pos[:, None] * inv_freq[None, :]).astype(np.float32).astype(np.float64)
    cosT = np.cos(ang).T
    sinT = np.sin(ang).T
    cs = np.stack([np.concatenate([cosT, cosT], 0), np.concatenate([sinT, sinT], 0)], 1)
    return cf.astype(np.float32), cb.astype(np.float32), np.ascontiguousarray(cs.astype(np.float32))


def build_program():
    nc = bass.Bass("TRN2", target_bir_lowering=False)
    P = Prog(nc)

    def din(name, shape):
        return nc.dram_tensor(name, list(shape), F32, kind="ExternalInput").ap()

    def dout(name, shape):
        return nc.dram_tensor(name, list(shape), F32, kind="ExternalOutput").ap()

    xp = din("xp", [SEQ, D]); xs = din("xs", [NS, D]); memp = din("memp", [NMEM, D])
    sret = din("sret", [NREQ, 4, 128, 128]); shg = din("shg", [NREQ, 4, 128, 128])
    cmk = din("cmk", [NREQ, NMEM, D]); cmv = din("cmv", [NREQ, NMEM, D])
    w_in = din("w_in", [D, 4096]); w_out = din("w_out", [D, D]); w_xq = din("w_xq", [D, D])
    w_xk = din("w_xk", [D, D]); w_xv = din("w_xv", [D, D]); w_xo = din("w_xo", [D, D])
    w_gate = din("w_gate", [D, DFF]); w_up = din("w_up", [D, DFF]); w_down = din("w_down", [DFF, D])
    cf_d = din("cf", [128, CF.n]); cb_d = din("cb", [128, CB.n]); cs_d = din("cs", [128, 2, SEQ + NS])
    yp = dout("yp", [SEQ, D]); ys = dout("ys", [NS, D])
    retp = dout("retp", [4, 128, 128]); hgp = dout("hgp", [4, 128, 128])
    mkp = dout("mkp", [NMEM, D]); mvp = dout("mvp", [NMEM, D])
    rets = dout("rets", [NREQ, 4, 128, 128]); hgs = dout("hgs", [NREQ, 4, 128, 128])

    def sb(name, n, dt=F32):
        return nc.alloc_sbuf_tensor("sb_" + name, [128, n], dt)

    cf = sb("cf", CF.n); cb = sb("cb", CB.n, BF16)
    b_const = Buf("const")
    xT = sb("xT", 8 * 512)
    G = [sb(f"G{i}", 8 * 512, BF16) for i in range(3)]
    hid = sb("hid", 22 * 512, BF16)
    xio = [sb(f"xio{i}", 1024) for i in range(2)]
    rstd = sb("rstd", 512)
    qT = sb("qT", 4 * 512, BF16); kT = sb("kT", 4 * 512, BF16)
    gate_r = sb("gate_r", 4 * 512, BF16); gate_h = sb("gate_h", 4 * 512, BF16)
    tmpA = sb("tmpA", 512); tmpB = sb("tmpB", 512); tmpC = sb("tmpC", 512); tmpA2 = sb("tmpA2", 512)
    rot_n = [0]
    tmpD = tmpB; tmpE = tmpC
    qb = sb("qb", 4 * 512, BF16); kb = sb("kb", 4 * 512, BF16)
    v_tm = sb("v_tm", 4 * 512, BF16); hi_tm = sb("hi_tm", 4 * 512, BF16)
    kd_tm = sb("kd_tm", 16 * 128, BF16); kbT_tm = sb("kbT_tm", 16 * 128, BF16)
    attT = sb("attT", 4 * 512, BF16)
    attH = [sb(f"attH{i}", 128, BF16) for i in range(2)]
    S_ret = sb("S_ret", 512); S_retb = sb("S_retb", 512, BF16)
    S_hg = sb("S_hg", 512); Sp = sb("Sp", 512, BF16)
    evec = sb("evec", 80)
    o_sbs = [sb(f"o_sb{i}", 512) for i in range(2)]; sqos = [sb(f"sqo{i}", 512, BF16) for i in range(2)]
    rrs = [sb(f"rr{i}", 512) for i in range(2)]
    qd = sb("qd", 512, BF16)
    PTs = [sb(f"PT{i}", 2 * 512, BF16) for i in range(2)]; rdens = [sb(f"rden{i}", 512) for i in range(2)]
    rden = rdens[0]
    KT = sb("KT", 8 * 256, BF16); Vb = sb("Vb", 2 * 1024, BF16)
    cs = [sb("cs0", 2 * 512)]
    wslot = [sb(f"w{i}", 4096, BF16) for i in range(3)]
    ab = sb("ab", 16)

    ident = cf[:, CF.sl("ident")]; perm = cf[:, CF.sl("perm")]
    identb = cb[:, CB.sl("identb")]; onesb = cb[:, CB.sl("onesb")]; onesf = cf[:, CF.sl("ones")]
    pv0 = CF.off["pvec"][0]

    def pvec(i):
        return cf[:, pv0 + i:pv0 + i + 1]
    G_MIX, G_XA, G_MEM, G_FFN, G_FIN, RGAIN, HGAIN, LB0, LB1 = 0, 8, 16, 24, 32, 40, 44, 48, 52

    banks = [nc.alloc_psum_tensor(f"ps{i}", [128, 512], F32) for i in range(8)]
    bbuf = [Buf(f"ps{i}") for i in range(8)]
    rot = {"L": [0, 1], "S": [2, 3, 4, 5, 6, 7]}
    rpos = {"L": 0, "S": 0}

    def psum(kind="S"):
        lst = rot[kind]
        i = lst[rpos[kind] % len(lst)]
        rpos[kind] += 1
        return banks[i], bbuf[i]

    P.dma_load("sp", b_const, cf[:], cf_d)
    P.dma_load("pool", b_const, cb[:], cb_d)
    b_ab = Buf("ab")
    P.op("dve", lambda e: e.tensor_tensor(out=ab[:, 12:16], in0=cf[:, pv0 + LB0:pv0 + LB0 + 4],
                                          in1=cf[:, pv0 + LB1:pv0 + LB1 + 4], op=ALU.subtract),
         reads=[b_const], writes=[b_ab])
    P.op("act", lambda e: e.activation(out=ab[:, 12:16], in_=ab[:, 12:16], func=AF.Tanh, scale=0.5),
         reads=[b_ab], writes=[b_ab])
    P.op("dve", lambda e: e.tensor_scalar(out=ab[:, 0:4], in0=ab[:, 12:16], scalar1=0.25, scalar2=0.75,
                                          op0=ALU.mult, op1=ALU.add), reads=[b_ab], writes=[b_ab])
    P.op("dve", lambda e: e.tensor_scalar(out=ab[:, 4:8], in0=ab[:, 12:16], scalar1=-0.25, scalar2=0.25,
                                          op0=ALU.mult, op1=ALU.add), reads=[b_ab], writes=[b_ab])
    P.op("dve", lambda e: e.tensor_scalar(out=ab[:, 8:12], in0=ab[:, 12:16], scalar1=0.25, scalar2=-0.25,
                                          op0=ALU.mult, op1=ALU.add), reads=[b_ab], writes=[b_ab])

    wbuf = [Buf(f"w{i}") for i in range(3)]
    plan = []
    wstate = {"issued": 0, "taken": 0}

    def plan_linear(W, K, c0, ncols):
        nk = K // 128
        step = 512 if nk == 8 else 128
        for c in range(c0, c0 + ncols, step):
            plan.append(([(W, c, min(step, c0 + ncols - c))], nk))

    NSLAB_TILE = 33
    wscr = nc.dram_tensor("wscr", [NSLAB_TILE, 128, 4096], BF16, kind="Internal").ap()
    scr_gate = [False]

    def slab_views(i):
        parts, nk = plan[i]
        views, off = [], 0
        for (_, _, n) in parts:
            views.append(wslot[i % 3][:, off:off + nk * n].rearrange("p (k n) -> p k n", n=n))
            off += nk * n
        return views, off

    def issue_next():
        i = wstate["issued"]
        if i >= len(plan):
            return
        parts, nk = plan[i]
        slot = wslot[i % 3]
        views, tot = slab_views(i)
        t, s_ = ((i - 4) // NSLAB_TILE, (i - 4) % NSLAB_TILE) if i >= 4 else (-1, -1)
        if t >= 1:
            if not scr_gate[0]:
                scr_gate[0] = True
                P.q["sp"].append(([(wbuf[k].dsem[kd], wbuf[k].dcnt[kd]) for k in range(3) for kd in wbuf[k].dsem], None, None))
            P.dma_load("sp", wbuf[i % 3], slot[:, 0:tot], wscr[s_][:, 0:tot])
        else:
            for (W, c0, n), dst in zip(parts, views):
                P.dma_load("pool", wbuf[i % 3], dst, W[:, c0:c0 + n].rearrange("(k p) n -> p k n", p=128))
            if t == 0:
                P.dma_store("sp", wbuf[i % 3], wscr[s_][:, 0:tot], slot[:, 0:tot])
        wstate["issued"] += 1

    def take_slab(W, c0, hold=0, multi=False):
        i = wstate["taken"]
        assert plan[i][0][0][0] is W and plan[i][0][0][1] == c0, (i, plan[i][0][0][1], c0)
        while wstate["issued"] < min(i + 3 - hold, len(plan)):
            issue_next()
        wstate["taken"] += 1
        views, _ = slab_views(i)
        return (views if multi else views[0]), wbuf[i % 3]

    W_IN_ORDER = [2560, 2048, 1024, 512, 0, 1536, 3072, 3584]
    plan_linear(w_xk, D, 0, D); plan_linear(w_xv, D, 0, D)
    for _t in range(5):
        for c0 in W_IN_ORDER:
            plan_linear(w_in, D, c0, 512)
        plan_linear(w_out, D, 0, D); plan_linear(w_xq, D, 0, D); plan_linear(w_xo, D, 0, D)
        for g0 in range(0, DFF, 256):
            plan.append(([(w_gate, g0, 256), (w_up, g0, 256)], 8))
        plan_linear(w_down, DFF, 0, D)
    assert len(plan) == 4 + 5 * NSLAB_TILE, len(plan)

    def fmview(t, nch, Tw):
        return t[:, 0:nch * Tw].rearrange("p (c t) -> p c t", t=Tw)

    gbuf = [[Buf(f"G{i}_{c}") for c in range(8)] for i in range(3)]
    xTb = [Buf(f"xT{c}") for c in range(8)]
    b_rstd = Buf("rstd")
    xiob = [Buf("xio0"), Buf("xio1")]
    xio_n = [0]

    def load_tile(src_rows, r0, Tw):
        xv = fmview(xT, 8, Tw)
        nb = (Tw + 127) // 128
        for b in range(nb):
            n = min(128, Tw - b * 128)
            k = xio_n[0] % 2
            xio_n[0] += 1
            P.dma_load("sp", xiob[k], xio[k][0:n, :], src_rows[r0 + b * 128:r0 + b * 128 + n, :])
            for g in range(2):
                ps, pb = psum()

                def tr(e, k=k, g=g, n=n, ps=ps):
                    ins = None
                    for cc in range(4):
                        c = g * 4 + cc
                        ins = e.transpose(ps[:, cc * 128:cc * 128 + n], xio[k][0:n, c * 128:(c + 1) * 128], ident[0:n, 0:n])
                    return ins
                P.op("pe", tr, reads=[xiob[k], b_const], writes=[pb])
                src = ps[:, :].rearrange("p (c t) -> p c t", t=128)[:, :, 0:n]
                dst = xv[:, g * 4:g * 4 + 4, b * 128:b * 128 + n]
                eng = "act" if g == 0 else "dve"
                if eng == "act":
                    P.op("act", lambda e, dst=dst, src=src: e.activation(out=dst, in_=src, func=AF.Copy),
                         reads=[pb], writes=xTb[g * 4:g * 4 + 4])
                else:
                    P.op("dve", lambda e, dst=dst, src=src: e.tensor_copy(dst, src),
                         reads=[pb], writes=xTb[g * 4:g * 4 + 4])

    def store_tile(dst_rows, r0, Tw):
        xv = fmview(xT, 8, Tw)
        nb = (Tw + 127) // 128
        for b in range(nb):
            n = min(128, Tw - b * 128)
            k = xio_n[0] % 2
            xio_n[0] += 1
            for g in range(2):
                ps, pb = psum()

                def tr(e, g=g, n=n, ps=ps, b=b):
                    ins = None
                    for cc in range(4):
                        c = g * 4 + cc
                        ins = e.transpose(ps[0:n, cc * 128:(cc + 1) * 128], xv[:, c, b * 128:b * 128 + n], ident)
                    return ins
                P.op("pe", tr, reads=xTb[g * 4:g * 4 + 4] + [b_const], writes=[pb])
                dst = xio[k][0:n, g * 512:(g + 1) * 512]
                if g == 0:
                    P.op("act", lambda e, dst=dst, ps=ps, n=n: e.activation(out=dst, in_=ps[0:n, :], func=AF.Copy),
                         reads=[pb], writes=[xiob[k]])
                else:
                    P.op("dve", lambda e, dst=dst, ps=ps, n=n: e.tensor_copy(dst, ps[0:n, :]),
                         reads=[pb], writes=[xiob[k]])
            P.dma_store("sp", xiob[k], dst_rows[r0 + b * 128:r0 + b * 128 + n, :], xio[k][0:n, :])

    def rstd_from_psum(ps, pb, Tw, inv_n, out_ap, out_buf):
        P.op("act", lambda e: e.activation(out=out_ap, in_=ps[:, 0:Tw], func=AF.Ln, scale=inv_n, bias=epsb[:, 0:1]),
             reads=[pb, b_ab], writes=[out_buf])
        P.op("act", lambda e: e.activation(out=out_ap, in_=out_ap, func=AF.Exp, scale=-0.5),
             reads=[out_buf], writes=[out_buf])

    def rmsnorm(gi, gidx, sqi, Tw, out_f32_inplace=False, pre=None):
        xv = fmview(xT, 8, Tw)
        sqv = fmview(G[sqi], 8, Tw)
        if pre is not None:
            flush()
            ps, pb = pre
        else:
            P.op("act", lambda e: e.activation(out=G[sqi][:, 0:8 * Tw], in_=xT[:, 0:8 * Tw], func=AF.Square),
                 reads=xTb, writes=gbuf[sqi])
            ps, pb = psum()

            def mm(e):
                ins = None
                for c in range(8):
                    ins = e.matmul(ps[:, 0:Tw], onesb, sqv[:, c, :], start=(c == 0), stop=(c == 7))
                return ins
            P.op("pe", mm, reads=gbuf[sqi] + [b_const], writes=[pb])
        rstd_from_psum(ps, pb, Tw, 1.0 / D, rstd[:, 0:Tw], b_rstd)
        for c in range(8):
            if out_f32_inplace:
                o, ob = xv[:, c, :], xTb[c]
            else:
                o, ob = fmview(G[gi], 8, Tw)[:, c, :], gbuf[gi][c]
            P.op("dve", lambda e, o=o, c=c: e.scalar_tensor_tensor(
                out=o, in0=xv[:, c, :], scalar=pvec(gidx + c), in1=rstd[:, 0:Tw], op0=ALU.mult, op1=ALU.mult),
                 reads=[xTb[c], b_rstd, b_const], writes=[ob])

    cur = {"ti": 0, "sample": False}

    def veng():
        return "pool" if (1 <= cur["ti"] <= 3) else "dve"

    pend = []

    def flush():
        while pend:
            pend.pop(0)()

    def linear_fm(W, c0, ncols, in_t, in_bufs, nk, Tw, consumer, slab=None):
        inv = fmview(in_t, nk, Tw)
        step = 512 if nk == 8 else 128
        for s0 in range(c0, c0 + ncols, step):
            n = min(step, c0 + ncols - s0)
            sl, slb = slab if slab is not None else take_slab(W, s0)
            for j in range(n // 128):
                ps, pb = psum()

                def mm(e, sl=sl, j=j, ps=ps):
                    ins = None
                    for k in range(nk):
                        ins = e.matmul(ps[:, 0:Tw], sl[:, k, j * 128:(j + 1) * 128], inv[:, k, :],
                                       start=(k == 0), stop=(k == nk - 1))
                    return ins
                P.op("pe", mm, reads=[slb] + list(in_bufs), writes=[pb])
                flush()
                consumer((s0 - c0) // 128 + j, ps, pb)

    def linear_tm(W, c0, in_t, in_bufs, Tw, consumer, slab=None):
        inv = fmview(in_t, 8, Tw)
        sl, slb = slab if slab is not None else take_slab(W, c0)
        nb = (Tw + 127) // 128
        for b in range(nb):
            n = min(128, Tw - b * 128)
            ps, pb = psum()

            def mm(e, b=b, n=n, ps=ps):
                ins = None
                for k in range(8):
                    ins = e.matmul(ps[0:n, :], inv[:, k, b * 128:b * 128 + n], sl[:, k, :],
                                   start=(k == 0), stop=(k == 7))
                return ins
            P.op("pe", mm, reads=[slb] + list(in_bufs), writes=[pb])
            consumer(b, n, ps, pb)

    def resid_add(Tw, enabled=True, sq=None):
        xv = fmview(xT, 8, Tw)
        if sq is not None:
            sqi, nps, npb = sq
            sqv = fmview(G[sqi], 8, Tw)

        def cons(c, ps, pb):
            if enabled:
                P.op("dve", lambda e: e.tensor_tensor(out=xv[:, c, :], in0=xv[:, c, :], in1=ps[:, 0:Tw], op=ALU.add),
                     reads=[pb, xTb[c]], writes=[xTb[c]])
            if sq is not None:
                P.op("act", lambda e: e.activation(out=sqv[:, c, :], in_=xv[:, c, :], func=AF.Square),
                     reads=[xTb[c]], writes=[gbuf[sqi][c]])

                def post():
                    P.op("pe", lambda e: e.matmul(nps[:, 0:Tw], onesb, sqv[:, c, :], start=(c == 0), stop=(c == 7)),
                         reads=[gbuf[sqi][c], b_const], writes=[npb])
                pend.append(post)
        return cons

    epsb = sb("epsb", 1)
    P.op("dve", lambda e: e.memset(epsb[:], EPS), writes=[b_ab])

    b_qT = [Buf(f"qT{h}") for h in range(4)]; b_kT = [Buf(f"kT{h}") for h in range(4)]
    b_gr = [Buf(f"gr{h}") for h in range(4)]; b_gh = [Buf(f"gh{h}") for h in range(4)]
    b_qb = [Buf(f"qb{h}") for h in range(4)]; b_kb = [Buf(f"kb{h}") for h in range(4)]
    b_v = [Buf(f"v{b}") for b in range(4)]; b_hi = [Buf(f"hi{b}") for b in range(4)]
    b_kd = [Buf(f"kd{h}") for h in range(4)]; b_kbT = [Buf(f"kbT{h}") for h in range(4)]
    b_tA, b_tB, b_tC, b_tA2 = Buf("tA"), Buf("tB"), Buf("tC"), Buf("tA2")
    b_tD, b_tE = b_tB, b_tC
    b_aR, b_aH = Buf("aR"), Buf("aH")
    b_attT = Buf("attT"); b_attH = [Buf("attH0"), Buf("attH1")]
    b_attT2 = Buf("attT2")
    b_bbs = [Buf(f"bbs{h}") for h in range(4)]
    b_attHs = [Buf(f"attHs{c}") for c in range(16)]
    b_Sps = [Buf(f"Sps{c}") for c in range(16)]
    b_Sr = [Buf(f"Sr{h}") for h in range(4)]; b_Srb = [Buf(f"Srb{h}") for h in range(4)]
    b_Sh = [Buf(f"Sh{h}") for h in range(4)]; b_Sp = [Buf(f"Sp{h}") for h in range(4)]
    b_ev = [Buf(f"ev{h}") for h in range(4)]
    b_osbs, b_sqos, b_rrs = [Buf("osb0"), Buf("osb1")], [Buf("sqo0"), Buf("sqo1")], [Buf("rr0"), Buf("rr1")]
    b_qd = Buf("qd")
    b_PTs, b_rdens = [Buf("PT0"), Buf("PT1")], [Buf("rden0"), Buf("rden1")]
    b_rden = b_rdens[0]
    hn_n = [0]
    b_KT, b_Vb = Buf("KT"), Buf("Vb")
    b_cs = [Buf("cs0")]
    b_hid = [Buf(f"hid{j}") for j in range(22)]

    P.op("dve", lambda e: e.memset(S_ret[:], 0.0), writes=b_Sr)
    P.op("dve", lambda e: e.memset(S_hg[:], 0.0), writes=b_Sh)
    P.op("dve", lambda e: e.memset(S_retb[:], 0.0), writes=b_Srb)

    load_tile(memp, 0, NMEM)
    rmsnorm(0, G_MEM, 1, NMEM)
    KTv = KT[:, :].rearrange("p (c m) -> p c m", m=NMEM)
    Vbv = Vb[:, :].rearrange("p (b n) -> p b n", n=D)
    for (W, dst, isk) in ((w_xk, mkp, True), (w_xv, mvp, False)):
        for s0 in (0, 512):
            slab = take_slab(W, s0)
            if isk:
                def consK(c, ps, pb, s0=s0):
                    cc = s0 // 128 + c
                    P.op("act", lambda e: e.activation(out=KTv[:, cc, :], in_=ps[:, 0:NMEM], func=AF.Copy),
                         reads=[pb], writes=[b_KT])
                linear_fm(W, s0, 512, G[0], gbuf[0], 8, NMEM, consK, slab=slab)

            def consTM(b, n, ps, pb, s0=s0, dst=dst, isk=isk):
                k = xio_n[0] % 2
                xio_n[0] += 1
                P.op("dve", lambda e: e.tensor_copy(xio[k][:, 0:512], ps[:, :]), reads=[pb], writes=[xiob[k]])
                if not isk:
                    P.op("act", lambda e: e.activation(out=Vbv[:, b, s0:s0 + 512], in_=xio[k][:, 0:512], func=AF.Copy),
                         reads=[xiob[k]], writes=[b_Vb])
                P.dma_store("sp", xiob[k], dst[b * 128:(b + 1) * 128, s0:s0 + 512], xio[k][:, 0:512])
            linear_tm(W, s0, G[0], gbuf[0], NMEM, consTM, slab=slab)

    def head_norm_gate(o_ps, pb, Tw, gain_col, gate_ap, gate_buf, out_ap, out_buf):
        i = hn_n[0] % 2
        hn_n[0] += 1
        o_sb, sqo, rr = o_sbs[i], sqos[i], rrs[i]
        b_osb, b_sqo, b_rr = b_osbs[i], b_sqos[i], b_rrs[i]
        P.op("act", lambda e: e.activation(out=o_sb[:, 0:Tw], in_=o_ps, func=AF.Copy), reads=[pb], writes=[b_osb])
        P.op("act", lambda e: e.activation(out=sqo[:, 0:Tw], in_=o_ps, func=AF.Square), reads=[pb], writes=[b_sqo])
        ps, pb2 = psum()
        P.op("pe", lambda e: e.matmul(ps[:, 0:Tw], onesb, sqo[:, 0:Tw], start=True, stop=True),
             reads=[b_sqo, b_const], writes=[pb2])
        rstd_from_psum(ps, pb2, Tw, 1.0 / 128, rr[:, 0:Tw], b_rr)
        P.op("dve", lambda e: e.scalar_tensor_tensor(out=o_sb[:, 0:Tw], in0=o_sb[:, 0:Tw], scalar=pvec(gain_col),
                                                      in1=rr[:, 0:Tw], op0=ALU.mult, op1=ALU.mult),
             reads=[b_osb, b_rr, b_const], writes=[b_osb])
        P.op(veng(), lambda e: e.tensor_tensor(out=out_ap, in0=o_sb[:, 0:Tw], in1=gate_ap, op=ALU.mult),
             reads=[b_osb, gate_buf], writes=[out_buf])

    def run_tile(ti, Tw, src_rows, r0, dst_rows, sample):
        cur["ti"], cur["sample"] = ti, sample
        nb = (Tw + 127) // 128
        ck = 0
        csv = cs[ck][:, 0:2 * Tw].rearrange("p (a t) -> p a t", t=Tw)
        P.dma_load("sp", b_cs[ck], csv, cs_d[:, :, (SEQ if sample else r0):(SEQ if sample else r0) + Tw])
        P.mark(f"t{ti}:load")
        load_tile(src_rows, r0, Tw)
        rmsnorm(0, G_MIX, 1, Tw)
        P.mark(f"t{ti}:w_in")
        h1, h1b = G[0], gbuf[0]
        qTv, kTv = fmview(qT, 4, Tw), fmview(kT, 4, Tw)
        grv, ghv = fmview(gate_r, 4, Tw), fmview(gate_h, 4, Tw)
        qbv, kbv = fmview(qb, 4, Tw), fmview(kb, 4, Tw)
        v_v = v_tm[:, :].rearrange("p (b n) -> p b n", n=512)
        hi_v = hi_tm[:, :].rearrange("p (b n) -> p b n", n=512)
        kd_v = kd_tm[:, :].rearrange("p (h b d) -> p h b d", h=4, b=4)
        kbT_v = kbT_tm[:, :].rearrange("p (h b d) -> p h b d", h=4, b=4)
        ofT, ofb = G[1], gbuf[1]
        ofv = fmview(ofT, 8, Tw)

        def cons_tm(dstv, bufs):
            def c(b, n, ps, pb):
                P.op("act", lambda e: e.activation(out=dstv[0:n, b, :], in_=ps[0:n, :], func=AF.Copy),
                     reads=[pb], writes=[bufs[b]])
            return c

        def cons_rot(dstv, bufs, scale):
            def c(h, ps, pb):
                i = rot_n[0] % 2
                rot_n[0] += 1
                tA, bA = (tmpA, b_tA) if i == 0 else (tmpA2, b_tA2)
                P.op("act", lambda e: e.activation(out=tA[:, 0:Tw], in_=ps[:, 0:Tw], func=AF.Copy, scale=scale),
                     reads=[pb], writes=[bA])

                def post():
                    ps2, pb2 = psum()
                    P.op("pe", lambda e: e.matmul(ps2[:, 0:Tw], perm, tA[:, 0:Tw], start=True, stop=True),
                         reads=[bA, b_const], writes=[pb2])
                    P.op("dve", lambda e: e.tensor_tensor(out=tmpB[:, 0:Tw], in0=ps2[:, 0:Tw], in1=csv[:, 1, :], op=ALU.mult),
                         reads=[pb2, b_cs[ck]], writes=[b_tB])
                    P.op(veng(), lambda e: e.tensor_tensor(out=tmpC[:, 0:Tw], in0=tA[:, 0:Tw], in1=csv[:, 0, :], op=ALU.mult),
                         reads=[bA, b_cs[ck]], writes=[b_tC])
                    P.op(veng(), lambda e: e.tensor_tensor(out=dstv[:, h, :], in0=tmpB[:, 0:Tw], in1=tmpC[:, 0:Tw], op=ALU.add),
                         reads=[b_tB, b_tC], writes=[bufs[h]])
                pend.append(post)
            return c

        def cons_silu(dstv, bufs):
            def c(h, ps, pb):
                P.op("act", lambda e: e.activation(out=dstv[:, h, :], in_=ps[:, 0:Tw], func=AF.Silu),
                     reads=[pb], writes=[bufs[h]])
            return c

        def emit_kd():
            for h in range(4):
                ps, pb = psum()
                psb = ps[:, :].bitcast(BF16)

                def tr(e, h=h, psb=psb):
                    ins = None
                    for b in range(nb):
                        n = min(128, Tw - b * 128)
                        ins = e.transpose(psb[0:n, b * 128:(b + 1) * 128], kTv[:, h, b * 128:b * 128 + n], identb)
                    return ins
                P.op("pe", tr, reads=[b_kT[h], b_const], writes=[pb])
                for b in range(nb):
                    n = min(128, Tw - b * 128)
                    if sample:
                        sc = cf[0:n, CF.off["kdecS"][0] + h:CF.off["kdecS"][0] + h + 1]
                    else:
                        sc = cf[0:n, CF.off["kdec"][0] + h * 4 + b:CF.off["kdec"][0] + h * 4 + b + 1]
                    P.op("dve", lambda e, b=b, n=n, sc=sc, psb=psb, h=h: e.tensor_scalar(
                        out=kd_v[0:n, h, b, :], in0=psb[0:n, b * 128:(b + 1) * 128], scalar1=sc, scalar2=None, op0=ALU.mult),
                        reads=[pb, b_const], writes=[b_kd[h]])

        def emit_kbT(h):
            ps2, pb2 = psum()
            psb = ps2[:, :].bitcast(BF16)

            def tr(e):
                ins = None
                for b in range(nb):
                    n = min(128, Tw - b * 128)
                    ins = e.transpose(psb[0:n, b * 128:(b + 1) * 128], kbv[:, h, b * 128:b * 128 + n], identb)
                return ins
            P.op("pe", tr, reads=[b_kb[h], b_const], writes=[pb2])
            n0 = min(128, Tw)
            P.op("act", lambda e: e.activation(out=kbT_v[0:n0, h, 0:nb, :],
                                               in_=psb[0:n0, 0:nb * 128].rearrange("p (b d) -> p b d", d=128), func=AF.Copy),
                 reads=[pb2], writes=[b_kbT[h]])


        sbase = 10624 if sample else 0
        bbs = hid[:, sbase:sbase + 8 * Tw].bitcast(F32).rearrange("p (h t) -> p h t", h=4)
        attHs = hid[:, 4096:6144].rearrange("p (c t) -> p c t", t=128)
        Sps = hid[:, 6144:8192].rearrange("p (c t) -> p c t", t=128)
        attT2 = hid[:, 8192:10240]

        def evv(h):
            return evec[:, h * 20:(h + 1) * 20]

        def cons_hf(h, ps, pb):
            P.op("act", lambda e: e.activation(out=hlf[:, h, 0:Tw], in_=ps[:, 0:Tw], func=AF.Tanh, scale=0.5),
                 reads=[pb], writes=[b_hsc[h]])

        def cons_hq(h, ps, pb):
            P.op("act", lambda e: e.activation(out=qbv[:, h, :], in_=ps[:, 0:Tw], func=AF.Silu), reads=[pb], writes=[b_qb[h]])

        def stB1():
            for h in range(4):
                P.op("dve", lambda e, h=h: e.tensor_scalar(out=hkk[:, h, 0:Tw], in0=hlf[:, h, 0:Tw], scalar1=ab[:, 8 + h:9 + h],
                                                           scalar2=ab[:, 4 + h:5 + h], op0=ALU.mult, op1=ALU.add),
                     reads=[b_hsc[h], b_ab], writes=[b_hsk[h]])
            for h in range(4):
                P.op("act", lambda e, h=h: e.activation(out=hlf[:, h, 0:Tw], in_=hlf[:, h, 0:Tw], func=AF.Ln,
                                                        scale=ab[:, 4 + h:5 + h], bias=ab[:, 0 + h:1 + h]),
                     reads=[b_hsc[h], b_ab], writes=[b_hsc[h]])

        def stB3():
            for h in range(4):
                if not sample:
                    for j in range(4):
                        P.op("dve", lambda e, h=h, j=j: e.tensor_tensor_scan(
                            out=bbs[:, h, j * 128:(j + 1) * 128], data0=onesf, data1=hlf[:, h, j * 128:(j + 1) * 128],
                            initial=0.0, op0=ALU.mult, op1=ALU.add), reads=[b_hsc[h], b_const], writes=[b_bbs[h]])
                else:
                    l3 = hlf[:, h, 0:Tw].rearrange("p (r l) -> p r l", l=4)
                    b3 = bbs[:, h, :].rearrange("p (r l) -> p r l", l=4)
                    P.op("dve", lambda e, l3=l3, b3=b3: e.tensor_copy(b3[:, :, 0], l3[:, :, 0]), reads=[b_hsc[h]], writes=[b_bbs[h]])
                    for l in range(1, 4):
                        P.op("dve", lambda e, l=l, l3=l3, b3=b3: e.tensor_tensor(out=b3[:, :, l], in0=b3[:, :, l - 1], in1=l3[:, :, l], op=ALU.add),
                             reads=[b_hsc[h], b_bbs[h]], writes=[b_bbs[h]])

        def stB4():
            if sample:
                return
            for h in range(4):
                ev = evv(h)
                b4 = bbs[:, h, :].rearrange("p (j t) -> p j t", t=128)
                P.op("dve", lambda e, ev=ev, b4=b4: e.tensor_copy(ev[:, 16:20], b4[:, :, 63]), reads=[b_bbs[h]], writes=[b_ev[h]])
                P.op("act", lambda e, ev=ev, b4=b4: e.activation(out=ev[:, 0:4], in_=b4[:, :, 127], func=AF.Exp), reads=[b_bbs[h]], writes=[b_ev[h]])
                P.op("act", lambda e, ev=ev, b4=b4: e.activation(out=ev[:, 8:12], in_=b4[:, :, 63], func=AF.Exp), reads=[b_bbs[h]], writes=[b_ev[h]])
                P.op("dve", lambda e, ev=ev, b4=b4: e.tensor_tensor(out=b4, in0=b4, in1=ev[:, 16:20].unsqueeze(2).to_broadcast([128, 4, 128]),
                                                                   op=ALU.subtract), reads=[b_bbs[h], b_ev[h]], writes=[b_bbs[h]])

        def stB5():
            for h in range(4):
                P.op("act", lambda e, h=h: e.activation(out=hlf[:, h, 0:Tw], in_=bbs[:, h, :], func=AF.Exp), reads=[b_bbs[h]], writes=[b_hsc[h]])
            for h in range(4):
                P.op("act", lambda e, h=h: e.activation(out=bbs[:, h, :], in_=bbs[:, h, :], func=AF.Exp, scale=-1.0), reads=[b_bbs[h]], writes=[b_bbs[h]])

        def stB6():
            for h in range(4):
                if not sample:
                    P.op("dve", lambda e, h=h: e.tensor_copy(evv(h)[:, 4:8], hlf[:, h, 0:Tw].rearrange("p (j t) -> p j t", t=128)[:, :, 127]),
                         reads=[b_hsc[h]], writes=[b_ev[h]])
                else:
                    P.op("dve", lambda e, h=h: e.tensor_copy(e1s[:, h, :], hlf[:, h, 0:Tw].rearrange("p (r l) -> p r l", l=4)[:, :, 3]),
                         reads=[b_hsc[h]], writes=[b_e1s])
                P.op(veng(), lambda e, h=h: e.tensor_tensor(out=qbv[:, h, :], in0=qbv[:, h, :], in1=hlf[:, h, 0:Tw], op=ALU.mult),
                     reads=[b_hsc[h]], writes=[b_qb[h]])
                P.op("dve", lambda e, h=h: e.tensor_tensor(out=kbv[:, h, :], in0=hkk[:, h, 0:Tw], in1=bbs[:, h, :], op=ALU.mult),
                     reads=[b_hsk[h], b_bbs[h]], writes=[b_kb[h]])

        linear_fm(w_in, 2560, 512, h1, h1b, 8, Tw, cons_hf)
        linear_fm(w_in, 2048, 512, h1, h1b, 8, Tw, cons_hq)
        stB1()
        linear_tm(w_in, 1024, h1, h1b, Tw, cons_tm(v_v, b_v))
        stB3()
        linear_fm(w_in, 512, 512, h1, h1b, 8, Tw, cons_rot(kTv, b_kT, 128.0 ** -0.5))
        stB4()
        linear_fm(w_in, 0, 512, h1, h1b, 8, Tw, cons_rot(qTv, b_qT, 1.0))
        flush()
        stB5()
        emit_kd()
        linear_fm(w_in, 1536, 512, h1, h1b, 8, Tw, cons_silu(grv, b_gr))
        stB6()
        linear_tm(w_in, 3072, h1, h1b, Tw, cons_tm(hi_v, b_hi))
        for h in range(4):
            emit_kbT(h)
        linear_fm(w_in, 3584, 512, h1, h1b, 8, Tw, cons_silu(ghv, b_gh))

        P.mark(f"t{ti}:mixers")
        if not sample:
            attvs = [attT[:, :].rearrange("p (j t) -> p j t", t=512), attT2.rearrange("p (j t) -> p j t", t=512)]
            b_atts = [b_attT, b_attT2]
            maskr = cb[:, CB.sl("maskr")].rearrange("p (h x) -> p h x", x=512)
            qdecv = cb[:, CB.sl("qdec")].rearrange("p (h x) -> p h x", x=512)

            def r_att(h):
                attv, b_att = attvs[h % 2], b_atts[h % 2]
                for j in range(4):
                    ps, pb = psum()
                    P.op("pe", lambda e, j=j, ps=ps: e.matmul(ps[:, j * 128:512], kTv[:, h, j * 128:(j + 1) * 128],
                                                              qTv[:, h, j * 128:512], start=True, stop=True),
                         reads=[b_kT[h], b_qT[h]], writes=[pb])
                    P.op("dve", lambda e, j=j, ps=ps: e.tensor_tensor(out=attv[:, j, j * 128:512], in0=ps[:, j * 128:512],
                                                                      in1=maskr[:, h, 0:512 - j * 128], op=ALU.mult),
                         reads=[pb, b_const], writes=[b_att])

            def r_rest(h):
                attv, b_att = attvs[h % 2], b_atts[h % 2]
                if ti > 0:
                    P.op(veng(), lambda e: e.tensor_tensor(out=qd[:, :], in0=qTv[:, h, :], in1=qdecv[:, h, :], op=ALU.mult),
                         reads=[b_qT[h], b_const], writes=[b_qd])
                ops, opb = psum("L")

                def pv(e):
                    ins = None
                    for j in range(4):
                        ins = e.matmul(ops[:, j * 128:512], v_v[:, j, h * 128:(h + 1) * 128], attv[:, j, j * 128:512],
                                       start=(j == 0), stop=(j == 3 and ti == 0))
                    if ti > 0:
                        ins = e.matmul(ops[:, :], S_retb[:, h * 128:(h + 1) * 128], qd[:, :], start=False, stop=True)
                    return ins
                P.op("pe", pv, reads=[b_att, b_qd, b_Srb[h]] + b_v, writes=[opb])
                ups, upb = psum()

                def su(e):
                    ins = None
                    for j in range(4):
                        ins = e.matmul(ups[:, 0:128], kd_v[:, h, j, :], v_v[:, j, h * 128:(h + 1) * 128],
                                       start=(j == 0), stop=(j == 3))
                    return ins
                P.op("pe", su, reads=[b_kd[h]] + b_v, writes=[upb])
                head_norm_gate(ops[:, :], opb, Tw, RGAIN + h, grv[:, h, :], b_gr[h], ofv[:, h, :], ofb[h])
                P.op("dve", lambda e: e.scalar_tensor_tensor(
                    out=S_ret[:, h * 128:(h + 1) * 128], in0=S_ret[:, h * 128:(h + 1) * 128], scalar=GAM[h] ** 512,
                    in1=ups[:, 0:128], op0=ALU.mult, op1=ALU.add), reads=[upb, b_Sr[h]], writes=[b_Sr[h]])
                P.op("act", lambda e: e.activation(out=S_retb[:, h * 128:(h + 1) * 128], in_=S_ret[:, h * 128:(h + 1) * 128],
                                                   func=AF.Copy), reads=[b_Sr[h]], writes=[b_Srb[h]])
            def ret_core():
                r_att(0)
                yield
                for h in range(4):
                    if h < 3:
                        r_att(h + 1)
                        yield
                    r_rest(h)
                    yield
            P.mark(f"t{ti}:hgrn")
            def hg_core():
                cmask = cb[:, CB.sl("cmask")]
                for h in range(4):
                    ev = evv(h)
                    aps, apb = psum()

                    def attm(e, h=h, aps=aps):
                        ins = None
                        for j in range(4):
                            ins = e.matmul(aps[:, j * 128:(j + 1) * 128], kbv[:, h, j * 128:(j + 1) * 128],
                                           qbv[:, h, j * 128:(j + 1) * 128], start=True, stop=True)
                        return ins
                    P.op("pe", attm, reads=[b_kb[h], b_qb[h]], writes=[apb])
                    P.op("dve", lambda e, aps=aps, h=h: e.tensor_tensor(
                        out=attHs[:, 4 * h:4 * h + 4, :], in0=aps[:, :].rearrange("p (j t) -> p j t", t=128),
                        in1=cmask.unsqueeze(1).to_broadcast([128, 4, 128]), op=ALU.mult),
                        reads=[apb, b_const], writes=b_attHs[4 * h:4 * h + 4])
                    ups, upb = psum()

                    def um(e, h=h, ups=ups):
                        ins = None
                        for j in range(4):
                            ins = e.matmul(ups[:, j * 128:(j + 1) * 128], kbT_v[:, h, j, :], hi_v[:, j, h * 128:(h + 1) * 128],
                                           start=True, stop=True)
                        return ins
                    P.op("pe", um, reads=[b_kbT[h]] + b_hi, writes=[upb])
                    for j in range(4):
                        c = h * 4 + j
                        first = (ti == 0 and j == 0)
                        if not first:
                            P.op("act", lambda e, h=h, j=j, ev=ev, c=c: e.activation(
                                out=Sps[:, c, :], in_=S_hg[:, h * 128:(h + 1) * 128], func=AF.Copy, scale=ev[:, 8 + j:9 + j]),
                                reads=[b_Sh[h], b_ev[h]], writes=[b_Sps[c]])
                        P.op("dve", lambda e, h=h, j=j, ev=ev: e.tensor_scalar(
                            out=S_hg[:, h * 128:(h + 1) * 128], in0=S_hg[:, h * 128:(h + 1) * 128], scalar1=ev[:, j:j + 1],
                            scalar2=None, op0=ALU.mult), reads=[b_Sh[h], b_ev[h]], writes=[b_Sh[h]])
                        P.op("dve", lambda e, h=h, j=j, ev=ev, ups=ups: e.scalar_tensor_tensor(
                            out=S_hg[:, h * 128:(h + 1) * 128], in0=ups[:, j * 128:(j + 1) * 128], scalar=ev[:, 4 + j:5 + j],
                            in1=S_hg[:, h * 128:(h + 1) * 128], op0=ALU.mult, op1=ALU.add),
                            reads=[upb, b_Sh[h], b_ev[h]], writes=[b_Sh[h]])
                        if j % 2 == 1:
                            yield
                for hp in range(2):
                    heads = (2 * hp, 2 * hp + 1)
                    for h in heads:
                        ops, opb = psum("L")
                        for j in range(4):
                            c = h * 4 + j
                            first = (ti == 0 and j == 0)

                            def pvh(e, h=h, j=j, ops=ops, c=c, first=first):
                                ins = e.matmul(ops[:, j * 128:(j + 1) * 128], hi_v[:, j, h * 128:(h + 1) * 128], attHs[:, c, :],
                                               start=True, stop=first)
                                if not first:
                                    ins = e.matmul(ops[:, j * 128:(j + 1) * 128], Sps[:, c, :],
                                                   qbv[:, h, j * 128:(j + 1) * 128], start=False, stop=True)
                                return ins
                            P.op("pe", pvh, reads=[b_attHs[c], b_hi[j], b_Sps[c], b_qb[h]], writes=[opb])
                        head_norm_gate(ops[:, :], opb, Tw, HGAIN + h, ghv[:, h, :], b_gh[h], ofv[:, 4 + h, :], ofb[4 + h])
                        yield

            gens = [ret_core(), hg_core()]
            while gens:
                for g_ in list(gens):
                    try:
                        next(g_)
                    except StopIteration:
                        gens.remove(g_)
        else:
            sample_mixers(Tw, qTv, kTv, qbv, kbv, v_v, hi_v, kd_v, kbT_v, grv, ghv, ofv, ofb)

        P.mark(f"t{ti}:w_out")
        nrm = psum("L")
        linear_fm(w_out, 0, D, ofT, ofb, 8, Tw, resid_add(Tw, DEBUG["mixer"], sq=(2,) + nrm))
        P.mark(f"t{ti}:xattn")
        rmsnorm(0, G_XA, 2, Tw, pre=nrm)
        xq, xqb = G[2], gbuf[2]
        xqv = fmview(xq, 8, Tw)

        def cons_xq(c, ps, pb):
            P.op("act", lambda e: e.activation(out=xqv[:, c, :], in_=ps[:, 0:Tw], func=AF.Copy), reads=[pb], writes=[xqb[c]])
        linear_fm(w_xq, 0, D, G[0], gbuf[0], 8, Tw, cons_xq)
        oa, oab = G[1], gbuf[1]
        oav = fmview(oa, 8, Tw)
        if not sample:
            def x_scores(hh):
                pk = hh % 2
                PTv = PTs[pk][:, :].rearrange("p (m t) -> p m t", t=512)
                for mb in range(2):
                    ps, pb = psum()

                    def sc(e, mb=mb, ps=ps):
                        e.matmul(ps[:, :], KTv[:, 2 * hh, mb * 128:(mb + 1) * 128], xqv[:, 2 * hh, :], start=True, stop=False)
                        return e.matmul(ps[:, :], KTv[:, 2 * hh + 1, mb * 128:(mb + 1) * 128], xqv[:, 2 * hh + 1, :], start=False, stop=True)
                    P.op("pe", sc, reads=[b_KT, xqb[2 * hh], xqb[2 * hh + 1]], writes=[pb])
                    P.op("act", lambda e, mb=mb, ps=ps: e.activation(out=PTv[:, mb, :], in_=ps[:, :], func=AF.Exp, scale=1.0 / 16.0),
                         reads=[pb], writes=[b_PTs[pk]])

            def x_pv(hh):
                pk = hh % 2
                PTv = PTs[pk][:, :].rearrange("p (m t) -> p m t", t=512)
                rd, b_rd = rdens[pk], b_rdens[pk]
                dps, dpb = psum()

                def den(e):
                    e.matmul(dps[:, :], onesb, PTv[:, 0, :], start=True, stop=False)
                    return e.matmul(dps[:, :], onesb, PTv[:, 1, :], start=False, stop=True)
                P.op("pe", den, reads=[b_PTs[pk], b_const], writes=[dpb])
                P.op("act", lambda e: e.activation(out=rd[:, :], in_=dps[:, :], func=AF.Ln), reads=[dpb], writes=[b_rd])
                P.op("act", lambda e: e.activation(out=rd[:, :], in_=rd[:, :], func=AF.Exp, scale=-1.0), reads=[b_rd], writes=[b_rd])
                for i in range(2):
                    ps, pb = psum()

                    def pvx(e, i=i, ps=ps):
                        c0 = hh * 256 + i * 128
                        e.matmul(ps[:, :], Vbv[:, 0, c0:c0 + 128], PTv[:, 0, :], start=True, stop=False)
                        return e.matmul(ps[:, :], Vbv[:, 1, c0:c0 + 128], PTv[:, 1, :], start=False, stop=True)
                    P.op("pe", pvx, reads=[b_PTs[pk], b_Vb], writes=[pb])
                    P.op("dve", lambda e, i=i, ps=ps: e.tensor_tensor(out=oav[:, 2 * hh + i, :], in0=ps[:, :], in1=rd[:, :], op=ALU.mult),
                         reads=[pb, b_rd], writes=[oab[2 * hh + i]])
            x_scores(0)
            for hh in range(4):
                if hh < 3:
                    x_scores(hh + 1)
                x_pv(hh)
        else:
            sample_xattn(xqv, xqb, oav, oab)
        nrm = psum("L")
        linear_fm(w_xo, 0, D, oa, oab, 8, Tw, resid_add(Tw, DEBUG["xattn"], sq=(2,) + nrm))
        P.mark(f"t{ti}:ffn")
        rmsnorm(0, G_FFN, 2, Tw, pre=nrm)
        hidv = fmview(hid, 22, Tw)
        sg = tmpA
        for g0 in range(0, DFF, 256):
            (gv, uv), slb = take_slab(w_gate, g0, multi=True)
            for j in range(2):
                hc = g0 // 128 + j
                gps, gpb = psum()
                ups, upb = psum()

                def mmg(e, sl=gv, j=j, ps=gps):
                    ins = None
                    inv = fmview(G[0], 8, Tw)
                    for k in range(8):
                        ins = e.matmul(ps[:, 0:Tw], sl[:, k, j * 128:(j + 1) * 128], inv[:, k, :], start=(k == 0), stop=(k == 7))
                    return ins
                P.op("pe", mmg, reads=[slb] + gbuf[0], writes=[gpb])
                P.op("pe", lambda e, sl=uv, j=j, ps=ups, mmg=mmg: mmg(e, sl, j, ps), reads=[slb] + gbuf[0], writes=[upb])
                P.op("act", lambda e, gps=gps: e.activation(out=sg[:, 0:Tw], in_=gps[:, 0:Tw], func=AF.Silu), reads=[gpb], writes=[b_tA])
                P.op("dve", lambda e, ups=ups, hc=hc: e.tensor_tensor(out=hidv[:, hc, :], in0=sg[:, 0:Tw], in1=ups[:, 0:Tw], op=ALU.mult),
                     reads=[upb, b_tA], writes=[b_hid[hc]])
        nrm = psum("L")
        linear_fm(w_down, 0, D, hid, b_hid, 22, Tw, resid_add(Tw, DEBUG["ffn"], sq=(2,) + nrm))
        P.mark(f"t{ti}:final")
        rmsnorm(0, G_FIN, 2, Tw, out_f32_inplace=True, pre=nrm)
        store_tile(dst_rows, r0, Tw)

    hlf_t = sb("hlf", 4 * 512)
    hk_t = attT
    hlf = hlf_t[:, :].rearrange("p (h t) -> p h t", h=4)
    hkk = hk_t[:, :].rearrange("p (h t) -> p h t", h=4)
    b_hsc = [Buf(f"hsc{h}") for h in range(4)]
    b_hsk = [Buf(f"hsk{h}") for h in range(4)]
    e1s_t = sb("e1s", 64)
    e1s = e1s_t[:, :].rearrange("p (h r) -> p h r", h=4)
    b_e1s = Buf("e1s")

    def tail(t, start, n, dt_bytes_ratio=1):
        return t[:, start:start + n]

    NRING = 3
    S0r = [xT[:, 512 + i * 512:512 + (i + 1) * 512] for i in range(NRING)]
    S0h = [xT[:, 2048 + i * 512:2048 + (i + 1) * 512] for i in range(NRING)]
    S0rb = [hid[:, 1408 + i * 512:1408 + (i + 1) * 512] for i in range(NRING)]
    S0hb = [hid[:, 2944 + i * 512:2944 + (i + 1) * 512] for i in range(NRING)]
    Kbf = [hid[:, 4480 + i * 2048:4480 + (i + 1) * 2048] for i in range(2)]
    KTr = hid[:, 8576:8576 + 2048]
    Vring = [G[1][:, 512:512 + 2048], G[2][:, 512:512 + 2048]]
    PS_s = G[0][:, 512:512 + 512]
    KDr = G[0][:, 1024:1024 + 2048]

    b_S0 = [Buf(f"S0_{i}") for i in range(NRING)]
    b_S0b = [Buf(f"S0b_{i}") for i in range(NRING)]
    b_Kbf = [Buf("Kbf0"), Buf("Kbf1")]
    b_Vr = [Buf("Vr0"), Buf("Vr1")]
    b_KTr = Buf("KTr"); b_PSs = Buf("PSs"); b_KD = Buf("KD")

    def sample_mixers(Tw, qTv, kTv, qbv, kbv, v_v, hi_v, kd_v, kbT_v, grv, ghv, ofv, ofb):
        msr = cb[0:64, CB.sl("maskSr")].rearrange("p (h x) -> p h x", x=64)
        msh = cb[0:64, CB.sl("maskSh")]
        qds = cb[:, CB.sl("qdecS")].rearrange("p (h x) -> p h x", x=64)
        memb = cb[0:64, CB.sl("memb")]
        decS = cf[:, CF.sl("decS")]
        qdv = attT[:, 0:256].rearrange("p (h t) -> p h t", t=64)
        P.op("dve", lambda e: e.tensor_tensor(out=qdv, in0=qTv, in1=qds, op=ALU.mult), reads=b_qT + [b_const], writes=[b_attT])
        aR = attT[0:64, 256:512].rearrange("p (h t) -> p h t", t=64)
        aH = attT[0:64, 512:768].rearrange("p (h t) -> p h t", t=64)
        oR, oRb = psum("L")
        oH, oHb = psum("L")
        for (isr, av, ov, ob_) in ((True, aR, oR, oRb), (False, aH, oH, oHb)):
            for h in range(4):
                ps, pb = psum()
                if isr:
                    P.op("pe", lambda e, h=h, ps=ps: e.matmul(ps[0:64, 0:64], kTv[:, h, :], qTv[:, h, :], start=True, stop=True),
                         reads=[b_kT[h], b_qT[h]], writes=[pb])
                    P.op("dve", lambda e, h=h, ps=ps, av=av: e.tensor_tensor(out=av[:, h, :], in0=ps[0:64, 0:64], in1=msr[:, h, :], op=ALU.mult),
                         reads=[pb, b_const], writes=[b_aR])
                else:
                    P.op("pe", lambda e, h=h, ps=ps: e.matmul(ps[0:64, 0:64], kbv[:, h, :], qbv[:, h, :], start=True, stop=True),
                         reads=[b_kb[h], b_qb[h]], writes=[pb])
                    P.op("dve", lambda e, h=h, ps=ps, av=av: e.tensor_tensor(out=av[:, h, :], in0=ps[0:64, 0:64], in1=msh, op=ALU.mult),
                         reads=[pb, b_const], writes=[b_aH])
            tok = b_aR if isr else b_aH
            vv = v_v if isr else hi_v
            vb = b_v[0] if isr else b_hi[0]

            def pvs(e, av=av, ov=ov, vv=vv):
                ins = None
                for h in range(4):
                    ins = e.matmul(ov[:, h * 64:(h + 1) * 64], vv[0:64, 0, h * 128:(h + 1) * 128], av[:, h, :], start=(h == 0), stop=False)
                return ins
            P.op("pe", pvs, reads=[tok, vb], writes=[ob_])
        stg = [Kbf[0], Kbf[1], Vring[0], Vring[1]]
        b_stg = [b_Kbf[0], b_Kbf[1], b_Vr[0], b_Vr[1]]

        def s_load(r):
            k = r % NRING
            P.dma_load("sp", b_S0[k], S0r[k].rearrange("p (h e) -> p h e", h=4), sret[r].rearrange("h d e -> d h e"))
            P.dma_load("sp", b_S0[k], S0h[k].rearrange("p (h e) -> p h e", h=4), shg[r].rearrange("h d e -> d h e"))

        b_KD2 = [Buf("KD0"), Buf("KD1")]

        def kd_views(bi):
            KDrv_ = KDr[:, bi * 1024:(bi + 1) * 1024].rearrange("p (h r d) -> p h r d", h=4, r=2)[0:64]
            KDhv_ = KTr[0:64, bi * 1024:(bi + 1) * 1024].rearrange("p (h r d) -> p h r d", h=4, r=2)
            return KDrv_, KDhv_

        def kd_expand(g2):
            bi = g2 % 2
            KDrv_, KDhv_ = kd_views(bi)
            for h in range(4):
                mb_ = memb[:, 2 * g2:2 * g2 + 2].unsqueeze(2).to_broadcast([64, 2, 128])
                P.op("pool", lambda e, h=h, mb_=mb_, KDrv_=KDrv_: e.tensor_tensor(
                    out=KDrv_[:, h, :, :], in0=kd_v[0:64, h, 0, :].unsqueeze(1).to_broadcast([64, 2, 128]), in1=mb_, op=ALU.mult),
                    reads=[b_kd[h], b_const], writes=[b_KD2[bi]])
                P.op("pool", lambda e, h=h, mb_=mb_, KDhv_=KDhv_: e.tensor_tensor(
                    out=KDhv_[:, h, :, :], in0=kbT_v[0:64, h, 0, :].unsqueeze(1).to_broadcast([64, 2, 128]), in1=mb_, op=ALU.mult),
                    reads=[b_kbT[h], b_const], writes=[b_KD2[bi]])

        kd_expand(0)
        for g in range(1):
            for r in range(NREQ):
                g2, rl = r // 2, r % 2
                if rl == 0 and g2 + 1 < NREQ // 2:
                    kd_expand(g2 + 1)
                KDrv, KDhv = kd_views(g2 % 2)
                b_KD = b_KD2[g2 % 2]
                k = r % NRING
                if r == 0:
                    s_load(0)
                    s_load(1)
                if r + 2 < NREQ:
                    s_load(r + 2)
                P.op("act", lambda e, k=k: e.activation(out=S0rb[k], in_=S0r[k], func=AF.Copy), reads=[b_S0[k]], writes=[b_S0b[k]])
                P.op("act", lambda e, k=k: e.activation(out=S0hb[k], in_=S0h[k], func=AF.Copy), reads=[b_S0[k]], writes=[b_S0b[k]])
                last = (r == NREQ - 1)

                def inter(e, k=k, r=r, last=last):
                    ins = None
                    for h in range(4):
                        ins = e.matmul(oR[:, h * 64 + 4 * r:h * 64 + 4 * r + 4], S0rb[k][:, h * 128:(h + 1) * 128],
                                       qdv[:, h, 4 * r:4 * r + 4], start=False, stop=(last and h == 3))
                    for h in range(4):
                        ins = e.matmul(oH[:, h * 64 + 4 * r:h * 64 + 4 * r + 4], S0hb[k][:, h * 128:(h + 1) * 128],
                                       qbv[:, h, 4 * r:4 * r + 4], start=False, stop=(last and h == 3))
                    return ins
                P.op("pe", inter, reads=[b_S0b[k], b_attT] + b_qb, writes=[oRb, oHb])
                uR, uRb = psum()
                uH, uHb = psum()

                def ust(e, rl=rl, uR=uR, uH=uH, KDrv=KDrv, KDhv=KDhv):
                    ins = None
                    for h in range(4):
                        ins = e.matmul(uR[:, h * 128:(h + 1) * 128], KDrv[:, h, rl, :], v_v[0:64, 0, h * 128:(h + 1) * 128], start=True, stop=True)
                    for h in range(4):
                        ins = e.matmul(uH[:, h * 128:(h + 1) * 128], KDhv[:, h, rl, :], hi_v[0:64, 0, h * 128:(h + 1) * 128], start=True, stop=True)
                    return ins
                P.op("pe", ust, reads=[b_KD, b_v[0], b_hi[0]], writes=[uRb, uHb])
                si = r % 4
                sf = stg[si].bitcast(F32)
                Snr, Snh = sf[:, 0:512], sf[:, 512:1024]
                decb = decS.unsqueeze(2).to_broadcast([128, 4, 128])
                P.op("pool", lambda e, k=k, decb=decb, Snr=Snr: e.tensor_tensor(out=Snr.rearrange("p (h e) -> p h e", h=4),
                                                                               in0=S0r[k].rearrange("p (h e) -> p h e", h=4), in1=decb, op=ALU.mult),
                     reads=[b_S0[k], b_const], writes=[b_stg[si]])
                P.op("dve", lambda e, uR=uR, Snr=Snr: e.tensor_tensor(out=Snr, in0=Snr, in1=uR[:, :], op=ALU.add),
                     reads=[uRb, b_stg[si]], writes=[b_stg[si]])
                e1b = e1s[:, :, r].unsqueeze(2).to_broadcast([128, 4, 128])
                P.op("dve", lambda e, k=k, uH=uH, Snh=Snh: e.tensor_tensor(out=Snh, in0=S0h[k], in1=uH[:, :], op=ALU.add),
                     reads=[uHb, b_S0[k], b_stg[si]], writes=[b_stg[si]])
                P.op("dve", lambda e, e1b=e1b, Snh=Snh: e.tensor_tensor(out=Snh.rearrange("p (h e) -> p h e", h=4),
                                                                        in0=Snh.rearrange("p (h e) -> p h e", h=4), in1=e1b, op=ALU.mult),
                     reads=[b_stg[si], b_e1s], writes=[b_stg[si]])
                P.dma_store("sp", b_stg[si], rets[r].rearrange("h d e -> d h e"), Snr.rearrange("p (h e) -> p h e", h=4))
                P.dma_store("sp", b_stg[si], hgs[r].rearrange("h d e -> d h e"), Snh.rearrange("p (h e) -> p h e", h=4))
        for h in range(4):
            head_norm_gate(oR[:, h * 64:(h + 1) * 64], oRb, Tw, RGAIN + h, grv[:, h, :], b_gr[h], ofv[:, h, :], ofb[h])
        for h in range(4):
            head_norm_gate(oH[:, h * 64:(h + 1) * 64], oHb, Tw, HGAIN + h, ghv[:, h, :], b_gh[h], ofv[:, 4 + h, :], ofb[4 + h])

    def sample_xattn(xqv, xqb, oav, oab):
        KTrv = KTr.rearrange("p (c m) -> p c m", m=256)
        PSv = PS_s.rearrange("p (m r h l) -> p m r h l", m=2, r=16, h=4)
        ops, opb = psum("L")
        opv = ops[:, :].rearrange("p (c t) -> p c t", t=64)
        b_PSr = [Buf(f"PSr{r}") for r in range(NREQ)]
        for r in range(NREQ):
            k = r % 2
            kv = Kbf[k].rearrange("p (b n) -> p b n", n=1024)
            vv = Vring[k].rearrange("p (b n) -> p b n", n=1024)
            P.dma_load("pool", b_Kbf[k], kv, cmk[r].rearrange("(b p) n -> p b n", p=128))
            P.dma_load("pool", b_Vr[k], vv, cmv[r].rearrange("(b p) n -> p b n", p=128))
            for mb in range(2):
                ps, pb = psum()
                psb = ps[:, :].bitcast(BF16)

                def tr(e, mb=mb, kv=kv, psb=psb):
                    ins = None
                    for c in range(8):
                        ins = e.transpose(psb[:, c * 128:(c + 1) * 128], kv[:, mb, c * 128:(c + 1) * 128], identb)
                    return ins
                P.op("pe", tr, reads=[b_Kbf[k], b_const], writes=[pb])
                src = psb[:, :].rearrange("p (c m) -> p c m", m=128)
                dst = KTrv[:, :, mb * 128:(mb + 1) * 128]
                if mb == 0:
                    P.op("act", lambda e, src=src, dst=dst: e.activation(out=dst, in_=src, func=AF.Copy), reads=[pb], writes=[b_KTr])
                else:
                    P.op("dve", lambda e, src=src, dst=dst: e.tensor_copy(dst, src), reads=[pb], writes=[b_KTr])
            sps, spb = psum()
            spv = sps[:, 0:32].rearrange("p (m h l) -> p m h l", m=2, h=4)

            def sc(e, r=r, spv=spv):
                ins = None
                for mb in range(2):
                    for hh in range(4):
                        for i in range(2):
                            ins = e.matmul(spv[:, mb, hh, :], KTrv[:, 2 * hh + i, mb * 128:(mb + 1) * 128],
                                           xqv[:, 2 * hh + i, 4 * r:4 * r + 4], start=(i == 0), stop=(i == 1))
                return ins
            P.op("pe", sc, reads=[b_KTr] + list(xqb), writes=[spb])
            P.op("act", lambda e, r=r, spv=spv: e.activation(out=PSv[:, :, r, :, :], in_=spv, func=AF.Exp, scale=1.0 / 16.0),
                 reads=[spb], writes=[b_PSr[r]])

            def pvx(e, r=r, vv=vv):
                ins = None
                for hh in range(4):
                    for i in range(2):
                        c0 = hh * 256 + i * 128
                        for mb in range(2):
                            ins = e.matmul(opv[:, 2 * hh + i, 4 * r:4 * r + 4], vv[:, mb, c0:c0 + 128], PSv[:, mb, r, hh, :],
                                           start=(mb == 0), stop=(mb == 1))
                return ins
            P.op("pe", pvx, reads=[b_Vr[k], b_PSr[r]], writes=[opb])
        dps, dpb = psum()

        def den(e):
            e.matmul(dps[:, 0:256], onesb, PS_s[:, 0:256], start=True, stop=False)
            return e.matmul(dps[:, 0:256], onesb, PS_s[:, 256:512], start=False, stop=True)
        P.op("pe", den, reads=b_PSr + [b_const], writes=[dpb])
        P.op("act", lambda e: e.activation(out=rden[:, 0:256], in_=dps[:, 0:256], func=AF.Ln), reads=[dpb], writes=[b_rden])
        P.op("act", lambda e: e.activation(out=rden[:, 0:256], in_=rden[:, 0:256], func=AF.Exp, scale=-1.0), reads=[b_rden], writes=[b_rden])
        rdv = rden[:, 0:256].rearrange("p (r h l) -> p r h l", r=16, h=4)
        for hh in range(4):
            in1 = rdv[:, :, hh, :].unsqueeze(1).to_broadcast([128, 2, 16, 4])
            P.op("dve", lambda e, hh=hh, in1=in1: e.tensor_tensor(
                out=oav[:, 2 * hh:2 * hh + 2, :].rearrange("p c (r l) -> p c r l", l=4),
                in0=opv[:, 2 * hh:2 * hh + 2, :].rearrange("p c (r l) -> p c r l", l=4), in1=in1, op=ALU.mult),
                reads=[opb, b_rden], writes=[oab[2 * hh], oab[2 * hh + 1]])

    for ti in range(4):
        run_tile(ti, 512, xp, ti * 512, yp, False)
    P.dma_store("sp", b_Sr, retp.rearrange("h d e -> d h e"), S_ret[:, :].rearrange("p (h e) -> p h e", h=4))
    P.dma_store("sp", b_Sh, hgp.rearrange("h d e -> d h e"), S_hg[:, :].rearrange("p (h e) -> p h e", h=4))
    P.barrier()
    run_tile(4, NS, xs, 0, ys, True)
    assert wstate["taken"] == len(plan), (wstate["taken"], len(plan))
    P.build()
    _CACHE["marks"] = P.mark_instr
    return nc


_CACHE = {}


def kernel(x_prompt, x_sample, mem_prompt, state_ret, state_hgrn, cache_mem_k, cache_mem_v,
           g_mix, w_in, ret_gain, hg_gain, hg_lb, w_out, g_xa, g_mem, w_xq, w_xk, w_xv, w_xo,
           g_ffn, w_gate, w_up, w_down, g_final):
    f = lambda a: np.ascontiguousarray(np.asarray(a, dtype=np.float32))
    if "nc" not in _CACHE:
        _CACHE["nc"] = build_program()
        _CACHE["consts"] = make_consts()
    nc = _CACHE["nc"]
    cf, cb, cs = (a.copy() for a in _CACHE["consts"])

    def fm(v, n):
        return f(v).reshape(n, 128).T
    pv = np.concatenate([fm(g_mix[0], 8), fm(g_xa[0], 8), fm(g_mem[0], 8), fm(g_ffn[0], 8), fm(g_final, 8),
                         fm(ret_gain[0], 4), fm(hg_gain[0], 4), fm(hg_lb[0], 4), fm(hg_lb[1], 4)], axis=1)
    cf[:, CF.sl("pvec")] = pv
    shared = {"w_in": f(w_in[0]), "w_out": f(w_out[0]), "w_xq": f(w_xq[0]), "w_xk": f(w_xk[0]), "w_xv": f(w_xv[0]),
              "w_xo": f(w_xo[0]), "w_gate": f(w_gate[0]), "w_up": f(w_up[0]), "w_down": f(w_down[0]),
              "cf": cf, "cb": cb, "cs": cs}
    in_maps = []
    for c in range(NCORES):
        rs = slice(c * NREQ, (c + 1) * NREQ)
        m = dict(shared)
        m["xp"] = f(x_prompt[c]); m["xs"] = f(x_sample[rs]).reshape(NS, D); m["memp"] = f(mem_prompt[c])
        m["sret"] = f(state_ret[0, rs]); m["shg"] = f(state_hgrn[0, rs])
        m["cmk"] = f(cache_mem_k[0, rs]).reshape(NREQ, NMEM, D); m["cmv"] = f(cache_mem_v[0, rs]).reshape(NREQ, NMEM, D)
        in_maps.append(m)
    res = run_bass_kernel_spmd(nc, in_maps, core_ids=list(range(NCORES)))
    R = res.results
    y_prompt = np.stack([R[c]["yp"] for c in range(NCORES)], 0)
    y_sample = np.concatenate([R[c]["ys"].reshape(NREQ, DEC, D) for c in range(NCORES)], 0)
    ret_p = np.stack([R[c]["retp"] for c in range(NCORES)], 0)[None]
    hg_p = np.stack([R[c]["hgp"] for c in range(NCORES)], 0)[None]
    mk_p = np.stack([R[c]["mkp"].reshape(NMEM, 4, 256) for c in range(NCORES)], 0)[None]
    mv_p = np.stack([R[c]["mvp"].reshape(NMEM, 4, 256) for c in range(NCORES)], 0)[None]
    ret_s = np.concatenate([R[c]["rets"] for c in range(NCORES)], 0)[None]
    hg_s = np.concatenate([R[c]["hgs"] for c in range(NCORES)], 0)[None]
    return (y_prompt.astype(np.float32), y_sample.astype(np.float32), ret_p.astype(np.float32), hg_p.astype(np.float32),
            mk_p.astype(np.float32), mv_p.astype(np.float32), ret_s.astype(np.float32), hg_s.astype(np.float32))
```

```python
import numpy as np
import concourse.bass as bass
import concourse.mybir as mybir
from concourse.bass_utils import run_bass_kernel_spmd

F32 = mybir.dt.float32
BF16 = mybir.dt.bfloat16
AF = mybir.ActivationFunctionType
ALU = mybir.AluOpType

D = 1024
SEQ = 2048
NREQ = 16
DEC = 4
NS = NREQ * DEC
PAST = 16384
DFF = 2816
NMEM = 256
EPS = 1e-6
NCORES = 8
GAM = [1.0 - 2.0 ** (-5.0 - h) for h in range(4)]
DEBUG = {"mixer": True, "xattn": True, "ffn": True}
STRICT_WAR = True


class Buf:
    __slots__ = ("name", "w", "r", "dsem", "dcnt")

    def __init__(self, name=""):
        self.name = name
        self.w = None
        self.r = {}
        self.dsem = {}
        self.dcnt = {}


class Prog:
    ENGS = ("pe", "act", "dve", "pool", "sp")

    def __init__(self, nc):
        self.nc = nc
        self.sem = {e: nc.alloc_semaphore(f"s_{e}") for e in self.ENGS}
        self.cnt = {e: 0 for e in self.ENGS}
        self.known = {e: {} for e in self.ENGS}
        self.q = {e: [] for e in self.ENGS}
        self.final = {}
        self.nsem = 0
        self.marks = []
        self.mark_instr = []

    def _need(self, eng, dep, waits):
        if dep is None:
            return
        sem, val = dep
        k = sem.name
        if self.known[eng].get(k, 0) >= val:
            return
        self.known[eng][k] = val
        waits.append((sem, val))

    def _deps(self, eng, reads, writes):
        waits = []
        for b in reads:
            self._need(eng, b.w, waits)
        for b in writes:
            self._need(eng, b.w, waits)
            for k, dep in b.r.items():
                if k == eng and not STRICT_WAR:
                    continue
                self._need(eng, dep, waits)
        return waits

    def op(self, eng, emit, reads=(), writes=()):
        waits = self._deps(eng, reads, writes)
        self.cnt[eng] += 1
        me = (self.sem[eng], self.cnt[eng])
        for b in reads:
            b.r[eng] = me
        for b in writes:
            b.w = me
            b.r = {}
        self.q[eng].append((waits, emit, (self.sem[eng], 1)))

    def _dsem(self, b, eng):
        kind = "sw" if eng == "pool" else "hw"
        if kind not in b.dsem:
            b.dsem[kind] = self.nc.alloc_semaphore(f"dq{self.nsem}")
            b.dcnt[kind] = 0
            self.nsem += 1
        b.dcnt[kind] += 16
        return b.dsem[kind], b.dcnt[kind]

    def dma_load(self, eng, buf, out, in_):
        waits = self._deps(eng, [], [buf])
        sem, cnt = self._dsem(buf, eng)
        buf.w = (sem, cnt)
        buf.r = {}
        self.q[eng].append((waits, lambda e: e.dma_start(out=out, in_=in_), (sem, 16)))

    def dma_store(self, eng, buf, out, in_):
        bufs = buf if isinstance(buf, (list, tuple)) else [buf]
        waits = self._deps(eng, bufs, [])
        own = bufs[0]
        sem, cnt = self._dsem(own, eng)
        for b in bufs:
            b.r["dma_" + sem.name] = (sem, cnt)
        self.final[sem.name] = (sem, cnt)
        self.q[eng].append((waits, lambda e: e.dma_start(out=out, in_=in_), (sem, 16)))

    def mark(self, name):
        self.marks.append((name, len(self.q["pe"])))

    def barrier(self):
        for e in ("pe", "act", "dve", "pool", "sp"):
            waits = []
            for o in ("pe", "act", "dve", "pool"):
                if o != e and self.cnt[o]:
                    self._need(e, (self.sem[o], self.cnt[o]), waits)
            if waits:
                self.q[e].append((waits, None, None))

    def build(self):
        nc = self.nc
        waits = list(self.final.values())
        for e in ("pe", "act", "dve", "pool"):
            if self.cnt[e]:
                waits.append((self.sem[e], self.cnt[e]))
        self.q["sp"].append((waits, None, None))

        prog = self

        class CountPE:
            def __init__(self, e):
                self.e = e
                self.n = 0

            def matmul(self, *a, **k):
                self.n += 1
                return self.e.matmul(*a, **k)

            def transpose(self, *a, **k):
                self.n += 1
                return self.e.transpose(*a, **k)

        def replay(items, e, count=False):
            ce = CountPE(e) if count else e
            mi = 0
            for idx, (waits, emit, inc) in enumerate(items):
                if count:
                    while mi < len(prog.marks) and prog.marks[mi][1] <= idx:
                        prog.mark_instr.append((prog.marks[mi][0], ce.n))
                        mi += 1
                for sem, val in waits:
                    e.wait_ge(sem, val)
                if emit is None:
                    continue
                ins = emit(ce)
                if inc is not None:
                    ins.then_inc(inc[0], inc[1])

        with nc.Block() as block:
            @block.tensor
            def _(e):
                replay(self.q["pe"], e, count=True)

            @block.scalar
            def _(e):
                replay(self.q["act"], e)

            @block.vector
            def _(e):
                replay(self.q["dve"], e)

            @block.gpsimd
            def _(e):
                replay(self.q["pool"], e)

            @block.sync
            def _(e):
                replay(self.q["sp"], e)


class Cols:
    def __init__(self):
        self.off = {}
        self.n = 0

    def add(self, name, n):
        self.off[name] = (self.n, n)
        self.n += n

    def sl(self, name):
        o, n = self.off[name]
        return slice(o, o + n)


CF = Cols()
for _n, _k in (("ident", 128), ("perm", 128), ("kdec", 16), ("kdecS", 4), ("decS", 4), ("pvec", 56), ("ones", 128)):
    CF.add(_n, _k)
CB = Cols()
for _n, _k in (("identb", 128), ("cmask", 128), ("maskSr", 256), ("maskSh", 64), ("memb", 16),
               ("qdec", 2048), ("qdecS", 256), ("maskr", 2048), ("onesb", 128)):
    CB.add(_n, _k)


def make_consts():
    cf = np.zeros((128, CF.n), np.float64)
    cb = np.zeros((128, CB.n), np.float64)
    p = np.arange(128)
    cf[:, CF.sl("ident")] = np.eye(128)
    perm = np.zeros((128, 128))
    for m in range(64):
        perm[m + 64, m] = -1.0
        perm[m, m + 64] = 1.0
    cf[:, CF.sl("perm")] = perm
    cf[:, CF.sl("ones")] = 1.0
    kdec = np.zeros((128, 4, 4))
    for h in range(4):
        for b in range(4):
            kdec[:, h, b] = GAM[h] ** (511 - (b * 128 + p))
    cf[:, CF.sl("kdec")] = kdec.reshape(128, 16)
    for h in range(4):
        cf[:, CF.off["kdecS"][0] + h] = GAM[h] ** (3 - (p % 4))
        cf[:, CF.off["decS"][0] + h] = GAM[h] ** 4
    cb[:, CB.sl("identb")] = np.eye(128)
    cb[:, CB.sl("onesb")] = 1.0
    cb[:, CB.sl("cmask")] = (p[:, None] <= p[None, :]).astype(np.float64)
    x = np.arange(512)
    mr = np.zeros((128, 4, 512))
    qd = np.zeros((128, 4, 512))
    for h in range(4):
        dlt = x[None, :] - p[:, None]
        mr[:, h, :] = np.where(dlt >= 0, GAM[h] ** np.maximum(dlt, 0), 0.0)
        qd[:, h, :] = GAM[h] ** (x[None, :] + 1.0)
    cb[:, CB.sl("maskr")] = mr.reshape(128, 2048)
    cb[:, CB.sl("qdec")] = qd.reshape(128, 2048)
    t = np.arange(64)
    same = (t[:, None] // 4) == (t[None, :] // 4)
    dl = (t[None, :] % 4) - (t[:, None] % 4)
    msr = np.zeros((128, 4, 64))
    qds = np.zeros((128, 4, 64))
    for h in range(4):
        msr[:64, h, :] = np.where(same & (dl >= 0), GAM[h] ** np.maximum(dl, 0), 0.0)
        qds[:, h, :] = GAM[h] ** ((t[None, :] % 4) + 1.0)
    cb[:, CB.sl("maskSr")] = msr.reshape(128, 256)
    cb[:, CB.sl("qdecS")] = qds.reshape(128, 256)
    cb[:64, CB.sl("maskSh")] = (same & (dl >= 0)).astype(np.float64)
    memb = np.zeros((128, 16))
    memb[:64] = ((t[:, None] // 4) == np.arange(16)[None, :]).astype(np.float64)
    cb[:, CB.sl("memb")] = memb
    inv_freq = (np.float32(10000.0) ** (-(np.arange(64, dtype=np.float32) / np.float32(64)))).astype(np.float32)
    pos = np.concatenate([np.arange(SEQ, dtype=np.float32),
                          (PAST + (np.arange(NS) % 4)).astype(np.float32)])
    ang = (pos[:, None] * inv_freq[None, :]).astype(np.float32).astype(np.float64)
    cosT = np.cos(ang).T
    sinT = np.sin(ang).T
    cs = np.stack([np.concatenate([cosT, cosT], 0), np.concatenate([sinT, sinT], 0)], 1)
    return cf.astype(np.float32), cb.astype(np.float32), np.ascontiguousarray(cs.astype(np.float32))


def build_program():
    nc = bass.Bass("TRN2", target_bir_lowering=False)
    P = Prog(nc)

    def din(name, shape):
        return nc.dram_tensor(name, list(shape), F32, kind="ExternalInput").ap()

    def dout(name, shape):
        return nc.dram_tensor(name, list(shape), F32, kind="ExternalOutput").ap()

    xp = din("xp", [SEQ, D]); xs = din("xs", [NS, D]); memp = din("memp", [NMEM, D])
    sret = din("sret", [NREQ, 4, 128, 128]); shg = din("shg", [NREQ, 4, 128, 128])
    cmk = din("cmk", [NREQ, NMEM, D]); cmv = din("cmv", [NREQ, NMEM, D])
    w_in = din("w_in", [D, 4096]); w_out = din("w_out", [D, D]); w_xq = din("w_xq", [D, D])
    w_xk = din("w_xk", [D, D]); w_xv = din("w_xv", [D, D]); w_xo = din("w_xo", [D, D])
    w_gate = din("w_gate", [D, DFF]); w_up = din("w_up", [D, DFF]); w_down = din("w_down", [DFF, D])
    cf_d = din("cf", [128, CF.n]); cb_d = din("cb", [128, CB.n]); cs_d = din("cs", [128, 2, SEQ + NS])
    yp = dout("yp", [SEQ, D]); ys = dout("ys", [NS, D])
    retp = dout("retp", [4, 128, 128]); hgp = dout("hgp", [4, 128, 128])
    mkp = dout("mkp", [NMEM, D]); mvp = dout("mvp", [NMEM, D])
    rets = dout("rets", [NREQ, 4, 128, 128]); hgs = dout("hgs", [NREQ, 4, 128, 128])

    def sb(name, n, dt=F32):
        return nc.alloc_sbuf_tensor("sb_" + name, [128, n], dt)

    cf = sb("cf", CF.n); cb = sb("cb", CB.n, BF16)
    b_const = Buf("const")
    xT = sb("xT", 8 * 512)
    G = [sb(f"G{i}", 8 * 512, BF16) for i in range(3)]
    hid = sb("hid", 22 * 512, BF16)
    xio = [sb(f"xio{i}", 1024) for i in range(2)]
    rstd = sb("rstd", 512)
    qT = sb("qT", 4 * 512, BF16); kT = sb("kT", 4 * 512, BF16)
    gate_r = sb("gate_r", 4 * 512, BF16); gate_h = sb("gate_h", 4 * 512, BF16)
    tmpA = sb("tmpA", 512); tmpB = sb("tmpB", 512); tmpC = sb("tmpC", 512); tmpA2 = sb("tmpA2", 512)
    rot_n = [0]
    tmpD = tmpB; tmpE = tmpC
    qb = sb("qb", 4 * 512, BF16); kb = sb("kb", 4 * 512, BF16)
    v_tm = sb("v_tm", 4 * 512, BF16); hi_tm = sb("hi_tm", 4 * 512, BF16)
    kd_tm = sb("kd_tm", 16 * 128, BF16); kbT_tm = sb("kbT_tm", 16 * 128, BF16)
    attT = sb("attT", 4 * 512, BF16)
    attH = [sb(f"attH{i}", 128, BF16) for i in range(2)]
    S_ret = sb("S_ret", 512); S_retb = sb("S_retb", 512, BF16)
    S_hg = sb("S_hg", 512); Sp = sb("Sp", 512, BF16)
    evec = sb("evec", 80)
    o_sbs = [sb(f"o_sb{i}", 512) for i in range(2)]; sqos = [sb(f"sqo{i}", 512, BF16) for i in range(2)]
    rrs = [sb(f"rr{i}", 512) for i in range(2)]
    qd = sb("qd", 512, BF16)
    PTs = [sb(f"PT{i}", 2 * 512, BF16) for i in range(2)]; rdens = [sb(f"rden{i}", 512) for i in range(2)]
    rden = rdens[0]
    KT = sb("KT", 8 * 256, BF16); Vb = sb("Vb", 2 * 1024, BF16)
    cs = [sb("cs0", 2 * 512)]
    wslot = [sb(f"w{i}", 4096, BF16) for i in range(3)]
    ab = sb("ab", 16)

    ident = cf[:, CF.sl("ident")]; perm = cf[:, CF.sl("perm")]
    identb = cb[:, CB.sl("identb")]; onesb = cb[:, CB.sl("onesb")]; onesf = cf[:, CF.sl("ones")]
    pv0 = CF.off["pvec"][0]

    def pvec(i):
        return cf[:, pv0 + i:pv0 + i + 1]
    G_MIX, G_XA, G_MEM, G_FFN, G_FIN, RGAIN, HGAIN, LB0, LB1 = 0, 8, 16, 24, 32, 40, 44, 48, 52

    banks = [nc.alloc_psum_tensor(f"ps{i}", [128, 512], F32) for i in range(8)]
    bbuf = [Buf(f"ps{i}") for i in range(8)]
    rot = {"L": [0, 1], "S": [2, 3, 4, 5, 6, 7]}
    rpos = {"L": 0, "S": 0}

    def psum(kind="S"):
        lst = rot[kind]
        i = lst[rpos[kind] % len(lst)]
        rpos[kind] += 1
        return banks[i], bbuf[i]

    P.dma_load("sp", b_const, cf[:], cf_d)
    P.dma_load("pool", b_const, cb[:], cb_d)
    b_ab = Buf("ab")
    P.op("dve", lambda e: e.tensor_tensor(out=ab[:, 12:16], in0=cf[:, pv0 + LB0:pv0 + LB0 + 4],
                                          in1=cf[:, pv0 + LB1:pv0 + LB1 + 4], op=ALU.subtract),
         reads=[b_const], writes=[b_ab])
    P.op("act", lambda e: e.activation(out=ab[:, 12:16], in_=ab[:, 12:16], func=AF.Tanh, scale=0.5),
         reads=[b_ab], writes=[b_ab])
    P.op("dve", lambda e: e.tensor_scalar(out=ab[:, 0:4], in0=ab[:, 12:16], scalar1=0.25, scalar2=0.75,
                                          op0=ALU.mult, op1=ALU.add), reads=[b_ab], writes=[b_ab])
    P.op("dve", lambda e: e.tensor_scalar(out=ab[:, 4:8], in0=ab[:, 12:16], scalar1=-0.25, scalar2=0.25,
                                          op0=ALU.mult, op1=ALU.add), reads=[b_ab], writes=[b_ab])
    P.op("dve", lambda e: e.tensor_scalar(out=ab[:, 8:12], in0=ab[:, 12:16], scalar1=0.25, scalar2=-0.25,
                                          op0=ALU.mult, op1=ALU.add), reads=[b_ab], writes=[b_ab])

    wbuf = [Buf(f"w{i}") for i in range(3)]
    plan = []
    wstate = {"issued": 0, "taken": 0}

    def plan_linear(W, K, c0, ncols):
        nk = K // 128
        step = 512 if nk == 8 else 128
        for c in range(c0, c0 + ncols, step):
            plan.append(([(W, c, min(step, c0 + ncols - c))], nk))

    NSLAB_TILE = 33
    wscr = nc.dram_tensor("wscr", [NSLAB_TILE, 128, 4096], BF16, kind="Internal").ap()
    scr_gate = [False]

    def slab_views(i):
        parts, nk = plan[i]
        views, off = [], 0
        for (_, _, n) in parts:
            views.append(wslot[i % 3][:, off:off + nk * n].rearrange("p (k n) -> p k n", n=n))
            off += nk * n
        return views, off

    def issue_next():
        i = wstate["issued"]
        if i >= len(plan):
            return
        parts, nk = plan[i]
        slot = wslot[i % 3]
        views, tot = slab_views(i)
        t, s_ = ((i - 4) // NSLAB_TILE, (i - 4) % NSLAB_TILE) if i >= 4 else (-1, -1)
        if t >= 1:
            if not scr_gate[0]:
                scr_gate[0] = True
                P.q["sp"].append(([(wbuf[k].dsem[kd], wbuf[k].dcnt[kd]) for k in range(3) for kd in wbuf[k].dsem], None, None))
            P.dma_load("sp", wbuf[i % 3], slot[:, 0:tot], wscr[s_][:, 0:tot])
        else:
            for (W, c0, n), dst in zip(parts, views):
                P.dma_load("pool", wbuf[i % 3], dst, W[:, c0:c0 + n].rearrange("(k p) n -> p k n", p=128))
            if t == 0:
                P.dma_store("sp", wbuf[i % 3], wscr[s_][:, 0:tot], slot[:, 0:tot])
        wstate["issued"] += 1

    def take_slab(W, c0, hold=0, multi=False):
        i = wstate["taken"]
        assert plan[i][0][0][0] is W and plan[i][0][0][1] == c0, (i, plan[i][0][0][1], c0)
        while wstate["issued"] < min(i + 3 - hold, len(plan)):
            issue_next()
        wstate["taken"] += 1
        views, _ = slab_views(i)
        return (views if multi else views[0]), wbuf[i % 3]

    W_IN_ORDER = [2560, 2048, 1024, 512, 0, 1536, 3072, 3584]
    plan_linear(w_xk, D, 0, D); plan_linear(w_xv, D, 0, D)
    for _t in range(5):
        for c0 in W_IN_ORDER:
            plan_linear(w_in, D, c0, 512)
        plan_linear(w_out, D, 0, D); plan_linear(w_xq, D, 0, D); plan_linear(w_xo, D, 0, D)
        for g0 in range(0, DFF, 256):
            plan.append(([(w_gate, g0, 256), (w_up, g0, 256)], 8))
        plan_linear(w_down, DFF, 0, D)
    assert len(plan) == 4 + 5 * NSLAB_TILE, len(plan)

    def fmview(t, nch, Tw):
        return t[:, 0:nch * Tw].rearrange("p (c t) -> p c t", t=Tw)

    gbuf = [[Buf(f"G{i}_{c}") for c in range(8)] for i in range(3)]
    xTb = [Buf(f"xT{c}") for c in range(8)]
    b_rstd = Buf("rstd")
    xiob = [Buf("xio0"), Buf("xio1")]
    xio_n = [0]

    def load_tile(src_rows, r0, Tw):
        xv = fmview(xT, 8, Tw)
        nb = (Tw + 127) // 128
        for b in range(nb):
            n = min(128, Tw - b * 128)
            k = xio_n[0] % 2
            xio_n[0] += 1
            P.dma_load("sp", xiob[k], xio[k][0:n, :], src_rows[r0 + b * 128:r0 + b * 128 + n, :])
            for g in range(2):
                ps, pb = psum()

                def tr(e, k=k, g=g, n=n, ps=ps):
                    ins = None
                    for cc in range(4):
                        c = g * 4 + cc
                        ins = e.transpose(ps[:, cc * 128:cc * 128 + n], xio[k][0:n, c * 128:(c + 1) * 128], ident[0:n, 0:n])
                    return ins
                P.op("pe", tr, reads=[xiob[k], b_const], writes=[pb])
                src = ps[:, :].rearrange("p (c t) -> p c t", t=128)[:, :, 0:n]
                dst = xv[:, g * 4:g * 4 + 4, b * 128:b * 128 + n]
                eng = "act" if g == 0 else "dve"
                if eng == "act":
                    P.op("act", lambda e, dst=dst, src=src: e.activation(out=dst, in_=src, func=AF.Copy),
                         reads=[pb], writes=xTb[g * 4:g * 4 + 4])
                else:
                    P.op("dve", lambda e, dst=dst, src=src: e.tensor_copy(dst, src),
                         reads=[pb], writes=xTb[g * 4:g * 4 + 4])

    def store_tile(dst_rows, r0, Tw):
        xv = fmview(xT, 8, Tw)
        nb = (Tw + 127) // 128
        for b in range(nb):
            n = min(128, Tw - b * 128)
            k = xio_n[0] % 2
            xio_n[0] += 1
            for g in range(2):
                ps, pb = psum()

                def tr(e, g=g, n=n, ps=ps, b=b):
                    ins = None
                    for cc in range(4):
                        c = g * 4 + cc
                        ins = e.transpose(ps[0:n, cc * 128:(cc + 1) * 128], xv[:, c, b * 128:b * 128 + n], ident)
                    return ins
                P.op("pe", tr, reads=xTb[g * 4:g * 4 + 4] + [b_const], writes=[pb])
                dst = xio[k][0:n, g * 512:(g + 1) * 512]
                if g == 0:
                    P.op("act", lambda e, dst=dst, ps=ps, n=n: e.activation(out=dst, in_=ps[0:n, :], func=AF.Copy),
                         reads=[pb], writes=[xiob[k]])
                else:
                    P.op("dve", lambda e, dst=dst, ps=ps, n=n: e.tensor_copy(dst, ps[0:n, :]),
                         reads=[pb], writes=[xiob[k]])
            P.dma_store("sp", xiob[k], dst_rows[r0 + b * 128:r0 + b * 128 + n, :], xio[k][0:n, :])

    def rstd_from_psum(ps, pb, Tw, inv_n, out_ap, out_buf):
        P.op("act", lambda e: e.activation(out=out_ap, in_=ps[:, 0:Tw], func=AF.Ln, scale=inv_n, bias=epsb[:, 0:1]),
             reads=[pb, b_ab], writes=[out_buf])
        P.op("act", lambda e: e.activation(out=out_ap, in_=out_ap, func=AF.Exp, scale=-0.5),
             reads=[out_buf], writes=[out_buf])

    def rmsnorm(gi, gidx, sqi, Tw, out_f32_inplace=False, pre=None):
        xv = fmview(xT, 8, Tw)
        sqv = fmview(G[sqi], 8, Tw)
        if pre is not None:
            flush()
            ps, pb = pre
        else:
            P.op("act", lambda e: e.activation(out=G[sqi][:, 0:8 * Tw], in_=xT[:, 0:8 * Tw], func=AF.Square),
                 reads=xTb, writes=gbuf[sqi])
            ps, pb = psum()

            def mm(e):
                ins = None
                for c in range(8):
                    ins = e.matmul(ps[:, 0:Tw], onesb, sqv[:, c, :], start=(c == 0), stop=(c == 7))
                return ins
            P.op("pe", mm, reads=gbuf[sqi] + [b_const], writes=[pb])
        rstd_from_psum(ps, pb, Tw, 1.0 / D, rstd[:, 0:Tw], b_rstd)
        for c in range(8):
            if out_f32_inplace:
                o, ob = xv[:, c, :], xTb[c]
            else:
                o, ob = fmview(G[gi], 8, Tw)[:, c, :], gbuf[gi][c]
            P.op("dve", lambda e, o=o, c=c: e.scalar_tensor_tensor(
                out=o, in0=xv[:, c, :], scalar=pvec(gidx + c), in1=rstd[:, 0:Tw], op0=ALU.mult, op1=ALU.mult),
                 reads=[xTb[c], b_rstd, b_const], writes=[ob])

    cur = {"ti": 0, "sample": False}

    def veng():
        return "pool" if (1 <= cur["ti"] <= 3) else "dve"

    pend = []

    def flush():
        while pend:
            pend.pop(0)()

    def linear_fm(W, c0, ncols, in_t, in_bufs, nk, Tw, consumer, slab=None):
        inv = fmview(in_t, nk, Tw)
        step = 512 if nk == 8 else 128
        for s0 in range(c0, c0 + ncols, step):
            n = min(step, c0 + ncols - s0)
            sl, slb = slab if slab is not None else take_slab(W, s0)
            for j in range(n // 128):
                ps, pb = psum()

                def mm(e, sl=sl, j=j, ps=ps):
                    ins = None
                    for k in range(nk):
                        ins = e.matmul(ps[:, 0:Tw], sl[:, k, j * 128:(j + 1) * 128], inv[:, k, :],
                                       start=(k == 0), stop=(k == nk - 1))
                    return ins
                P.op("pe", mm, reads=[slb] + list(in_bufs), writes=[pb])
                flush()
                consumer((s0 - c0) // 128 + j, ps, pb)

    def linear_tm(W, c0, in_t, in_bufs, Tw, consumer, slab=None):
        inv = fmview(in_t, 8, Tw)
        sl, slb = slab if slab is not None else take_slab(W, c0)
        nb = (Tw + 127) // 128
        for b in range(nb):
            n = min(128, Tw - b * 128)
            ps, pb = psum()

            def mm(e, b=b, n=n, ps=ps):
                ins = None
                for k in range(8):
                    ins = e.matmul(ps[0:n, :], inv[:, k, b * 128:b * 128 + n], sl[:, k, :],
                                   start=(k == 0), stop=(k == 7))
                return ins
            P.op("pe", mm, reads=[slb] + list(in_bufs), writes=[pb])
            consumer(b, n, ps, pb)

    def resid_add(Tw, enabled=True, sq=None):
        xv = fmview(xT, 8, Tw)
        if sq is not None:
            sqi, nps, npb = sq
            sqv = fmview(G[sqi], 8, Tw)

        def cons(c, ps, pb):
            if enabled:
                P.op("dve", lambda e: e.tensor_tensor(out=xv[:, c, :], in0=xv[:, c, :], in1=ps[:, 0:Tw], op=ALU.add),
                     reads=[pb, xTb[c]], writes=[xTb[c]])
            if sq is not None:
                P.op("act", lambda e: e.activation(out=sqv[:, c, :], in_=xv[:, c, :], func=AF.Square),
                     reads=[xTb[c]], writes=[gbuf[sqi][c]])

                def post():
                    P.op("pe", lambda e: e.matmul(nps[:, 0:Tw], onesb, sqv[:, c, :], start=(c == 0), stop=(c == 7)),
                         reads=[gbuf[sqi][c], b_const], writes=[npb])
                pend.append(post)
        return cons

    epsb = sb("epsb", 1)
    P.op("dve", lambda e: e.memset(epsb[:], EPS), writes=[b_ab])

    b_qT = [Buf(f"qT{h}") for h in range(4)]; b_kT = [Buf(f"kT{h}") for h in range(4)]
    b_gr = [Buf(f"gr{h}") for h in range(4)]; b_gh = [Buf(f"gh{h}") for h in range(4)]
    b_qb = [Buf(f"qb{h}") for h in range(4)]; b_kb = [Buf(f"kb{h}") for h in range(4)]
    b_v = [Buf(f"v{b}") for b in range(4)]; b_hi = [Buf(f"hi{b}") for b in range(4)]
    b_kd = [Buf(f"kd{h}") for h in range(4)]; b_kbT = [Buf(f"kbT{h}") for h in range(4)]
    b_tA, b_tB, b_tC, b_tA2 = Buf("tA"), Buf("tB"), Buf("tC"), Buf("tA2")
    b_tD, b_tE = b_tB, b_tC
    b_aR, b_aH = Buf("aR"), Buf("aH")
    b_attT = Buf("attT"); b_attH = [Buf("attH0"), Buf("attH1")]
    b_attT2 = Buf("attT2")
    b_bbs = [Buf(f"bbs{h}") for h in range(4)]
    b_attHs = [Buf(f"attHs{c}") for c in range(16)]
    b_Sps = [Buf(f"Sps{c}") for c in range(16)]
    b_Sr = [Buf(f"Sr{h}") for h in range(4)]; b_Srb = [Buf(f"Srb{h}") for h in range(4)]
    b_Sh = [Buf(f"Sh{h}") for h in range(4)]; b_Sp = [Buf(f"Sp{h}") for h in range(4)]
    b_ev = [Buf(f"ev{h}") for h in range(4)]
    b_osbs, b_sqos, b_rrs = [Buf("osb0"), Buf("osb1")], [Buf("sqo0"), Buf("sqo1")], [Buf("rr0"), Buf("rr1")]
    b_qd = Buf("qd")
    b_PTs, b_rdens = [Buf("PT0"), Buf("PT1")], [Buf("rden0"), Buf("rden1")]
    b_rden = b_rdens[0]
    hn_n = [0]
    b_KT, b_Vb = Buf("KT"), Buf("Vb")
    b_cs = [Buf("cs0")]
    b_hid = [Buf(f"hid{j}") for j in range(22)]

    P.op("dve", lambda e: e.memset(S_ret[:], 0.0), writes=b_Sr)
    P.op("dve", lambda e: e.memset(S_hg[:], 0.0), writes=b_Sh)
    P.op("dve", lambda e: e.memset(S_retb[:], 0.0), writes=b_Srb)

    load_tile(memp, 0, NMEM)
    rmsnorm(0, G_MEM, 1, NMEM)
    KTv = KT[:, :].rearrange("p (c m) -> p c m", m=NMEM)
    Vbv = Vb[:, :].rearrange("p (b n) -> p b n", n=D)
    for (W, dst, isk) in ((w_xk, mkp, True), (w_xv, mvp, False)):
        for s0 in (0, 512):
            slab = take_slab(W, s0)
            if isk:
                def consK(c, ps, pb, s0=s0):
                    cc = s0 // 128 + c
                    P.op("act", lambda e: e.activation(out=KTv[:, cc, :], in_=ps[:, 0:NMEM], func=AF.Copy),
                         reads=[pb], writes=[b_KT])
                linear_fm(W, s0, 512, G[0], gbuf[0], 8, NMEM, consK, slab=slab)

            def consTM(b, n, ps, pb, s0=s0, dst=dst, isk=isk):
                k = xio_n[0] % 2
                xio_n[0] += 1
                P.op("dve", lambda e: e.tensor_copy(xio[k][:, 0:512], ps[:, :]), reads=[pb], writes=[xiob[k]])
                if not isk:
                    P.op("act", lambda e: e.activation(out=Vbv[:, b, s0:s0 + 512], in_=xio[k][:, 0:512], func=AF.Copy),
                         reads=[xiob[k]], writes=[b_Vb])
                P.dma_store("sp", xiob[k], dst[b * 128:(b + 1) * 128, s0:s0 + 512], xio[k][:, 0:512])
            linear_tm(W, s0, G[0], gbuf[0], NMEM, consTM, slab=slab)

    def head_norm_gate(o_ps, pb, Tw, gain_col, gate_ap, gate_buf, out_ap, out_buf):
        i = hn_n[0] % 2
        hn_n[0] += 1
        o_sb, sqo, rr = o_sbs[i], sqos[i], rrs[i]
        b_osb, b_sqo, b_rr = b_osbs[i], b_sqos[i], b_rrs[i]
        P.op("act", lambda e: e.activation(out=o_sb[:, 0:Tw], in_=o_ps, func=AF.Copy), reads=[pb], writes=[b_osb])
        P.op("act", lambda e: e.activation(out=sqo[:, 0:Tw], in_=o_ps, func=AF.Square), reads=[pb], writes=[b_sqo])
        ps, pb2 = psum()
        P.op("pe", lambda e: e.matmul(ps[:, 0:Tw], onesb, sqo[:, 0:Tw], start=True, stop=True),
             reads=[b_sqo, b_const], writes=[pb2])
        rstd_from_psum(ps, pb2, Tw, 1.0 / 128, rr[:, 0:Tw], b_rr)
        P.op("dve", lambda e: e.scalar_tensor_tensor(out=o_sb[:, 0:Tw], in0=o_sb[:, 0:Tw], scalar=pvec(gain_col),
                                                      in1=rr[:, 0:Tw], op0=ALU.mult, op1=ALU.mult),
             reads=[b_osb, b_rr, b_const], writes=[b_osb])
        P.op(veng(), lambda e: e.tensor_tensor(out=out_ap, in0=o_sb[:, 0:Tw], in1=gate_ap, op=ALU.mult),
             reads=[b_osb, gate_buf], writes=[out_buf])

    def run_tile(ti, Tw, src_rows, r0, dst_rows, sample):
        cur["ti"], cur["sample"] = ti, sample
        nb = (Tw + 127) // 128
        ck = 0
        csv = cs[ck][:, 0:2 * Tw].rearrange("p (a t) -> p a t", t=Tw)
        P.dma_load("sp", b_cs[ck], csv, cs_d[:, :, (SEQ if sample else r0):(SEQ if sample else r0) + Tw])
        P.mark(f"t{ti}:load")
        load_tile(src_rows, r0, Tw)
        rmsnorm(0, G_MIX, 1, Tw)
        P.mark(f"t{ti}:w_in")
        h1, h1b = G[0], gbuf[0]
        qTv, kTv = fmview(qT, 4, Tw), fmview(kT, 4, Tw)
        grv, ghv = fmview(gate_r, 4, Tw), fmview(gate_h, 4, Tw)
        qbv, kbv = fmview(qb, 4, Tw), fmview(kb, 4, Tw)
        v_v = v_tm[:, :].rearrange("p (b n) -> p b n", n=512)
        hi_v = hi_tm[:, :].rearrange("p (b n) -> p b n", n=512)
        kd_v = kd_tm[:, :].rearrange("p (h b d) -> p h b d", h=4, b=4)
        kbT_v = kbT_tm[:, :].rearrange("p (h b d) -> p h b d", h=4, b=4)
        ofT, ofb = G[1], gbuf[1]
        ofv = fmview(ofT, 8, Tw)

        def cons_tm(dstv, bufs):
            def c(b, n, ps, pb):
                P.op("act", lambda e: e.activation(out=dstv[0:n, b, :], in_=ps[0:n, :], func=AF.Copy),
                     reads=[pb], writes=[bufs[b]])
            return c

        def cons_rot(dstv, bufs, scale):
            def c(h, ps, pb):
                i = rot_n[0] % 2
                rot_n[0] += 1
                tA, bA = (tmpA, b_tA) if i == 0 else (tmpA2, b_tA2)
                P.op("act", lambda e: e.activation(out=tA[:, 0:Tw], in_=ps[:, 0:Tw], func=AF.Copy, scale=scale),
                     reads=[pb], writes=[bA])

                def post():
                    ps2, pb2 = psum()
                    P.op("pe", lambda e: e.matmul(ps2[:, 0:Tw], perm, tA[:, 0:Tw], start=True, stop=True),
                         reads=[bA, b_const], writes=[pb2])
                    P.op("dve", lambda e: e.tensor_tensor(out=tmpB[:, 0:Tw], in0=ps2[:, 0:Tw], in1=csv[:, 1, :], op=ALU.mult),
                         reads=[pb2, b_cs[ck]], writes=[b_tB])
                    P.op(veng(), lambda e: e.tensor_tensor(out=tmpC[:, 0:Tw], in0=tA[:, 0:Tw], in1=csv[:, 0, :], op=ALU.mult),
                         reads=[bA, b_cs[ck]], writes=[b_tC])
                    P.op(veng(), lambda e: e.tensor_tensor(out=dstv[:, h, :], in0=tmpB[:, 0:Tw], in1=tmpC[:, 0:Tw], op=ALU.add),
                         reads=[b_tB, b_tC], writes=[bufs[h]])
                pend.append(post)
            return c

        def cons_silu(dstv, bufs):
            def c(h, ps, pb):
                P.op("act", lambda e: e.activation(out=dstv[:, h, :], in_=ps[:, 0:Tw], func=AF.Silu),
                     reads=[pb], writes=[bufs[h]])
            return c

        def emit_kd():
            for h in range(4):
                ps, pb = psum()
                psb = ps[:, :].bitcast(BF16)

                def tr(e, h=h, psb=psb):
                    ins = None
                    for b in range(nb):
                        n = min(128, Tw - b * 128)
                        ins = e.transpose(psb[0:n, b * 128:(b + 1) * 128], kTv[:, h, b * 128:b * 128 + n], identb)
                    return ins
                P.op("pe", tr, reads=[b_kT[h], b_const], writes=[pb])
                for b in range(nb):
                    n = min(128, Tw - b * 128)
                    if sample:
                        sc = cf[0:n, CF.off["kdecS"][0] + h:CF.off["kdecS"][0] + h + 1]
                    else:
                        sc = cf[0:n, CF.off["kdec"][0] + h * 4 + b:CF.off["kdec"][0] + h * 4 + b + 1]
                    P.op("dve", lambda e, b=b, n=n, sc=sc, psb=psb, h=h: e.tensor_scalar(
                        out=kd_v[0:n, h, b, :], in0=psb[0:n, b * 128:(b + 1) * 128], scalar1=sc, scalar2=None, op0=ALU.mult),
                        reads=[pb, b_const], writes=[b_kd[h]])

        def emit_kbT(h):
            ps2, pb2 = psum()
            psb = ps2[:, :].bitcast(BF16)

            def tr(e):
                ins = None
                for b in range(nb):
                    n = min(128, Tw - b * 128)
                    ins = e.transpose(psb[0:n, b * 128:(b + 1) * 128], kbv[:, h, b * 128:b * 128 + n], identb)
                return ins
            P.op("pe", tr, reads=[b_kb[h], b_const], writes=[pb2])
            n0 = min(128, Tw)
            P.op("act", lambda e: e.activation(out=kbT_v[0:n0, h, 0:nb, :],
                                               in_=psb[0:n0, 0:nb * 128].rearrange("p (b d) -> p b d", d=128), func=AF.Copy),
                 reads=[pb2], writes=[b_kbT[h]])


        sbase = 10624 if sample else 0
        bbs = hid[:, sbase:sbase + 8 * Tw].bitcast(F32).rearrange("p (h t) -> p h t", h=4)
        attHs = hid[:, 4096:6144].rearrange("p (c t) -> p c t", t=128)
        Sps = hid[:, 6144:8192].rearrange("p (c t) -> p c t", t=128)
        attT2 = hid[:, 8192:10240]

        def evv(h):
            return evec[:, h * 20:(h + 1) * 20]

        def cons_hf(h, ps, pb):
            P.op("act", lambda e: e.activation(out=hlf[:, h, 0:Tw], in_=ps[:, 0:Tw], func=AF.Tanh, scale=0.5),
                 reads=[pb], writes=[b_hsc[h]])

        def cons_hq(h, ps, pb):
            P.op("act", lambda e: e.activation(out=qbv[:, h, :], in_=ps[:, 0:Tw], func=AF.Silu), reads=[pb], writes=[b_qb[h]])

        def stB1():
            for h in range(4):
                P.op("dve", lambda e, h=h: e.tensor_scalar(out=hkk[:, h, 0:Tw], in0=hlf[:, h, 0:Tw], scalar1=ab[:, 8 + h:9 + h],
                                                           scalar2=ab[:, 4 + h:5 + h], op0=ALU.mult, op1=ALU.add),
                     reads=[b_hsc[h], b_ab], writes=[b_hsk[h]])
            for h in range(4):
                P.op("act", lambda e, h=h: e.activation(out=hlf[:, h, 0:Tw], in_=hlf[:, h, 0:Tw], func=AF.Ln,
                                                        scale=ab[:, 4 + h:5 + h], bias=ab[:, 0 + h:1 + h]),
                     reads=[b_hsc[h], b_ab], writes=[b_hsc[h]])

        def stB3():
            for h in range(4):
                if not sample:
                    for j in range(4):
                        P.op("dve", lambda e, h=h, j=j: e.tensor_tensor_scan(
                            out=bbs[:, h, j * 128:(j + 1) * 128], data0=onesf, data1=hlf[:, h, j * 128:(j + 1) * 128],
                            initial=0.0, op0=ALU.mult, op1=ALU.add), reads=[b_hsc[h], b_const], writes=[b_bbs[h]])
                else:
                    l3 = hlf[:, h, 0:Tw].rearrange("p (r l) -> p r l", l=4)
                    b3 = bbs[:, h, :].rearrange("p (r l) -> p r l", l=4)
                    P.op("dve", lambda e, l3=l3, b3=b3: e.tensor_copy(b3[:, :, 0], l3[:, :, 0]), reads=[b_hsc[h]], writes=[b_bbs[h]])
                    for l in range(1, 4):
                        P.op("dve", lambda e, l=l, l3=l3, b3=b3: e.tensor_tensor(out=b3[:, :, l], in0=b3[:, :, l - 1], in1=l3[:, :, l], op=ALU.add),
                             reads=[b_hsc[h], b_bbs[h]], writes=[b_bbs[h]])

        def stB4():
            if sample:
                return
            for h in range(4):
                ev = evv(h)
                b4 = bbs[:, h, :].rearrange("p (j t) -> p j t", t=128)
                P.op("dve", lambda e, ev=ev, b4=b4: e.tensor_copy(ev[:, 16:20], b4[:, :, 63]), reads=[b_bbs[h]], writes=[b_ev[h]])
                P.op("act", lambda e, ev=ev, b4=b4: e.activation(out=ev[:, 0:4], in_=b4[:, :, 127], func=AF.Exp), reads=[b_bbs[h]], writes=[b_ev[h]])
                P.op("act", lambda e, ev=ev, b4=b4: e.activation(out=ev[:, 8:12], in_=b4[:, :, 63], func=AF.Exp), reads=[b_bbs[h]], writes=[b_ev[h]])
                P.op("dve", lambda e, ev=ev, b4=b4: e.tensor_tensor(out=b4, in0=b4, in1=ev[:, 16:20].unsqueeze(2).to_broadcast([128, 4, 128]),
                                                                   op=ALU.subtract), reads=[b_bbs[h], b_ev[h]], writes=[b_bbs[h]])

        def stB5():
            for h in range(4):
                P.op("act", lambda e, h=h: e.activation(out=hlf[:, h, 0:Tw], in_=bbs[:, h, :], func=AF.Exp), reads=[b_bbs[h]], writes=[b_hsc[h]])
            for h in range(4):
                P.op("act", lambda e, h=h: e.activation(out=bbs[:, h, :], in_=bbs[:, h, :], func=AF.Exp, scale=-1.0), reads=[b_bbs[h]], writes=[b_bbs[h]])

        def stB6():
            for h in range(4):
                if not sample:
                    P.op("dve", lambda e, h=h: e.tensor_copy(evv(h)[:, 4:8], hlf[:, h, 0:Tw].rearrange("p (j t) -> p j t", t=128)[:, :, 127]),
                         reads=[b_hsc[h]], writes=[b_ev[h]])
                else:
                    P.op("dve", lambda e, h=h: e.tensor_copy(e1s[:, h, :], hlf[:, h, 0:Tw].rearrange("p (r l) -> p r l", l=4)[:, :, 3]),
                         reads=[b_hsc[h]], writes=[b_e1s])
                P.op(veng(), lambda e, h=h: e.tensor_tensor(out=qbv[:, h, :], in0=qbv[:, h, :], in1=hlf[:, h, 0:Tw], op=ALU.mult),
                     reads=[b_hsc[h]], writes=[b_qb[h]])
                P.op("dve", lambda e, h=h: e.tensor_tensor(out=kbv[:, h, :], in0=hkk[:, h, 0:Tw], in1=bbs[:, h, :], op=ALU.mult),
                     reads=[b_hsk[h], b_bbs[h]], writes=[b_kb[h]])

        linear_fm(w_in, 2560, 512, h1, h1b, 8, Tw, cons_hf)
        linear_fm(w_in, 2048, 512, h1, h1b, 8, Tw, cons_hq)
        stB1()
        linear_tm(w_in, 1024, h1, h1b, Tw, cons_tm(v_v, b_v))
        stB3()
        linear_fm(w_in, 512, 512, h1, h1b, 8, Tw, cons_rot(kTv, b_kT, 128.0 ** -0.5))
        stB4()
        linear_fm(w_in, 0, 512, h1, h1b, 8, Tw, cons_rot(qTv, b_qT, 1.0))
        flush()
        stB5()
        emit_kd()
        linear_fm(w_in, 1536, 512, h1, h1b, 8, Tw, cons_silu(grv, b_gr))
        stB6()
        linear_tm(w_in, 3072, h1, h1b, Tw, cons_tm(hi_v, b_hi))
        for h in range(4):
            emit_kbT(h)
        linear_fm(w_in, 3584, 512, h1, h1b, 8, Tw, cons_silu(ghv, b_gh))

        P.mark(f"t{ti}:mixers")
        if not sample:
            attvs = [attT[:, :].rearrange("p (j t) -> p j t", t=512), attT2.rearrange("p (j t) -> p j t", t=512)]
            b_atts = [b_attT, b_attT2]
            maskr = cb[:, CB.sl("maskr")].rearrange("p (h x) -> p h x", x=512)
            qdecv = cb[:, CB.sl("qdec")].rearrange("p (h x) -> p h x", x=512)

            def r_att(h):
                attv, b_att = attvs[h % 2], b_atts[h % 2]
                for j in range(4):
                    ps, pb = psum()
                    P.op("pe", lambda e, j=j, ps=ps: e.matmul(ps[:, j * 128:512], kTv[:, h, j * 128:(j + 1) * 128],
                                                              qTv[:, h, j * 128:512], start=True, stop=True),
                         reads=[b_kT[h], b_qT[h]], writes=[pb])
                    P.op("dve", lambda e, j=j, ps=ps: e.tensor_tensor(out=attv[:, j, j * 128:512], in0=ps[:, j * 128:512],
                                                                      in1=maskr[:, h, 0:512 - j * 128], op=ALU.mult),
                         reads=[pb, b_const], writes=[b_att])

            def r_rest(h):
                attv, b_att = attvs[h % 2], b_atts[h % 2]
                if ti > 0:
                    P.op(veng(), lambda e: e.tensor_tensor(out=qd[:, :], in0=qTv[:, h, :], in1=qdecv[:, h, :], op=ALU.mult),
                         reads=[b_qT[h], b_const], writes=[b_qd])
                ops, opb = psum("L")

                def pv(e):
                    ins = None
                    for j in range(4):
                        ins = e.matmul(ops[:, j * 128:512], v_v[:, j, h * 128:(h + 1) * 128], attv[:, j, j * 128:512],
                                       start=(j == 0), stop=(j == 3 and ti == 0))
                    if ti > 0:
                        ins = e.matmul(ops[:, :], S_retb[:, h * 128:(h + 1) * 128], qd[:, :], start=False, stop=True)
                    return ins
                P.op("pe", pv, reads=[b_att, b_qd, b_Srb[h]] + b_v, writes=[opb])
                ups, upb = psum()

                def su(e):
                    ins = None
                    for j in range(4):
                        ins = e.matmul(ups[:, 0:128], kd_v[:, h, j, :], v_v[:, j, h * 128:(h + 1) * 128],
                                       start=(j == 0), stop=(j == 3))
                    return ins
                P.op("pe", su, reads=[b_kd[h]] + b_v, writes=[upb])
                head_norm_gate(ops[:, :], opb, Tw, RGAIN + h, grv[:, h, :], b_gr[h], ofv[:, h, :], ofb[h])
                P.op("dve", lambda e: e.scalar_tensor_tensor(
                    out=S_ret[:, h * 128:(h + 1) * 128], in0=S_ret[:, h * 128:(h + 1) * 128], scalar=GAM[h] ** 512,
                    in1=ups[:, 0:128], op0=ALU.mult, op1=ALU.add), reads=[upb, b_Sr[h]], writes=[b_Sr[h]])
                P.op("act", lambda e: e.activation(out=S_retb[:, h * 128:(h + 1) * 128], in_=S_ret[:, h * 128:(h + 1) * 128],
                                                   func=AF.Copy), reads=[b_Sr[h]], writes=[b_Srb[h]])
            def ret_core():
                r_att(0)
                yield
                for h in range(4):
                    if h < 3:
                        r_att(h + 1)
                        yield
                    r_rest(h)
                    yield
            P.mark(f"t{ti}:hgrn")
            def hg_core():
                cmask = cb[:, CB.sl("cmask")]
                for h in range(4):
                    ev = evv(h)
                    aps, apb = psum()

                    def attm(e, h=h, aps=aps):
                        ins = None
                        for j in range(4):
                            ins = e.matmul(aps[:, j * 128:(j + 1) * 128], kbv[:, h, j * 128:(j + 1) * 128],
                                           qbv[:, h, j * 128:(j + 1) * 128], start=True, stop=True)
                        return ins
                    P.op("pe", attm, reads=[b_kb[h], b_qb[h]], writes=[apb])
                    P.op("dve", lambda e, aps=aps, h=h: e.tensor_tensor(
                        out=attHs[:, 4 * h:4 * h + 4, :], in0=aps[:, :].rearrange("p (j t) -> p j t", t=128),
                        in1=cmask.unsqueeze(1).to_broadcast([128, 4, 128]), op=ALU.mult),
                        reads=[apb, b_const], writes=b_attHs[4 * h:4 * h + 4])
                    ups, upb = psum()

                    def um(e, h=h, ups=ups):
                        ins = None
                        for j in range(4):
                            ins = e.matmul(ups[:, j * 128:(j + 1) * 128], kbT_v[:, h, j, :], hi_v[:, j, h * 128:(h + 1) * 128],
                                           start=True, stop=True)
                        return ins
                    P.op("pe", um, reads=[b_kbT[h]] + b_hi, writes=[upb])
                    for j in range(4):
                        c = h * 4 + j
                        first = (ti == 0 and j == 0)
                        if not first:
                            P.op("act", lambda e, h=h, j=j, ev=ev, c=c: e.activation(
                                out=Sps[:, c, :], in_=S_hg[:, h * 128:(h + 1) * 128], func=AF.Copy, scale=ev[:, 8 + j:9 + j]),
                                reads=[b_Sh[h], b_ev[h]], writes=[b_Sps[c]])
                        P.op("dve", lambda e, h=h, j=j, ev=ev: e.tensor_scalar(
                            out=S_hg[:, h * 128:(h + 1) * 128], in0=S_hg[:, h * 128:(h + 1) * 128], scalar1=ev[:, j:j + 1],
                            scalar2=None, op0=ALU.mult), reads=[b_Sh[h], b_ev[h]], writes=[b_Sh[h]])
                        P.op("dve", lambda e, h=h, j=j, ev=ev, ups=ups: e.scalar_tensor_tensor(
                            out=S_hg[:, h * 128:(h + 1) * 128], in0=ups[:, j * 128:(j + 1) * 128], scalar=ev[:, 4 + j:5 + j],
                            in1=S_hg[:, h * 128:(h + 1) * 128], op0=ALU.mult, op1=ALU.add),
                            reads=[upb, b_Sh[h], b_ev[h]], writes=[b_Sh[h]])
                        if j % 2 == 1:
                            yield
                for hp in range(2):
                    heads = (2 * hp, 2 * hp + 1)
                    for h in heads:
                        ops, opb = psum("L")
                        for j in range(4):
                            c = h * 4 + j
                            first = (ti == 0 and j == 0)

                            def pvh(e, h=h, j=j, ops=ops, c=c, first=first):
                                ins = e.matmul(ops[:, j * 128:(j + 1) * 128], hi_v[:, j, h * 128:(h + 1) * 128], attHs[:, c, :],
                                               start=True, stop=first)
                                if not first:
                                    ins = e.matmul(ops[:, j * 128:(j + 1) * 128], Sps[:, c, :],
                                                   qbv[:, h, j * 128:(j + 1) * 128], start=False, stop=True)
                                return ins
                            P.op("pe", pvh, reads=[b_attHs[c], b_hi[j], b_Sps[c], b_qb[h]], writes=[opb])
                        head_norm_gate(ops[:, :], opb, Tw, HGAIN + h, ghv[:, h, :], b_gh[h], ofv[:, 4 + h, :], ofb[4 + h])
                        yield

            gens = [ret_core(), hg_core()]
            while gens:
                for g_ in list(gens):
                    try:
                        next(g_)
                    except StopIteration:
                        gens.remove(g_)
        else:
            sample_mixers(Tw, qTv, kTv, qbv, kbv, v_v, hi_v, kd_v, kbT_v, grv, ghv, ofv, ofb)

        P.mark(f"t{ti}:w_out")
        nrm = psum("L")
        linear_fm(w_out, 0, D, ofT, ofb, 8, Tw, resid_add(Tw, DEBUG["mixer"], sq=(2,) + nrm))
        P.mark(f"t{ti}:xattn")
        rmsnorm(0, G_XA, 2, Tw, pre=nrm)
        xq, xqb = G[2], gbuf[2]
        xqv = fmview(xq, 8, Tw)

        def cons_xq(c, ps, pb):
            P.op("act", lambda e: e.activation(out=xqv[:, c, :], in_=ps[:, 0:Tw], func=AF.Copy), reads=[pb], writes=[xqb[c]])
        linear_fm(w_xq, 0, D, G[0], gbuf[0], 8, Tw, cons_xq)
        oa, oab = G[1], gbuf[1]
        oav = fmview(oa, 8, Tw)
        if not sample:
            def x_scores(hh):
                pk = hh % 2
                PTv = PTs[pk][:, :].rearrange("p (m t) -> p m t", t=512)
                for mb in range(2):
                    ps, pb = psum()

                    def sc(e, mb=mb, ps=ps):
                        e.matmul(ps[:, :], KTv[:, 2 * hh, mb * 128:(mb + 1) * 128], xqv[:, 2 * hh, :], start=True, stop=False)
                        return e.matmul(ps[:, :], KTv[:, 2 * hh + 1, mb * 128:(mb + 1) * 128], xqv[:, 2 * hh + 1, :], start=False, stop=True)
                    P.op("pe", sc, reads=[b_KT, xqb[2 * hh], xqb[2 * hh + 1]], writes=[pb])
                    P.op("act", lambda e, mb=mb, ps=ps: e.activation(out=PTv[:, mb, :], in_=ps[:, :], func=AF.Exp, scale=1.0 / 16.0),
                         reads=[pb], writes=[b_PTs[pk]])

            def x_pv(hh):
                pk = hh % 2
                PTv = PTs[pk][:, :].rearrange("p (m t) -> p m t", t=512)
                rd, b_rd = rdens[pk], b_rdens[pk]
                dps, dpb = psum()

                def den(e):
                    e.matmul(dps[:, :], onesb, PTv[:, 0, :], start=True, stop=False)
                    return e.matmul(dps[:, :], onesb, PTv[:, 1, :], start=False, stop=True)
                P.op("pe", den, reads=[b_PTs[pk], b_const], writes=[dpb])
                P.op("act", lambda e: e.activation(out=rd[:, :], in_=dps[:, :], func=AF.Ln), reads=[dpb], writes=[b_rd])
                P.op("act", lambda e: e.activation(out=rd[:, :], in_=rd[:, :], func=AF.Exp, scale=-1.0), reads=[b_rd], writes=[b_rd])
                for i in range(2):
                    ps, pb = psum()

                    def pvx(e, i=i, ps=ps):
                        c0 = hh * 256 + i * 128
                        e.matmul(ps[:, :], Vbv[:, 0, c0:c0 + 128], PTv[:, 0, :], start=True, stop=False)
                        return e.matmul(ps[:, :], Vbv[:, 1, c0:c0 + 128], PTv[:, 1, :], start=False, stop=True)
                    P.op("pe", pvx, reads=[b_PTs[pk], b_Vb], writes=[pb])
                    P.op("dve", lambda e, i=i, ps=ps: e.tensor_tensor(out=oav[:, 2 * hh + i, :], in0=ps[:, :], in1=rd[:, :], op=ALU.mult),
                         reads=[pb, b_rd], writes=[oab[2 * hh + i]])
            x_scores(0)
            for hh in range(4):
                if hh < 3:
                    x_scores(hh + 1)
                x_pv(hh)
        else:
            sample_xattn(xqv, xqb, oav, oab)
        nrm = psum("L")
        linear_fm(w_xo, 0, D, oa, oab, 8, Tw, resid_add(Tw, DEBUG["xattn"], sq=(2,) + nrm))
        P.mark(f"t{ti}:ffn")
        rmsnorm(0, G_FFN, 2, Tw, pre=nrm)
        hidv = fmview(hid, 22, Tw)
        sg = tmpA
        for g0 in range(0, DFF, 256):
            (gv, uv), slb = take_slab(w_gate, g0, multi=True)
            for j in range(2):
                hc = g0 // 128 + j
                gps, gpb = psum()
                ups, upb = psum()

                def mmg(e, sl=gv, j=j, ps=gps):
                    ins = None
                    inv = fmview(G[0], 8, Tw)
                    for k in range(8):
                        ins = e.matmul(ps[:, 0:Tw], sl[:, k, j * 128:(j + 1) * 128], inv[:, k, :], start=(k == 0), stop=(k == 7))
                    return ins
                P.op("pe", mmg, reads=[slb] + gbuf[0], writes=[gpb])
                P.op("pe", lambda e, sl=uv, j=j, ps=ups, mmg=mmg: mmg(e, sl, j, ps), reads=[slb] + gbuf[0], writes=[upb])
                P.op("act", lambda e, gps=gps: e.activation(out=sg[:, 0:Tw], in_=gps[:, 0:Tw], func=AF.Silu), reads=[gpb], writes=[b_tA])
                P.op("dve", lambda e, ups=ups, hc=hc: e.tensor_tensor(out=hidv[:, hc, :], in0=sg[:, 0:Tw], in1=ups[:, 0:Tw], op=ALU.mult),
                     reads=[upb, b_tA], writes=[b_hid[hc]])
        nrm = psum("L")
        linear_fm(w_down, 0, D, hid, b_hid, 22, Tw, resid_add(Tw, DEBUG["ffn"], sq=(2,) + nrm))
        P.mark(f"t{ti}:final")
        rmsnorm(0, G_FIN, 2, Tw, out_f32_inplace=True, pre=nrm)
        store_tile(dst_rows, r0, Tw)

    hlf_t = sb("hlf", 4 * 512)
    hk_t = attT
    hlf = hlf_t[:, :].rearrange("p (h t) -> p h t", h=4)
    hkk = hk_t[:, :].rearrange("p (h t) -> p h t", h=4)
    b_hsc = [Buf(f"hsc{h}") for h in range(4)]
    b_hsk = [Buf(f"hsk{h}") for h in range(4)]
    e1s_t = sb("e1s", 64)
    e1s = e1s_t[:, :].rearrange("p (h r) -> p h r", h=4)
    b_e1s = Buf("e1s")

    def tail(t, start, n, dt_bytes_ratio=1):
        return t[:, start:start + n]

    NRING = 3
    S0r = [xT[:, 512 + i * 512:512 + (i + 1) * 512] for i in range(NRING)]
    S0h = [xT[:, 2048 + i * 512:2048 + (i + 1) * 512] for i in range(NRING)]
    S0rb = [hid[:, 1408 + i * 512:1408 + (i + 1) * 512] for i in range(NRING)]
    S0hb = [hid[:, 2944 + i * 512:2944 + (i + 1) * 512] for i in range(NRING)]
    Kbf = [hid[:, 4480 + i * 2048:4480 + (i + 1) * 2048] for i in range(2)]
    KTr = hid[:, 8576:8576 + 2048]
    Vring = [G[1][:, 512:512 + 2048], G[2][:, 512:512 + 2048]]
    PS_s = G[0][:, 512:512 + 512]
    KDr = G[0][:, 1024:1024 + 2048]

    NIN = 5
    S0r = S0r + [qT[:, 256:1280].bitcast(F32), gate_r[:, 256:1280].bitcast(F32)]
    S0h = S0h + [kT[:, 256:1280].bitcast(F32), gate_h[:, 256:1280].bitcast(F32)]
    b_S0 = [Buf(f"S0_{i}") for i in range(NIN)]
    b_S0b = [Buf(f"S0b_{i}") for i in range(NRING)]
    b_Kbf = [Buf("Kbf0"), Buf("Kbf1")]
    b_Vr = [Buf("Vr0"), Buf("Vr1")]
    b_KTr = Buf("KTr"); b_PSs = Buf("PSs"); b_KD = Buf("KD")

    def sample_mixers(Tw, qTv, kTv, qbv, kbv, v_v, hi_v, kd_v, kbT_v, grv, ghv, ofv, ofb):
        msr = cb[0:64, CB.sl("maskSr")].rearrange("p (h x) -> p h x", x=64)
        msh = cb[0:64, CB.sl("maskSh")]
        qds = cb[:, CB.sl("qdecS")].rearrange("p (h x) -> p h x", x=64)
        memb = cb[0:64, CB.sl("memb")]
        decS = cf[:, CF.sl("decS")]
        qdv = attT[:, 0:256].rearrange("p (h t) -> p h t", t=64)
        P.op("dve", lambda e: e.tensor_tensor(out=qdv, in0=qTv, in1=qds, op=ALU.mult), reads=b_qT + [b_const], writes=[b_attT])
        aR = attT[0:64, 256:512].rearrange("p (h t) -> p h t", t=64)
        aH = attT[0:64, 512:768].rearrange("p (h t) -> p h t", t=64)
        oR, oRb = psum("L")
        oH, oHb = psum("L")
        for (isr, av, ov, ob_) in ((True, aR, oR, oRb), (False, aH, oH, oHb)):
            for h in range(4):
                ps, pb = psum()
                if isr:
                    P.op("pe", lambda e, h=h, ps=ps: e.matmul(ps[0:64, 0:64], kTv[:, h, :], qTv[:, h, :], start=True, stop=True),
                         reads=[b_kT[h], b_qT[h]], writes=[pb])
                    P.op("dve", lambda e, h=h, ps=ps, av=av: e.tensor_tensor(out=av[:, h, :], in0=ps[0:64, 0:64], in1=msr[:, h, :], op=ALU.mult),
                         reads=[pb, b_const], writes=[b_aR])
                else:
                    P.op("pe", lambda e, h=h, ps=ps: e.matmul(ps[0:64, 0:64], kbv[:, h, :], qbv[:, h, :], start=True, stop=True),
                         reads=[b_kb[h], b_qb[h]], writes=[pb])
                    P.op("dve", lambda e, h=h, ps=ps, av=av: e.tensor_tensor(out=av[:, h, :], in0=ps[0:64, 0:64], in1=msh, op=ALU.mult),
                         reads=[pb, b_const], writes=[b_aH])
            tok = b_aR if isr else b_aH
            vv = v_v if isr else hi_v
            vb = b_v[0] if isr else b_hi[0]

            def pvs(e, av=av, ov=ov, vv=vv):
                ins = None
                for h in range(4):
                    ins = e.matmul(ov[:, h * 64:(h + 1) * 64], vv[0:64, 0, h * 128:(h + 1) * 128], av[:, h, :], start=(h == 0), stop=False)
                return ins
            P.op("pe", pvs, reads=[tok, vb], writes=[ob_])
        stg = [Kbf[0], Kbf[1], Vring[0], Vring[1]]
        b_stg = [b_Kbf[0], b_Kbf[1], b_Vr[0], b_Vr[1]]

        def s_load(r):
            k = r % NIN
            P.dma_load("sp", b_S0[k], S0r[k].rearrange("p (h e) -> p h e", h=4), sret[r].rearrange("h d e -> d h e"))
            P.dma_load("sp", b_S0[k], S0h[k].rearrange("p (h e) -> p h e", h=4), shg[r].rearrange("h d e -> d h e"))

        b_KD2 = [Buf("KD0"), Buf("KD1")]

        def kd_views(bi):
            KDrv_ = KDr[:, bi * 1024:(bi + 1) * 1024].rearrange("p (h r d) -> p h r d", h=4, r=2)[0:64]
            KDhv_ = KTr[0:64, bi * 1024:(bi + 1) * 1024].rearrange("p (h r d) -> p h r d", h=4, r=2)
            return KDrv_, KDhv_

        def kd_expand(g2):
            bi = g2 % 2
            KDrv_, KDhv_ = kd_views(bi)
            for h in range(4):
                mb_ = memb[:, 2 * g2:2 * g2 + 2].unsqueeze(2).to_broadcast([64, 2, 128])
                P.op("pool", lambda e, h=h, mb_=mb_, KDrv_=KDrv_: e.tensor_tensor(
                    out=KDrv_[:, h, :, :], in0=kd_v[0:64, h, 0, :].unsqueeze(1).to_broadcast([64, 2, 128]), in1=mb_, op=ALU.mult),
                    reads=[b_kd[h], b_const], writes=[b_KD2[bi]])
                P.op("pool", lambda e, h=h, mb_=mb_, KDhv_=KDhv_: e.tensor_tensor(
                    out=KDhv_[:, h, :, :], in0=kbT_v[0:64, h, 0, :].unsqueeze(1).to_broadcast([64, 2, 128]), in1=mb_, op=ALU.mult),
                    reads=[b_kbT[h], b_const], writes=[b_KD2[bi]])

        kd_expand(0)
        for g in range(1):
            for r in range(NREQ):
                g2, rl = r // 2, r % 2
                if rl == 0 and g2 + 1 < NREQ // 2:
                    kd_expand(g2 + 1)
                KDrv, KDhv = kd_views(g2 % 2)
                b_KD = b_KD2[g2 % 2]
                k = r % NIN
                kb = r % NRING
                if r == 0:
                    for r_ in range(NIN - 1):
                        s_load(r_)
                if r + NIN - 1 < NREQ:
                    s_load(r + NIN - 1)
                P.op("act", lambda e, k=k, kb=kb: e.activation(out=S0rb[kb], in_=S0r[k], func=AF.Copy), reads=[b_S0[k]], writes=[b_S0b[kb]])
                P.op("act", lambda e, k=k, kb=kb: e.activation(out=S0hb[kb], in_=S0h[k], func=AF.Copy), reads=[b_S0[k]], writes=[b_S0b[kb]])
                last = (r == NREQ - 1)

                def inter(e, k=kb, r=r, last=last):
                    ins = None
                    for h in range(4):
                        ins = e.matmul(oR[:, h * 64 + 4 * r:h * 64 + 4 * r + 4], S0rb[k][:, h * 128:(h + 1) * 128],
                                       qdv[:, h, 4 * r:4 * r + 4], start=False, stop=(last and h == 3))
                    for h in range(4):
                        ins = e.matmul(oH[:, h * 64 + 4 * r:h * 64 + 4 * r + 4], S0hb[k][:, h * 128:(h + 1) * 128],
                                       qbv[:, h, 4 * r:4 * r + 4], start=False, stop=(last and h == 3))
                    return ins
                P.op("pe", inter, reads=[b_S0b[kb], b_attT] + b_qb, writes=[oRb, oHb])
                uR, uRb = psum()
                uH, uHb = psum()

                def ust(e, rl=rl, uR=uR, uH=uH, KDrv=KDrv, KDhv=KDhv):
                    ins = None
                    for h in range(4):
                        ins = e.matmul(uR[:, h * 128:(h + 1) * 128], KDrv[:, h, rl, :], v_v[0:64, 0, h * 128:(h + 1) * 128], start=True, stop=True)
                    for h in range(4):
                        ins = e.matmul(uH[:, h * 128:(h + 1) * 128], KDhv[:, h, rl, :], hi_v[0:64, 0, h * 128:(h + 1) * 128], start=True, stop=True)
                    return ins
                P.op("pe", ust, reads=[b_KD, b_v[0], b_hi[0]], writes=[uRb, uHb])
                si = r % 4
                sf = stg[si].bitcast(F32)
                Snr, Snh = sf[:, 0:512], sf[:, 512:1024]
                decb = decS.unsqueeze(2).to_broadcast([128, 4, 128])
                P.op("pool", lambda e, k=k, decb=decb, Snr=Snr: e.tensor_tensor(out=Snr.rearrange("p (h e) -> p h e", h=4),
                                                                               in0=S0r[k].rearrange("p (h e) -> p h e", h=4), in1=decb, op=ALU.mult),
                     reads=[b_S0[k], b_const], writes=[b_stg[si]])
                P.op("dve", lambda e, uR=uR, Snr=Snr: e.tensor_tensor(out=Snr, in0=Snr, in1=uR[:, :], op=ALU.add),
                     reads=[uRb, b_stg[si]], writes=[b_stg[si]])
                e1b = e1s[:, :, r].unsqueeze(2).to_broadcast([128, 4, 128])
                P.op("dve", lambda e, k=k, uH=uH, Snh=Snh: e.tensor_tensor(out=Snh, in0=S0h[k], in1=uH[:, :], op=ALU.add),
                     reads=[uHb, b_S0[k], b_stg[si]], writes=[b_stg[si]])
                P.op("dve", lambda e, e1b=e1b, Snh=Snh: e.tensor_tensor(out=Snh.rearrange("p (h e) -> p h e", h=4),
                                                                        in0=Snh.rearrange("p (h e) -> p h e", h=4), in1=e1b, op=ALU.mult),
                     reads=[b_stg[si], b_e1s], writes=[b_stg[si]])
                P.dma_store("sp", b_stg[si], rets[r].rearrange("h d e -> d h e"), Snr.rearrange("p (h e) -> p h e", h=4))
                P.dma_store("sp", b_stg[si], hgs[r].rearrange("h d e -> d h e"), Snh.rearrange("p (h e) -> p h e", h=4))
        for h in range(4):
            head_norm_gate(oR[:, h * 64:(h + 1) * 64], oRb, Tw, RGAIN + h, grv[:, h, :], b_gr[h], ofv[:, h, :], ofb[h])
        for h in range(4):
            head_norm_gate(oH[:, h * 64:(h + 1) * 64], oHb, Tw, HGAIN + h, ghv[:, h, :], b_gh[h], ofv[:, 4 + h, :], ofb[4 + h])

    def sample_xattn(xqv, xqb, oav, oab):
        KTrv = KTr.rearrange("p (c m) -> p c m", m=256)
        PSv = PS_s.rearrange("p (m r h l) -> p m r h l", m=2, r=16, h=4)
        ops, opb = psum("L")
        opv = ops[:, :].rearrange("p (c t) -> p c t", t=64)
        b_PSr = [Buf(f"PSr{r}") for r in range(NREQ)]
        for r in range(NREQ):
            k = r % 2
            kv = Kbf[k].rearrange("p (b n) -> p b n", n=1024)
            vv = Vring[k].rearrange("p (b n) -> p b n", n=1024)
            P.dma_load("pool", b_Kbf[k], kv, cmk[r].rearrange("(b p) n -> p b n", p=128))
            P.dma_load("pool", b_Vr[k], vv, cmv[r].rearrange("(b p) n -> p b n", p=128))
            for mb in range(2):
                ps, pb = psum()
                psb = ps[:, :].bitcast(BF16)

                def tr(e, mb=mb, kv=kv, psb=psb):
                    ins = None
                    for c in range(8):
                        ins = e.transpose(psb[:, c * 128:(c + 1) * 128], kv[:, mb, c * 128:(c + 1) * 128], identb)
                    return ins
                P.op("pe", tr, reads=[b_Kbf[k], b_const], writes=[pb])
                src = psb[:, :].rearrange("p (c m) -> p c m", m=128)
                dst = KTrv[:, :, mb * 128:(mb + 1) * 128]
                if mb == 0:
                    P.op("act", lambda e, src=src, dst=dst: e.activation(out=dst, in_=src, func=AF.Copy), reads=[pb], writes=[b_KTr])
                else:
                    P.op("dve", lambda e, src=src, dst=dst: e.tensor_copy(dst, src), reads=[pb], writes=[b_KTr])
            sps, spb = psum()
            spv = sps[:, 0:32].rearrange("p (m h l) -> p m h l", m=2, h=4)

            def sc(e, r=r, spv=spv):
                ins = None
                for mb in range(2):
                    for hh in range(4):
                        for i in range(2):
                            ins = e.matmul(spv[:, mb, hh, :], KTrv[:, 2 * hh + i, mb * 128:(mb + 1) * 128],
                                           xqv[:, 2 * hh + i, 4 * r:4 * r + 4], start=(i == 0), stop=(i == 1))
                return ins
            P.op("pe", sc, reads=[b_KTr] + list(xqb), writes=[spb])
            P.op("act", lambda e, r=r, spv=spv: e.activation(out=PSv[:, :, r, :, :], in_=spv, func=AF.Exp, scale=1.0 / 16.0),
                 reads=[spb], writes=[b_PSr[r]])

            def pvx(e, r=r, vv=vv):
                ins = None
                for hh in range(4):
                    for i in range(2):
                        c0 = hh * 256 + i * 128
                        for mb in range(2):
                            ins = e.matmul(opv[:, 2 * hh + i, 4 * r:4 * r + 4], vv[:, mb, c0:c0 + 128], PSv[:, mb, r, hh, :],
                                           start=(mb == 0), stop=(mb == 1))
                return ins
            P.op("pe", pvx, reads=[b_Vr[k], b_PSr[r]], writes=[opb])
        dps, dpb = psum()

        def den(e):
            e.matmul(dps[:, 0:256], onesb, PS_s[:, 0:256], start=True, stop=False)
            return e.matmul(dps[:, 0:256], onesb, PS_s[:, 256:512], start=False, stop=True)
        P.op("pe", den, reads=b_PSr + [b_const], writes=[dpb])
        P.op("act", lambda e: e.activation(out=rden[:, 0:256], in_=dps[:, 0:256], func=AF.Ln), reads=[dpb], writes=[b_rden])
        P.op("act", lambda e: e.activation(out=rden[:, 0:256], in_=rden[:, 0:256], func=AF.Exp, scale=-1.0), reads=[b_rden], writes=[b_rden])
        rdv = rden[:, 0:256].rearrange("p (r h l) -> p r h l", r=16, h=4)
        for hh in range(4):
            in1 = rdv[:, :, hh, :].unsqueeze(1).to_broadcast([128, 2, 16, 4])
            P.op("dve", lambda e, hh=hh, in1=in1: e.tensor_tensor(
                out=oav[:, 2 * hh:2 * hh + 2, :].rearrange("p c (r l) -> p c r l", l=4),
                in0=opv[:, 2 * hh:2 * hh + 2, :].rearrange("p c (r l) -> p c r l", l=4), in1=in1, op=ALU.mult),
                reads=[opb, b_rden], writes=[oab[2 * hh], oab[2 * hh + 1]])

    for ti in range(4):
        run_tile(ti, 512, xp, ti * 512, yp, False)
    P.dma_store("sp", b_Sr, retp.rearrange("h d e -> d h e"), S_ret[:, :].rearrange("p (h e) -> p h e", h=4))
    P.dma_store("sp", b_Sh, hgp.rearrange("h d e -> d h e"), S_hg[:, :].rearrange("p (h e) -> p h e", h=4))
    P.barrier()
    run_tile(4, NS, xs, 0, ys, True)
    assert wstate["taken"] == len(plan), (wstate["taken"], len(plan))
    P.build()
    _CACHE["marks"] = P.mark_instr
    return nc


_CACHE = {}


def kernel(x_prompt, x_sample, mem_prompt, state_ret, state_hgrn, cache_mem_k, cache_mem_v,
           g_mix, w_in, ret_gain, hg_gain, hg_lb, w_out, g_xa, g_mem, w_xq, w_xk, w_xv, w_xo,
           g_ffn, w_gate, w_up, w_down, g_final):
    f = lambda a: np.ascontiguousarray(np.asarray(a, dtype=np.float32))
    if "nc" not in _CACHE:
        _CACHE["nc"] = build_program()
        _CACHE["consts"] = make_consts()
    nc = _CACHE["nc"]
    cf, cb, cs = (a.copy() for a in _CACHE["consts"])

    def fm(v, n):
        return f(v).reshape(n, 128).T
    pv = np.concatenate([fm(g_mix[0], 8), fm(g_xa[0], 8), fm(g_mem[0], 8), fm(g_ffn[0], 8), fm(g_final, 8),
                         fm(ret_gain[0], 4), fm(hg_gain[0], 4), fm(hg_lb[0], 4), fm(hg_lb[1], 4)], axis=1)
    cf[:, CF.sl("pvec")] = pv
    shared = {"w_in": f(w_in[0]), "w_out": f(w_out[0]), "w_xq": f(w_xq[0]), "w_xk": f(w_xk[0]), "w_xv": f(w_xv[0]),
              "w_xo": f(w_xo[0]), "w_gate": f(w_gate[0]), "w_up": f(w_up[0]), "w_down": f(w_down[0]),
              "cf": cf, "cb": cb, "cs": cs}
    in_maps = []
    for c in range(NCORES):
        rs = slice(c * NREQ, (c + 1) * NREQ)
        m = dict(shared)
        m["xp"] = f(x_prompt[c]); m["xs"] = f(x_sample[rs]).reshape(NS, D); m["memp"] = f(mem_prompt[c])
        m["sret"] = f(state_ret[0, rs]); m["shg"] = f(state_hgrn[0, rs])
        m["cmk"] = f(cache_mem_k[0, rs]).reshape(NREQ, NMEM, D); m["cmv"] = f(cache_mem_v[0, rs]).reshape(NREQ, NMEM, D)
        in_maps.append(m)
    res = run_bass_kernel_spmd(nc, in_maps, core_ids=list(range(NCORES)))
    R = res.results
    y_prompt = np.stack([R[c]["yp"] for c in range(NCORES)], 0)
    y_sample = np.concatenate([R[c]["ys"].reshape(NREQ, DEC, D) for c in range(NCORES)], 0)
    ret_p = np.stack([R[c]["retp"] for c in range(NCORES)], 0)[None]
    hg_p = np.stack([R[c]["hgp"] for c in range(NCORES)], 0)[None]
    mk_p = np.stack([R[c]["mkp"].reshape(NMEM, 4, 256) for c in range(NCORES)], 0)[None]
    mv_p = np.stack([R[c]["mvp"].reshape(NMEM, 4, 256) for c in range(NCORES)], 0)[None]
    ret_s = np.concatenate([R[c]["rets"] for c in range(NCORES)], 0)[None]
    hg_s = np.concatenate([R[c]["hgs"] for c in range(NCORES)], 0)[None]
    return (y_prompt.astype(np.float32), y_sample.astype(np.float32), ret_p.astype(np.float32), hg_p.astype(np.float32),
            mk_p.astype(np.float32), mv_p.astype(np.float32), ret_s.astype(np.float32), hg_s.astype(np.float32))
```

```python
import numpy as np
import concourse.bass as bass
import concourse.mybir as mybir
from concourse.bass_utils import run_bass_kernel_spmd

F32 = mybir.dt.float32
BF16 = mybir.dt.bfloat16
AF = mybir.ActivationFunctionType
ALU = mybir.AluOpType

D = 1024
SEQ = 2048
NREQ = 16
DEC = 4
NS = NREQ * DEC
PAST = 16384
DFF = 2816
NMEM = 256
EPS = 1e-6
NCORES = 8
GAM = [1.0 - 2.0 ** (-5.0 - h) for h in range(4)]
DEBUG = {"mixer": True, "xattn": True, "ffn": True}
STRICT_WAR = True


class Buf:
    __slots__ = ("name", "w", "r", "dsem", "dcnt")

    def __init__(self, name=""):
        self.name = name
        self.w = None
        self.r = {}
        self.dsem = {}
        self.dcnt = {}


class Prog:
    ENGS = ("pe", "act", "dve", "pool", "sp")

    def __init__(self, nc):
        self.nc = nc
        self.sem = {e: nc.alloc_semaphore(f"s_{e}") for e in self.ENGS}
        self.cnt = {e: 0 for e in self.ENGS}
        self.known = {e: {} for e in self.ENGS}
        self.q = {e: [] for e in self.ENGS}
        self.final = {}
        self.nsem = 0
        self.marks = []
        self.mark_instr = []

    def _need(self, eng, dep, waits):
        if dep is None:
            return
        sem, val = dep
        k = sem.name
        if self.known[eng].get(k, 0) >= val:
            return
        self.known[eng][k] = val
        waits.append((sem, val))

    def _deps(self, eng, reads, writes):
        waits = []
        for b in reads:
            self._need(eng, b.w, waits)
        for b in writes:
            self._need(eng, b.w, waits)
            for k, dep in b.r.items():
                if k == eng and not STRICT_WAR:
                    continue
                self._need(eng, dep, waits)
        return waits

    def op(self, eng, emit, reads=(), writes=()):
        waits = self._deps(eng, reads, writes)
        self.cnt[eng] += 1
        me = (self.sem[eng], self.cnt[eng])
        for b in reads:
            b.r[eng] = me
        for b in writes:
            b.w = me
            b.r = {}
        self.q[eng].append((waits, emit, (self.sem[eng], 1)))

    def _dsem(self, b, eng):
        kind = "sw" if eng == "pool" else "hw"
        if kind not in b.dsem:
            b.dsem[kind] = self.nc.alloc_semaphore(f"dq{self.nsem}")
            b.dcnt[kind] = 0
            self.nsem += 1
        b.dcnt[kind] += 16
        return b.dsem[kind], b.dcnt[kind]

    def dma_load(self, eng, buf, out, in_):
        waits = self._deps(eng, [], [buf])
        sem, cnt = self._dsem(buf, eng)
        buf.w = (sem, cnt)
        buf.r = {}
        self.q[eng].append((waits, lambda e: e.dma_start(out=out, in_=in_), (sem, 16)))

    def dma_store(self, eng, buf, out, in_):
        bufs = buf if isinstance(buf, (list, tuple)) else [buf]
        waits = self._deps(eng, bufs, [])
        own = bufs[0]
        sem, cnt = self._dsem(own, eng)
        for b in bufs:
            b.r["dma_" + sem.name] = (sem, cnt)
        self.final[sem.name] = (sem, cnt)
        self.q[eng].append((waits, lambda e: e.dma_start(out=out, in_=in_), (sem, 16)))

    def mark(self, name):
        self.marks.append((name, len(self.q["pe"])))

    def barrier(self):
        for e in ("pe", "act", "dve", "pool", "sp"):
            waits = []
            for o in ("pe", "act", "dve", "pool"):
                if o != e and self.cnt[o]:
                    self._need(e, (self.sem[o], self.cnt[o]), waits)
            if waits:
                self.q[e].append((waits, None, None))

    def build(self):
        nc = self.nc
        waits = list(self.final.values())
        for e in ("pe", "act", "dve", "pool"):
            if self.cnt[e]:
                waits.append((self.sem[e], self.cnt[e]))
        self.q["sp"].append((waits, None, None))

        prog = self

        class CountPE:
            def __init__(self, e):
                self.e = e
                self.n = 0

            def matmul(self, *a, **k):
                self.n += 1
                return self.e.matmul(*a, **k)

            def transpose(self, *a, **k):
                self.n += 1
                return self.e.transpose(*a, **k)

        def replay(items, e, count=False):
            ce = CountPE(e) if count else e
            mi = 0
            for idx, (waits, emit, inc) in enumerate(items):
                if count:
                    while mi < len(prog.marks) and prog.marks[mi][1] <= idx:
                        prog.mark_instr.append((prog.marks[mi][0], ce.n))
                        mi += 1
                for sem, val in waits:
                    e.wait_ge(sem, val)
                if emit is None:
                    continue
                ins = emit(ce)
                if inc is not None:
                    ins.then_inc(inc[0], inc[1])

        with nc.Block() as block:
            @block.tensor
            def _(e):
                replay(self.q["pe"], e, count=True)

            @block.scalar
            def _(e):
                replay(self.q["act"], e)

            @block.vector
            def _(e):
                replay(self.q["dve"], e)

            @block.gpsimd
            def _(e):
                replay(self.q["pool"], e)

            @block.sync
            def _(e):
                replay(self.q["sp"], e)


class Cols:
    def __init__(self):
        self.off = {}
        self.n = 0

    def add(self, name, n):
        self.off[name] = (self.n, n)
        self.n += n

    def sl(self, name):
        o, n = self.off[name]
        return slice(o, o + n)


CF = Cols()
for _n, _k in (("ident", 128), ("perm", 128), ("kdec", 16), ("kdecS", 4), ("decS", 4), ("pvec", 56), ("ones", 128)):
    CF.add(_n, _k)
CB = Cols()
for _n, _k in (("identb", 128), ("cmask", 128), ("maskSr", 256), ("maskSh", 64), ("memb", 16),
               ("qdec", 2048), ("qdecS", 256), ("maskr", 2048), ("onesb", 128)):
    CB.add(_n, _k)


def make_consts():
    cf = np.zeros((128, CF.n), np.float64)
    cb = np.zeros((128, CB.n), np.float64)
    p = np.arange(128)
    cf[:, CF.sl("ident")] = np.eye(128)
    perm = np.zeros((128, 128))
    for m in range(64):
        perm[m + 64, m] = -1.0
        perm[m, m + 64] = 1.0
    cf[:, CF.sl("perm")] = perm
    cf[:, CF.sl("ones")] = 1.0
    kdec = np.zeros((128, 4, 4))
    for h in range(4):
        for b in range(4):
            kdec[:, h, b] = GAM[h] ** (511 - (b * 128 + p))
    cf[:, CF.sl("kdec")] = kdec.reshape(128, 16)
    for h in range(4):
        cf[:, CF.off["kdecS"][0] + h] = GAM[h] ** (3 - (p % 4))
        cf[:, CF.off["decS"][0] + h] = GAM[h] ** 4
    cb[:, CB.sl("identb")] = np.eye(128)
    cb[:, CB.sl("onesb")] = 1.0
    cb[:, CB.sl("cmask")] = (p[:, None] <= p[None, :]).astype(np.float64)
    x = np.arange(512)
    mr = np.zeros((128, 4, 512))
    qd = np.zeros((128, 4, 512))
    for h in range(4):
        dlt = x[None, :] - p[:, None]
        mr[:, h, :] = np.where(dlt >= 0, GAM[h] ** np.maximum(dlt, 0), 0.0)
        qd[:, h, :] = GAM[h] ** (x[None, :] + 1.0)
    cb[:, CB.sl("maskr")] = mr.reshape(128, 2048)
    cb[:, CB.sl("qdec")] = qd.reshape(128, 2048)
    t = np.arange(64)
    same = (t[:, None] // 4) == (t[None, :] // 4)
    dl = (t[None, :] % 4) - (t[:, None] % 4)
    msr = np.zeros((128, 4, 64))
    qds = np.zeros((128, 4, 64))
    for h in range(4):
        msr[:64, h, :] = np.where(same & (dl >= 0), GAM[h] ** np.maximum(dl, 0), 0.0)
        qds[:, h, :] = GAM[h] ** ((t[None, :] % 4) + 1.0)
    cb[:, CB.sl("maskSr")] = msr.reshape(128, 256)
    cb[:, CB.sl("qdecS")] = qds.reshape(128, 256)
    cb[:64, CB.sl("maskSh")] = (same & (dl >= 0)).astype(np.float64)
    memb = np.zeros((128, 16))
    memb[:64] = ((t[:, None] // 4) == np.arange(16)[None, :]).astype(np.float64)
    cb[:, CB.sl("memb")] = memb
    inv_freq = (np.float32(10000.0) ** (-(np.arange(64, dtype=np.float32) / np.float32(64)))).astype(np.float32)
    pos = np.concatenate([np.arange(SEQ, dtype=np.float32),
                          (PAST + (np.arange(NS) % 4)).astype(np.float32)])
    ang = (pos[:, None] * inv_freq[None, :]).astype(np.float32).astype(np.float64)
    cosT = np.cos(ang).T
    sinT = np.sin(ang).T
    cs = np.stack([np.concatenate([cosT, cosT], 0), np.concatenate([sinT, sinT], 0)], 1)
    return cf.astype(np.float32), cb.astype(np.float32), np.ascontiguousarray(cs.astype(np.float32))


def build_program():
    nc = bass.Bass("TRN2", target_bir_lowering=False)
    P = Prog(nc)

    def din(name, shape):
        return nc.dram_tensor(name, list(shape), F32, kind="ExternalInput").ap()

    def dout(name, shape):
        return nc.dram_tensor(name, list(shape), F32, kind="ExternalOutput").ap()

    xp = din("xp", [SEQ, D]); xs = din("xs", [NS, D]); memp = din("memp", [NMEM, D])
    sret = din("sret", [NREQ, 4, 128, 128]); shg = din("shg", [NREQ, 4, 128, 128])
    cmk = din("cmk", [NREQ, NMEM, D]); cmv = din("cmv", [NREQ, NMEM, D])
    w_in = din("w_in", [D, 4096]); w_out = din("w_out", [D, D]); w_xq = din("w_xq", [D, D])
    w_xk = din("w_xk", [D, D]); w_xv = din("w_xv", [D, D]); w_xo = din("w_xo", [D, D])
    w_gate = din("w_gate", [D, DFF]); w_up = din("w_up", [D, DFF]); w_down = din("w_down", [DFF, D])
    cf_d = din("cf", [128, CF.n]); cb_d = din("cb", [128, CB.n]); cs_d = din("cs", [128, 2, SEQ + NS])
    yp = dout("yp", [SEQ, D]); ys = dout("ys", [NS, D])
    retp = dout("retp", [4, 128, 128]); hgp = dout("hgp", [4, 128, 128])
    mkp = dout("mkp", [NMEM, D]); mvp = dout("mvp", [NMEM, D])
    rets = dout("rets", [NREQ, 4, 128, 128]); hgs = dout("hgs", [NREQ, 4, 128, 128])

    def sb(name, n, dt=F32):
        return nc.alloc_sbuf_tensor("sb_" + name, [128, n], dt)

    cf = sb("cf", CF.n); cb = sb("cb", CB.n, BF16)
    b_const = Buf("const")
    xT = sb("xT", 8 * 512)
    G = [sb(f"G{i}", 8 * 512, BF16) for i in range(3)]
    hid = sb("hid", 22 * 512, BF16)
    xio = [sb(f"xio{i}", 1024) for i in range(2)]
    rstd = sb("rstd", 512)
    qT = sb("qT", 4 * 512, BF16); kT = sb("kT", 4 * 512, BF16)
    gate_r = sb("gate_r", 4 * 512, BF16); gate_h = sb("gate_h", 4 * 512, BF16)
    tmpA = sb("tmpA", 512); tmpB = sb("tmpB", 512); tmpC = sb("tmpC", 512); tmpA2 = sb("tmpA2", 512)
    rot_n = [0]
    tmpD = tmpB; tmpE = tmpC
    qb = sb("qb", 4 * 512, BF16); kb = sb("kb", 4 * 512, BF16)
    v_tm = sb("v_tm", 4 * 512, BF16); hi_tm = sb("hi_tm", 4 * 512, BF16)
    kd_tm = sb("kd_tm", 16 * 128, BF16); kbT_tm = sb("kbT_tm", 16 * 128, BF16)
    attT = sb("attT", 4 * 512, BF16)
    attH = [sb(f"attH{i}", 128, BF16) for i in range(2)]
    S_ret = sb("S_ret", 512); S_retb = sb("S_retb", 512, BF16)
    S_hg = sb("S_hg", 512); Sp = sb("Sp", 512, BF16)
    evec = sb("evec", 80)
    o_sbs = [sb(f"o_sb{i}", 512) for i in range(2)]; sqos = [sb(f"sqo{i}", 512, BF16) for i in range(2)]
    rrs = [sb(f"rr{i}", 512) for i in range(2)]
    qd = sb("qd", 512, BF16)
    PTs = [sb(f"PT{i}", 2 * 512, BF16) for i in range(2)]; rdens = [sb(f"rden{i}", 512) for i in range(2)]
    rden = rdens[0]
    KT = sb("KT", 8 * 256, BF16); Vb = sb("Vb", 2 * 1024, BF16)
    cs = [sb("cs0", 2 * 512)]
    wslot = [sb(f"w{i}", 4096, BF16) for i in range(3)]
    ab = sb("ab", 16)

    ident = cf[:, CF.sl("ident")]; perm = cf[:, CF.sl("perm")]
    identb = cb[:, CB.sl("identb")]; onesb = cb[:, CB.sl("onesb")]; onesf = cf[:, CF.sl("ones")]
    pv0 = CF.off["pvec"][0]

    def pvec(i):
        return cf[:, pv0 + i:pv0 + i + 1]
    G_MIX, G_XA, G_MEM, G_FFN, G_FIN, RGAIN, HGAIN, LB0, LB1 = 0, 8, 16, 24, 32, 40, 44, 48, 52

    banks = [nc.alloc_psum_tensor(f"ps{i}", [128, 512], F32) for i in range(8)]
    bbuf = [Buf(f"ps{i}") for i in range(8)]
    rot = {"L": [0, 1], "S": [2, 3, 4, 5, 6, 7]}
    rpos = {"L": 0, "S": 0}

    def psum(kind="S"):
        lst = rot[kind]
        i = lst[rpos[kind] % len(lst)]
        rpos[kind] += 1
        return banks[i], bbuf[i]

    P.dma_load("sp", b_const, cf[:], cf_d)
    P.dma_load("pool", b_const, cb[:], cb_d)
    b_ab = Buf("ab")
    P.op("dve", lambda e: e.tensor_tensor(out=ab[:, 12:16], in0=cf[:, pv0 + LB0:pv0 + LB0 + 4],
                                          in1=cf[:, pv0 + LB1:pv0 + LB1 + 4], op=ALU.subtract),
         reads=[b_const], writes=[b_ab])
    P.op("act", lambda e: e.activation(out=ab[:, 12:16], in_=ab[:, 12:16], func=AF.Tanh, scale=0.5),
         reads=[b_ab], writes=[b_ab])
    P.op("dve", lambda e: e.tensor_scalar(out=ab[:, 0:4], in0=ab[:, 12:16], scalar1=0.25, scalar2=0.75,
                                          op0=ALU.mult, op1=ALU.add), reads=[b_ab], writes=[b_ab])
    P.op("dve", lambda e: e.tensor_scalar(out=ab[:, 4:8], in0=ab[:, 12:16], scalar1=-0.25, scalar2=0.25,
                                          op0=ALU.mult, op1=ALU.add), reads=[b_ab], writes=[b_ab])
    P.op("dve", lambda e: e.tensor_scalar(out=ab[:, 8:12], in0=ab[:, 12:16], scalar1=0.25, scalar2=-0.25,
                                          op0=ALU.mult, op1=ALU.add), reads=[b_ab], writes=[b_ab])

    wbuf = [Buf(f"w{i}") for i in range(3)]
    plan = []
    wstate = {"issued": 0, "taken": 0}

    def plan_linear(W, K, c0, ncols):
        nk = K // 128
        step = 512 if nk == 8 else 128
        for c in range(c0, c0 + ncols, step):
            plan.append(([(W, c, min(step, c0 + ncols - c))], nk))

    NSLAB_TILE = 33
    wscr = nc.dram_tensor("wscr", [NSLAB_TILE, 128, 4096], BF16, kind="Internal").ap()
    scr_gate = [False]

    def slab_views(i):
        parts, nk = plan[i]
        views, off = [], 0
        for (_, _, n) in parts:
            views.append(wslot[i % 3][:, off:off + nk * n].rearrange("p (k n) -> p k n", n=n))
            off += nk * n
        return views, off

    def issue_next():
        i = wstate["issued"]
        if i >= len(plan):
            return
        parts, nk = plan[i]
        slot = wslot[i % 3]
        views, tot = slab_views(i)
        t, s_ = ((i - 4) // NSLAB_TILE, (i - 4) % NSLAB_TILE) if i >= 4 else (-1, -1)
        if t >= 1:
            if not scr_gate[0]:
                scr_gate[0] = True
                P.q["sp"].append(([(wbuf[k].dsem[kd], wbuf[k].dcnt[kd]) for k in range(3) for kd in wbuf[k].dsem], None, None))
            P.dma_load("sp", wbuf[i % 3], slot[:, 0:tot], wscr[s_][:, 0:tot])
        else:
            for (W, c0, n), dst in zip(parts, views):
                P.dma_load("pool", wbuf[i % 3], dst, W[:, c0:c0 + n].rearrange("(k p) n -> p k n", p=128))
            if t == 0:
                P.dma_store("sp", wbuf[i % 3], wscr[s_][:, 0:tot], slot[:, 0:tot])
        wstate["issued"] += 1

    def take_slab(W, c0, hold=0, multi=False):
        i = wstate["taken"]
        assert plan[i][0][0][0] is W and plan[i][0][0][1] == c0, (i, plan[i][0][0][1], c0)
        while wstate["issued"] < min(i + 3 - hold, len(plan)):
            issue_next()
        wstate["taken"] += 1
        views, _ = slab_views(i)
        return (views if multi else views[0]), wbuf[i % 3]

    W_IN_ORDER = [2560, 2048, 1024, 512, 0, 1536, 3072, 3584]
    plan_linear(w_xk, D, 0, D); plan_linear(w_xv, D, 0, D)
    for _t in range(5):
        for c0 in W_IN_ORDER:
            plan_linear(w_in, D, c0, 512)
        plan_linear(w_out, D, 0, D); plan_linear(w_xq, D, 0, D); plan_linear(w_xo, D, 0, D)
        for g0 in range(0, DFF, 256):
            plan.append(([(w_gate, g0, 256), (w_up, g0, 256)], 8))
        plan_linear(w_down, DFF, 0, D)
    assert len(plan) == 4 + 5 * NSLAB_TILE, len(plan)

    def fmview(t, nch, Tw):
        return t[:, 0:nch * Tw].rearrange("p (c t) -> p c t", t=Tw)

    gbuf = [[Buf(f"G{i}_{c}") for c in range(8)] for i in range(3)]
    xTb = [Buf(f"xT{c}") for c in range(8)]
    b_rstd = Buf("rstd")
    xiob = [Buf("xio0"), Buf("xio1")]
    xio_n = [0]

    def load_tile(src_rows, r0, Tw):
        xv = fmview(xT, 8, Tw)
        nb = (Tw + 127) // 128
        for b in range(nb):
            n = min(128, Tw - b * 128)
            k = xio_n[0] % 2
            xio_n[0] += 1
            P.dma_load("sp", xiob[k], xio[k][0:n, :], src_rows[r0 + b * 128:r0 + b * 128 + n, :])
            for g in range(2):
                ps, pb = psum()

                def tr(e, k=k, g=g, n=n, ps=ps):
                    ins = None
                    for cc in range(4):
                        c = g * 4 + cc
                        ins = e.transpose(ps[:, cc * 128:cc * 128 + n], xio[k][0:n, c * 128:(c + 1) * 128], ident[0:n, 0:n])
                    return ins
                P.op("pe", tr, reads=[xiob[k], b_const], writes=[pb])
                src = ps[:, :].rearrange("p (c t) -> p c t", t=128)[:, :, 0:n]
                dst = xv[:, g * 4:g * 4 + 4, b * 128:b * 128 + n]
                eng = "act" if g == 0 else "dve"
                if eng == "act":
                    P.op("act", lambda e, dst=dst, src=src: e.activation(out=dst, in_=src, func=AF.Copy),
                         reads=[pb], writes=xTb[g * 4:g * 4 + 4])
                else:
                    P.op("dve", lambda e, dst=dst, src=src: e.tensor_copy(dst, src),
                         reads=[pb], writes=xTb[g * 4:g * 4 + 4])

    def store_tile(dst_rows, r0, Tw):
        xv = fmview(xT, 8, Tw)
        nb = (Tw + 127) // 128
        for b in range(nb):
            n = min(128, Tw - b * 128)
            k = xio_n[0] % 2
            xio_n[0] += 1
            for g in range(2):
                ps, pb = psum()

                def tr(e, g=g, n=n, ps=ps, b=b):
                    ins = None
                    for cc in range(4):
                        c = g * 4 + cc
                        ins = e.transpose(ps[0:n, cc * 128:(cc + 1) * 128], xv[:, c, b * 128:b * 128 + n], ident)
                    return ins
                P.op("pe", tr, reads=xTb[g * 4:g * 4 + 4] + [b_const], writes=[pb])
                dst = xio[k][0:n, g * 512:(g + 1) * 512]
                if g == 0:
                    P.op("act", lambda e, dst=dst, ps=ps, n=n: e.activation(out=dst, in_=ps[0:n, :], func=AF.Copy),
                         reads=[pb], writes=[xiob[k]])
                else:
                    P.op("dve", lambda e, dst=dst, ps=ps, n=n: e.tensor_copy(dst, ps[0:n, :]),
                         reads=[pb], writes=[xiob[k]])
            P.dma_store("sp", xiob[k], dst_rows[r0 + b * 128:r0 + b * 128 + n, :], xio[k][0:n, :])

    def rstd_from_psum(ps, pb, Tw, inv_n, out_ap, out_buf):
        P.op("act", lambda e: e.activation(out=out_ap, in_=ps[:, 0:Tw], func=AF.Ln, scale=inv_n, bias=epsb[:, 0:1]),
             reads=[pb, b_ab], writes=[out_buf])
        P.op("act", lambda e: e.activation(out=out_ap, in_=out_ap, func=AF.Exp, scale=-0.5),
             reads=[out_buf], writes=[out_buf])

    def rmsnorm(gi, gidx, sqi, Tw, out_f32_inplace=False, pre=None):
        xv = fmview(xT, 8, Tw)
        sqv = fmview(G[sqi], 8, Tw)
        if pre is not None:
            flush()
            ps, pb = pre
        else:
            P.op("act", lambda e: e.activation(out=G[sqi][:, 0:8 * Tw], in_=xT[:, 0:8 * Tw], func=AF.Square),
                 reads=xTb, writes=gbuf[sqi])
            ps, pb = psum()

            def mm(e):
                ins = None
                for c in range(8):
                    ins = e.matmul(ps[:, 0:Tw], onesb, sqv[:, c, :], start=(c == 0), stop=(c == 7))
                return ins
            P.op("pe", mm, reads=gbuf[sqi] + [b_const], writes=[pb])
        rstd_from_psum(ps, pb, Tw, 1.0 / D, rstd[:, 0:Tw], b_rstd)
        for c in range(8):
            if out_f32_inplace:
                o, ob = xv[:, c, :], xTb[c]
            else:
                o, ob = fmview(G[gi], 8, Tw)[:, c, :], gbuf[gi][c]
            P.op("dve", lambda e, o=o, c=c: e.scalar_tensor_tensor(
                out=o, in0=xv[:, c, :], scalar=pvec(gidx + c), in1=rstd[:, 0:Tw], op0=ALU.mult, op1=ALU.mult),
                 reads=[xTb[c], b_rstd, b_const], writes=[ob])

    cur = {"ti": 0, "sample": False}

    def veng():
        return "pool" if (1 <= cur["ti"] <= 3) else "dve"

    pend = []

    def flush():
        while pend:
            pend.pop(0)()

    def linear_fm(W, c0, ncols, in_t, in_bufs, nk, Tw, consumer, slab=None, fine=False):
        inv = fmview(in_t, nk, Tw)
        step = 512 if nk == 8 else 128
        for s0 in range(c0, c0 + ncols, step):
            n = min(step, c0 + ncols - s0)
            sl, slb = slab if slab is not None else take_slab(W, s0)
            for j in range(n // 128):
                ps, pb = psum()

                def mm(e, sl=sl, j=j, ps=ps):
                    ins = None
                    for k in range(nk):
                        ins = e.matmul(ps[:, 0:Tw], sl[:, k, j * 128:(j + 1) * 128], inv[:, k, :],
                                       start=(k == 0), stop=(k == nk - 1))
                    return ins
                if fine and s0 == c0 and j == 0:
                    for k in range(nk):
                        P.op("pe", lambda e, sl=sl, j=j, ps=ps, k=k: e.matmul(
                            ps[:, 0:Tw], sl[:, k, j * 128:(j + 1) * 128], inv[:, k, :], start=(k == 0), stop=(k == nk - 1)),
                            reads=[slb, in_bufs[k]], writes=[pb])
                else:
                    P.op("pe", mm, reads=[slb] + list(in_bufs), writes=[pb])
                flush()
                consumer((s0 - c0) // 128 + j, ps, pb)

    def linear_tm(W, c0, in_t, in_bufs, Tw, consumer, slab=None):
        inv = fmview(in_t, 8, Tw)
        sl, slb = slab if slab is not None else take_slab(W, c0)
        nb = (Tw + 127) // 128
        for b in range(nb):
            n = min(128, Tw - b * 128)
            ps, pb = psum()

            def mm(e, b=b, n=n, ps=ps):
                ins = None
                for k in range(8):
                    ins = e.matmul(ps[0:n, :], inv[:, k, b * 128:b * 128 + n], sl[:, k, :],
                                   start=(k == 0), stop=(k == 7))
                return ins
            P.op("pe", mm, reads=[slb] + list(in_bufs), writes=[pb])
            consumer(b, n, ps, pb)

    def resid_add(Tw, enabled=True, sq=None):
        xv = fmview(xT, 8, Tw)
        if sq is not None:
            sqi, nps, npb = sq
            sqv = fmview(G[sqi], 8, Tw)

        def cons(c, ps, pb):
            if enabled:
                P.op("dve", lambda e: e.tensor_tensor(out=xv[:, c, :], in0=xv[:, c, :], in1=ps[:, 0:Tw], op=ALU.add),
                     reads=[pb, xTb[c]], writes=[xTb[c]])
            if sq is not None:
                P.op("act", lambda e: e.activation(out=sqv[:, c, :], in_=xv[:, c, :], func=AF.Square),
                     reads=[xTb[c]], writes=[gbuf[sqi][c]])

                def post():
                    P.op("pe", lambda e: e.matmul(nps[:, 0:Tw], onesb, sqv[:, c, :], start=(c == 0), stop=(c == 7)),
                         reads=[gbuf[sqi][c], b_const], writes=[npb])
                pend.append(post)
        return cons

    epsb = sb("epsb", 1)
    P.op("dve", lambda e: e.memset(epsb[:], EPS), writes=[b_ab])

    b_qT = [Buf(f"qT{h}") for h in range(4)]; b_kT = [Buf(f"kT{h}") for h in range(4)]
    b_gr = [Buf(f"gr{h}") for h in range(4)]; b_gh = [Buf(f"gh{h}") for h in range(4)]
    b_qb = [Buf(f"qb{h}") for h in range(4)]; b_kb = [Buf(f"kb{h}") for h in range(4)]
    b_v = [Buf(f"v{b}") for b in range(4)]; b_hi = [Buf(f"hi{b}") for b in range(4)]
    b_kd = [Buf(f"kd{h}") for h in range(4)]; b_kbT = [Buf(f"kbT{h}") for h in range(4)]
    b_tA, b_tB, b_tC, b_tA2 = Buf("tA"), Buf("tB"), Buf("tC"), Buf("tA2")
    b_tD, b_tE = b_tB, b_tC
    b_aR, b_aH = Buf("aR"), Buf("aH")
    b_attT = Buf("attT"); b_attH = [Buf("attH0"), Buf("attH1")]
    b_attT2 = Buf("attT2")
    b_bbs = [Buf(f"bbs{h}") for h in range(4)]
    b_attHs = [Buf(f"attHs{c}") for c in range(16)]
    b_Sps = [Buf(f"Sps{c}") for c in range(16)]
    b_Sr = [Buf(f"Sr{h}") for h in range(4)]; b_Srb = [Buf(f"Srb{h}") for h in range(4)]
    b_Sh = [Buf(f"Sh{h}") for h in range(4)]; b_Sp = [Buf(f"Sp{h}") for h in range(4)]
    b_ev = [Buf(f"ev{h}") for h in range(4)]
    b_osbs, b_sqos, b_rrs = [Buf("osb0"), Buf("osb1")], [Buf("sqo0"), Buf("sqo1")], [Buf("rr0"), Buf("rr1")]
    b_qd = Buf("qd")
    b_PTs, b_rdens = [Buf("PT0"), Buf("PT1")], [Buf("rden0"), Buf("rden1")]
    b_rden = b_rdens[0]
    hn_n = [0]
    b_KT, b_Vb = Buf("KT"), Buf("Vb")
    b_cs = [Buf("cs0")]
    b_hid = [Buf(f"hid{j}") for j in range(22)]

    P.op("dve", lambda e: e.memset(S_ret[:], 0.0), writes=b_Sr)
    P.op("dve", lambda e: e.memset(S_hg[:], 0.0), writes=b_Sh)
    P.op("dve", lambda e: e.memset(S_retb[:], 0.0), writes=b_Srb)

    load_tile(memp, 0, NMEM)
    rmsnorm(0, G_MEM, 1, NMEM)
    KTv = KT[:, :].rearrange("p (c m) -> p c m", m=NMEM)
    Vbv = Vb[:, :].rearrange("p (b n) -> p b n", n=D)
    for (W, dst, isk) in ((w_xk, mkp, True), (w_xv, mvp, False)):
        for s0 in (0, 512):
            slab = take_slab(W, s0)
            if isk:
                def consK(c, ps, pb, s0=s0):
                    cc = s0 // 128 + c
                    P.op("act", lambda e: e.activation(out=KTv[:, cc, :], in_=ps[:, 0:NMEM], func=AF.Copy),
                         reads=[pb], writes=[b_KT])
                linear_fm(W, s0, 512, G[0], gbuf[0], 8, NMEM, consK, slab=slab)

            def consTM(b, n, ps, pb, s0=s0, dst=dst, isk=isk):
                k = xio_n[0] % 2
                xio_n[0] += 1
                P.op("dve", lambda e: e.tensor_copy(xio[k][:, 0:512], ps[:, :]), reads=[pb], writes=[xiob[k]])
                if not isk:
                    P.op("act", lambda e: e.activation(out=Vbv[:, b, s0:s0 + 512], in_=xio[k][:, 0:512], func=AF.Copy),
                         reads=[xiob[k]], writes=[b_Vb])
                P.dma_store("sp", xiob[k], dst[b * 128:(b + 1) * 128, s0:s0 + 512], xio[k][:, 0:512])
            linear_tm(W, s0, G[0], gbuf[0], NMEM, consTM, slab=slab)

    def head_norm_gate(o_ps, pb, Tw, gain_col, gate_ap, gate_buf, out_ap, out_buf):
        i = hn_n[0] % 2
        hn_n[0] += 1
        o_sb, sqo, rr = o_sbs[i], sqos[i], rrs[i]
        b_osb, b_sqo, b_rr = b_osbs[i], b_sqos[i], b_rrs[i]
        P.op("act", lambda e: e.activation(out=o_sb[:, 0:Tw], in_=o_ps, func=AF.Copy), reads=[pb], writes=[b_osb])
        P.op("act", lambda e: e.activation(out=sqo[:, 0:Tw], in_=o_ps, func=AF.Square), reads=[pb], writes=[b_sqo])
        ps, pb2 = psum()
        P.op("pe", lambda e: e.matmul(ps[:, 0:Tw], onesb, sqo[:, 0:Tw], start=True, stop=True),
             reads=[b_sqo, b_const], writes=[pb2])
        rstd_from_psum(ps, pb2, Tw, 1.0 / 128, rr[:, 0:Tw], b_rr)
        P.op("dve", lambda e: e.scalar_tensor_tensor(out=o_sb[:, 0:Tw], in0=o_sb[:, 0:Tw], scalar=pvec(gain_col),
                                                      in1=rr[:, 0:Tw], op0=ALU.mult, op1=ALU.mult),
             reads=[b_osb, b_rr, b_const], writes=[b_osb])
        P.op(veng(), lambda e: e.tensor_tensor(out=out_ap, in0=o_sb[:, 0:Tw], in1=gate_ap, op=ALU.mult),
             reads=[b_osb, gate_buf], writes=[out_buf])

    def run_tile(ti, Tw, src_rows, r0, dst_rows, sample):
        cur["ti"], cur["sample"] = ti, sample
        nb = (Tw + 127) // 128
        ck = 0
        csv = cs[ck][:, 0:2 * Tw].rearrange("p (a t) -> p a t", t=Tw)
        P.dma_load("sp", b_cs[ck], csv, cs_d[:, :, (SEQ if sample else r0):(SEQ if sample else r0) + Tw])
        P.mark(f"t{ti}:load")
        load_tile(src_rows, r0, Tw)
        rmsnorm(0, G_MIX, 1, Tw)
        P.mark(f"t{ti}:w_in")
        h1, h1b = G[0], gbuf[0]
        qTv, kTv = fmview(qT, 4, Tw), fmview(kT, 4, Tw)
        grv, ghv = fmview(gate_r, 4, Tw), fmview(gate_h, 4, Tw)
        qbv, kbv = fmview(qb, 4, Tw), fmview(kb, 4, Tw)
        v_v = v_tm[:, :].rearrange("p (b n) -> p b n", n=512)
        hi_v = hi_tm[:, :].rearrange("p (b n) -> p b n", n=512)
        kd_v = kd_tm[:, :].rearrange("p (h b d) -> p h b d", h=4, b=4)
        kbT_v = kbT_tm[:, :].rearrange("p (h b d) -> p h b d", h=4, b=4)
        ofT, ofb = G[1], gbuf[1]
        ofv = fmview(ofT, 8, Tw)

        def cons_tm(dstv, bufs):
            def c(b, n, ps, pb):
                P.op("act", lambda e: e.activation(out=dstv[0:n, b, :], in_=ps[0:n, :], func=AF.Copy),
                     reads=[pb], writes=[bufs[b]])
            return c

        def cons_rot(dstv, bufs, scale):
            def c(h, ps, pb):
                i = rot_n[0] % 2
                rot_n[0] += 1
                tA, bA = (tmpA, b_tA) if i == 0 else (tmpA2, b_tA2)
                P.op("act", lambda e: e.activation(out=tA[:, 0:Tw], in_=ps[:, 0:Tw], func=AF.Copy, scale=scale),
                     reads=[pb], writes=[bA])

                def post():
                    ps2, pb2 = psum()
                    P.op("pe", lambda e: e.matmul(ps2[:, 0:Tw], perm, tA[:, 0:Tw], start=True, stop=True),
                         reads=[bA, b_const], writes=[pb2])
                    P.op("dve", lambda e: e.tensor_tensor(out=tmpB[:, 0:Tw], in0=ps2[:, 0:Tw], in1=csv[:, 1, :], op=ALU.mult),
                         reads=[pb2, b_cs[ck]], writes=[b_tB])
                    P.op(veng(), lambda e: e.tensor_tensor(out=tmpC[:, 0:Tw], in0=tA[:, 0:Tw], in1=csv[:, 0, :], op=ALU.mult),
                         reads=[bA, b_cs[ck]], writes=[b_tC])
                    P.op(veng(), lambda e: e.tensor_tensor(out=dstv[:, h, :], in0=tmpB[:, 0:Tw], in1=tmpC[:, 0:Tw], op=ALU.add),
                         reads=[b_tB, b_tC], writes=[bufs[h]])
                pend.append(post)
            return c

        def cons_silu(dstv, bufs):
            def c(h, ps, pb):
                P.op("act", lambda e: e.activation(out=dstv[:, h, :], in_=ps[:, 0:Tw], func=AF.Silu),
                     reads=[pb], writes=[bufs[h]])
            return c

        def emit_kd():
            for h in range(4):
                ps, pb = psum()
                psb = ps[:, :].bitcast(BF16)

                def tr(e, h=h, psb=psb):
                    ins = None
                    for b in range(nb):
                        n = min(128, Tw - b * 128)
                        ins = e.transpose(psb[0:n, b * 128:(b + 1) * 128], kTv[:, h, b * 128:b * 128 + n], identb)
                    return ins
                P.op("pe", tr, reads=[b_kT[h], b_const], writes=[pb])
                for b in range(nb):
                    n = min(128, Tw - b * 128)
                    if sample:
                        sc = cf[0:n, CF.off["kdecS"][0] + h:CF.off["kdecS"][0] + h + 1]
                    else:
                        sc = cf[0:n, CF.off["kdec"][0] + h * 4 + b:CF.off["kdec"][0] + h * 4 + b + 1]
                    P.op("dve", lambda e, b=b, n=n, sc=sc, psb=psb, h=h: e.tensor_scalar(
                        out=kd_v[0:n, h, b, :], in0=psb[0:n, b * 128:(b + 1) * 128], scalar1=sc, scalar2=None, op0=ALU.mult),
                        reads=[pb, b_const], writes=[b_kd[h]])

        def emit_kbT(h):
            ps2, pb2 = psum()
            psb = ps2[:, :].bitcast(BF16)

            def tr(e):
                ins = None
                for b in range(nb):
                    n = min(128, Tw - b * 128)
                    ins = e.transpose(psb[0:n, b * 128:(b + 1) * 128], kbv[:, h, b * 128:b * 128 + n], identb)
                return ins
            P.op("pe", tr, reads=[b_kb[h], b_const], writes=[pb2])
            n0 = min(128, Tw)
            P.op("act", lambda e: e.activation(out=kbT_v[0:n0, h, 0:nb, :],
                                               in_=psb[0:n0, 0:nb * 128].rearrange("p (b d) -> p b d", d=128), func=AF.Copy),
                 reads=[pb2], writes=[b_kbT[h]])


        sbase = 10624 if sample else 0
        bbs = hid[:, sbase:sbase + 8 * Tw].bitcast(F32).rearrange("p (h t) -> p h t", h=4)
        attHs = hid[:, 4096:6144].rearrange("p (c t) -> p c t", t=128)
        Sps = hid[:, 6144:8192].rearrange("p (c t) -> p c t", t=128)
        attT2 = hid[:, 8192:10240]

        def evv(h):
            return evec[:, h * 20:(h + 1) * 20]

        def cons_hf(h, ps, pb):
            P.op("act", lambda e: e.activation(out=hlf[:, h, 0:Tw], in_=ps[:, 0:Tw], func=AF.Tanh, scale=0.5),
                 reads=[pb], writes=[b_hsc[h]])

        def cons_hq(h, ps, pb):
            P.op("act", lambda e: e.activation(out=qbv[:, h, :], in_=ps[:, 0:Tw], func=AF.Silu), reads=[pb], writes=[b_qb[h]])

        def stB1():
            for h in range(4):
                P.op("dve", lambda e, h=h: e.tensor_scalar(out=hkk[:, h, 0:Tw], in0=hlf[:, h, 0:Tw], scalar1=ab[:, 8 + h:9 + h],
                                                           scalar2=ab[:, 4 + h:5 + h], op0=ALU.mult, op1=ALU.add),
                     reads=[b_hsc[h], b_ab], writes=[b_hsk[h]])
            for h in range(4):
                P.op("act", lambda e, h=h: e.activation(out=hlf[:, h, 0:Tw], in_=hlf[:, h, 0:Tw], func=AF.Ln,
                                                        scale=ab[:, 4 + h:5 + h], bias=ab[:, 0 + h:1 + h]),
                     reads=[b_hsc[h], b_ab], writes=[b_hsc[h]])

        def stB3():
            for h in range(4):
                if not sample:
                    for j in range(4):
                        P.op("dve", lambda e, h=h, j=j: e.tensor_tensor_scan(
                            out=bbs[:, h, j * 128:(j + 1) * 128], data0=onesf, data1=hlf[:, h, j * 128:(j + 1) * 128],
                            initial=0.0, op0=ALU.mult, op1=ALU.add), reads=[b_hsc[h], b_const], writes=[b_bbs[h]])
                else:
                    l3 = hlf[:, h, 0:Tw].rearrange("p (r l) -> p r l", l=4)
                    b3 = bbs[:, h, :].rearrange("p (r l) -> p r l", l=4)
                    P.op("dve", lambda e, l3=l3, b3=b3: e.tensor_copy(b3[:, :, 0], l3[:, :, 0]), reads=[b_hsc[h]], writes=[b_bbs[h]])
                    for l in range(1, 4):
                        P.op("dve", lambda e, l=l, l3=l3, b3=b3: e.tensor_tensor(out=b3[:, :, l], in0=b3[:, :, l - 1], in1=l3[:, :, l], op=ALU.add),
                             reads=[b_hsc[h], b_bbs[h]], writes=[b_bbs[h]])

        def stB4():
            if sample:
                return
            for h in range(4):
                ev = evv(h)
                b4 = bbs[:, h, :].rearrange("p (j t) -> p j t", t=128)
                P.op("dve", lambda e, ev=ev, b4=b4: e.tensor_copy(ev[:, 16:20], b4[:, :, 63]), reads=[b_bbs[h]], writes=[b_ev[h]])
                P.op("act", lambda e, ev=ev, b4=b4: e.activation(out=ev[:, 0:4], in_=b4[:, :, 127], func=AF.Exp), reads=[b_bbs[h]], writes=[b_ev[h]])
                P.op("act", lambda e, ev=ev, b4=b4: e.activation(out=ev[:, 8:12], in_=b4[:, :, 63], func=AF.Exp), reads=[b_bbs[h]], writes=[b_ev[h]])
                P.op("dve", lambda e, ev=ev, b4=b4: e.tensor_tensor(out=b4, in0=b4, in1=ev[:, 16:20].unsqueeze(2).to_broadcast([128, 4, 128]),
                                                                   op=ALU.subtract), reads=[b_bbs[h], b_ev[h]], writes=[b_bbs[h]])

        def stB5():
            for h in range(4):
                P.op("act", lambda e, h=h: e.activation(out=hlf[:, h, 0:Tw], in_=bbs[:, h, :], func=AF.Exp), reads=[b_bbs[h]], writes=[b_hsc[h]])
            for h in range(4):
                P.op("act", lambda e, h=h: e.activation(out=bbs[:, h, :], in_=bbs[:, h, :], func=AF.Exp, scale=-1.0), reads=[b_bbs[h]], writes=[b_bbs[h]])

        def stB6():
            for h in range(4):
                if not sample:
                    P.op("dve", lambda e, h=h: e.tensor_copy(evv(h)[:, 4:8], hlf[:, h, 0:Tw].rearrange("p (j t) -> p j t", t=128)[:, :, 127]),
                         reads=[b_hsc[h]], writes=[b_ev[h]])
                else:
                    P.op("dve", lambda e, h=h: e.tensor_copy(e1s[:, h, :], hlf[:, h, 0:Tw].rearrange("p (r l) -> p r l", l=4)[:, :, 3]),
                         reads=[b_hsc[h]], writes=[b_e1s])
                P.op(veng(), lambda e, h=h: e.tensor_tensor(out=qbv[:, h, :], in0=qbv[:, h, :], in1=hlf[:, h, 0:Tw], op=ALU.mult),
                     reads=[b_hsc[h]], writes=[b_qb[h]])
                P.op("dve", lambda e, h=h: e.tensor_tensor(out=kbv[:, h, :], in0=hkk[:, h, 0:Tw], in1=bbs[:, h, :], op=ALU.mult),
                     reads=[b_hsk[h], b_bbs[h]], writes=[b_kb[h]])

        linear_fm(w_in, 2560, 512, h1, h1b, 8, Tw, cons_hf, fine=True)
        linear_fm(w_in, 2048, 512, h1, h1b, 8, Tw, cons_hq)
        stB1()
        linear_tm(w_in, 1024, h1, h1b, Tw, cons_tm(v_v, b_v))
        stB3()
        linear_fm(w_in, 512, 512, h1, h1b, 8, Tw, cons_rot(kTv, b_kT, 128.0 ** -0.5))
        stB4()
        linear_fm(w_in, 0, 512, h1, h1b, 8, Tw, cons_rot(qTv, b_qT, 1.0))
        flush()
        stB5()
        emit_kd()
        linear_fm(w_in, 1536, 512, h1, h1b, 8, Tw, cons_silu(grv, b_gr))
        stB6()
        linear_tm(w_in, 3072, h1, h1b, Tw, cons_tm(hi_v, b_hi))
        for h in range(4):
            emit_kbT(h)
        linear_fm(w_in, 3584, 512, h1, h1b, 8, Tw, cons_silu(ghv, b_gh))

        P.mark(f"t{ti}:mixers")
        if not sample:
            attvs = [attT[:, :].rearrange("p (j t) -> p j t", t=512), attT2.rearrange("p (j t) -> p j t", t=512)]
            b_atts = [b_attT, b_attT2]
            maskr = cb[:, CB.sl("maskr")].rearrange("p (h x) -> p h x", x=512)
            qdecv = cb[:, CB.sl("qdec")].rearrange("p (h x) -> p h x", x=512)

            def r_att(h):
                attv, b_att = attvs[h % 2], b_atts[h % 2]
                for j in range(4):
                    ps, pb = psum()
                    P.op("pe", lambda e, j=j, ps=ps: e.matmul(ps[:, j * 128:512], kTv[:, h, j * 128:(j + 1) * 128],
                                                              qTv[:, h, j * 128:512], start=True, stop=True),
                         reads=[b_kT[h], b_qT[h]], writes=[pb])
                    P.op("dve", lambda e, j=j, ps=ps: e.tensor_tensor(out=attv[:, j, j * 128:512], in0=ps[:, j * 128:512],
                                                                      in1=maskr[:, h, 0:512 - j * 128], op=ALU.mult),
                         reads=[pb, b_const], writes=[b_att])

            def r_rest(h):
                attv, b_att = attvs[h % 2], b_atts[h % 2]
                if ti > 0:
                    P.op(veng(), lambda e: e.tensor_tensor(out=qd[:, :], in0=qTv[:, h, :], in1=qdecv[:, h, :], op=ALU.mult),
                         reads=[b_qT[h], b_const], writes=[b_qd])
                ops, opb = psum("L")

                def pv(e):
                    ins = None
                    for j in range(4):
                        ins = e.matmul(ops[:, j * 128:512], v_v[:, j, h * 128:(h + 1) * 128], attv[:, j, j * 128:512],
                                       start=(j == 0), stop=(j == 3 and ti == 0))
                    if ti > 0:
                        ins = e.matmul(ops[:, :], S_retb[:, h * 128:(h + 1) * 128], qd[:, :], start=False, stop=True)
                    return ins
                P.op("pe", pv, reads=[b_att, b_qd, b_Srb[h]] + b_v, writes=[opb])
                ups, upb = psum()

                def su(e):
                    ins = None
                    for j in range(4):
                        ins = e.matmul(ups[:, 0:128], kd_v[:, h, j, :], v_v[:, j, h * 128:(h + 1) * 128],
                                       start=(j == 0), stop=(j == 3))
                    return ins
                P.op("pe", su, reads=[b_kd[h]] + b_v, writes=[upb])
                head_norm_gate(ops[:, :], opb, Tw, RGAIN + h, grv[:, h, :], b_gr[h], ofv[:, h, :], ofb[h])
                P.op("dve", lambda e: e.scalar_tensor_tensor(
                    out=S_ret[:, h * 128:(h + 1) * 128], in0=S_ret[:, h * 128:(h + 1) * 128], scalar=GAM[h] ** 512,
                    in1=ups[:, 0:128], op0=ALU.mult, op1=ALU.add), reads=[upb, b_Sr[h]], writes=[b_Sr[h]])
                P.op("act", lambda e: e.activation(out=S_retb[:, h * 128:(h + 1) * 128], in_=S_ret[:, h * 128:(h + 1) * 128],
                                                   func=AF.Copy), reads=[b_Sr[h]], writes=[b_Srb[h]])
            def ret_core():
                r_att(0)
                yield
                for h in range(4):
                    if h < 3:
                        r_att(h + 1)
                        yield
                    r_rest(h)
                    yield
            P.mark(f"t{ti}:hgrn")
            def hg_core():
                cmask = cb[:, CB.sl("cmask")]
                for h in range(4):
                    ev = evv(h)
                    aps, apb = psum()

                    def attm(e, h=h, aps=aps):
                        ins = None
                        for j in range(4):
                            ins = e.matmul(aps[:, j * 128:(j + 1) * 128], kbv[:, h, j * 128:(j + 1) * 128],
                                           qbv[:, h, j * 128:(j + 1) * 128], start=True, stop=True)
                        return ins
                    P.op("pe", attm, reads=[b_kb[h], b_qb[h]], writes=[apb])
                    P.op("dve", lambda e, aps=aps, h=h: e.tensor_tensor(
                        out=attHs[:, 4 * h:4 * h + 4, :], in0=aps[:, :].rearrange("p (j t) -> p j t", t=128),
                        in1=cmask.unsqueeze(1).to_broadcast([128, 4, 128]), op=ALU.mult),
                        reads=[apb, b_const], writes=b_attHs[4 * h:4 * h + 4])
                    ups, upb = psum()

                    def um(e, h=h, ups=ups):
                        ins = None
                        for j in range(4):
                            ins = e.matmul(ups[:, j * 128:(j + 1) * 128], kbT_v[:, h, j, :], hi_v[:, j, h * 128:(h + 1) * 128],
                                           start=True, stop=True)
                        return ins
                    P.op("pe", um, reads=[b_kbT[h]] + b_hi, writes=[upb])
                    for j in range(4):
                        c = h * 4 + j
                        first = (ti == 0 and j == 0)
                        if not first:
                            P.op("act", lambda e, h=h, j=j, ev=ev, c=c: e.activation(
                                out=Sps[:, c, :], in_=S_hg[:, h * 128:(h + 1) * 128], func=AF.Copy, scale=ev[:, 8 + j:9 + j]),
                                reads=[b_Sh[h], b_ev[h]], writes=[b_Sps[c]])
                        P.op("dve", lambda e, h=h, j=j, ev=ev: e.tensor_scalar(
                            out=S_hg[:, h * 128:(h + 1) * 128], in0=S_hg[:, h * 128:(h + 1) * 128], scalar1=ev[:, j:j + 1],
                            scalar2=None, op0=ALU.mult), reads=[b_Sh[h], b_ev[h]], writes=[b_Sh[h]])
                        P.op("dve", lambda e, h=h, j=j, ev=ev, ups=ups: e.scalar_tensor_tensor(
                            out=S_hg[:, h * 128:(h + 1) * 128], in0=ups[:, j * 128:(j + 1) * 128], scalar=ev[:, 4 + j:5 + j],
                            in1=S_hg[:, h * 128:(h + 1) * 128], op0=ALU.mult, op1=ALU.add),
                            reads=[upb, b_Sh[h], b_ev[h]], writes=[b_Sh[h]])
                        if j % 2 == 1:
                            yield
                for hp in range(2):
                    heads = (2 * hp, 2 * hp + 1)
                    for h in heads:
                        ops, opb = psum("L")
                        for j in range(4):
                            c = h * 4 + j
                            first = (ti == 0 and j == 0)

                            def pvh(e, h=h, j=j, ops=ops, c=c, first=first):
                                ins = e.matmul(ops[:, j * 128:(j + 1) * 128], hi_v[:, j, h * 128:(h + 1) * 128], attHs[:, c, :],
                                               start=True, stop=first)
                                if not first:
                                    ins = e.matmul(ops[:, j * 128:(j + 1) * 128], Sps[:, c, :],
                                                   qbv[:, h, j * 128:(j + 1) * 128], start=False, stop=True)
                                return ins
                            P.op("pe", pvh, reads=[b_attHs[c], b_hi[j], b_Sps[c], b_qb[h]], writes=[opb])
                        head_norm_gate(ops[:, :], opb, Tw, HGAIN + h, ghv[:, h, :], b_gh[h], ofv[:, 4 + h, :], ofb[4 + h])
                        yield

            gens = [ret_core(), hg_core()]
            while gens:
                for g_ in list(gens):
                    try:
                        next(g_)
                    except StopIteration:
                        gens.remove(g_)
        else:
            sample_mixers(Tw, qTv, kTv, qbv, kbv, v_v, hi_v, kd_v, kbT_v, grv, ghv, ofv, ofb)

        P.mark(f"t{ti}:w_out")
        nrm = psum("L")
        linear_fm(w_out, 0, D, ofT, ofb, 8, Tw, resid_add(Tw, DEBUG["mixer"], sq=(2,) + nrm), fine=True)
        P.mark(f"t{ti}:xattn")
        rmsnorm(0, G_XA, 2, Tw, pre=nrm)
        xq, xqb = G[2], gbuf[2]
        xqv = fmview(xq, 8, Tw)

        def cons_xq(c, ps, pb):
            P.op("act", lambda e: e.activation(out=xqv[:, c, :], in_=ps[:, 0:Tw], func=AF.Copy), reads=[pb], writes=[xqb[c]])
        linear_fm(w_xq, 0, D, G[0], gbuf[0], 8, Tw, cons_xq, fine=True)
        oa, oab = G[1], gbuf[1]
        oav = fmview(oa, 8, Tw)
        if not sample:
            def x_scores(hh):
                pk = hh % 2
                PTv = PTs[pk][:, :].rearrange("p (m t) -> p m t", t=512)
                for mb in range(2):
                    ps, pb = psum()

                    def sc(e, mb=mb, ps=ps):
                        e.matmul(ps[:, :], KTv[:, 2 * hh, mb * 128:(mb + 1) * 128], xqv[:, 2 * hh, :], start=True, stop=False)
                        return e.matmul(ps[:, :], KTv[:, 2 * hh + 1, mb * 128:(mb + 1) * 128], xqv[:, 2 * hh + 1, :], start=False, stop=True)
                    P.op("pe", sc, reads=[b_KT, xqb[2 * hh], xqb[2 * hh + 1]], writes=[pb])
                    P.op("act", lambda e, mb=mb, ps=ps: e.activation(out=PTv[:, mb, :], in_=ps[:, :], func=AF.Exp, scale=1.0 / 16.0),
                         reads=[pb], writes=[b_PTs[pk]])

            def x_pv(hh):
                pk = hh % 2
                PTv = PTs[pk][:, :].rearrange("p (m t) -> p m t", t=512)
                rd, b_rd = rdens[pk], b_rdens[pk]
                dps, dpb = psum()

                def den(e):
                    e.matmul(dps[:, :], onesb, PTv[:, 0, :], start=True, stop=False)
                    return e.matmul(dps[:, :], onesb, PTv[:, 1, :], start=False, stop=True)
                P.op("pe", den, reads=[b_PTs[pk], b_const], writes=[dpb])
                P.op("act", lambda e: e.activation(out=rd[:, :], in_=dps[:, :], func=AF.Ln), reads=[dpb], writes=[b_rd])
                P.op("act", lambda e: e.activation(out=rd[:, :], in_=rd[:, :], func=AF.Exp, scale=-1.0), reads=[b_rd], writes=[b_rd])
                for i in range(2):
                    ps, pb = psum()

                    def pvx(e, i=i, ps=ps):
                        c0 = hh * 256 + i * 128
                        e.matmul(ps[:, :], Vbv[:, 0, c0:c0 + 128], PTv[:, 0, :], start=True, stop=False)
                        return e.matmul(ps[:, :], Vbv[:, 1, c0:c0 + 128], PTv[:, 1, :], start=False, stop=True)
                    P.op("pe", pvx, reads=[b_PTs[pk], b_Vb], writes=[pb])
                    P.op("dve", lambda e, i=i, ps=ps: e.tensor_tensor(out=oav[:, 2 * hh + i, :], in0=ps[:, :], in1=rd[:, :], op=ALU.mult),
                         reads=[pb, b_rd], writes=[oab[2 * hh + i]])
            x_scores(0)
            for hh in range(4):
                if hh < 3:
                    x_scores(hh + 1)
                x_pv(hh)
        else:
            sample_xattn(xqv, xqb, oav, oab)
        nrm = psum("L")
        linear_fm(w_xo, 0, D, oa, oab, 8, Tw, resid_add(Tw, DEBUG["xattn"], sq=(2,) + nrm), fine=True)
        P.mark(f"t{ti}:ffn")
        rmsnorm(0, G_FFN, 2, Tw, pre=nrm)
        hidv = fmview(hid, 22, Tw)
        sg = tmpA
        for g0 in range(0, DFF, 256):
            (gv, uv), slb = take_slab(w_gate, g0, multi=True)
            for j in range(2):
                hc = g0 // 128 + j
                gps, gpb = psum()
                ups, upb = psum()

                def mmg(e, sl=gv, j=j, ps=gps):
                    ins = None
                    inv = fmview(G[0], 8, Tw)
                    for k in range(8):
                        ins = e.matmul(ps[:, 0:Tw], sl[:, k, j * 128:(j + 1) * 128], inv[:, k, :], start=(k == 0), stop=(k == 7))
                    return ins
                if g0 == 0 and j == 0:
                    inv_ = fmview(G[0], 8, Tw)
                    for (sl_, ps_, pb_) in ((gv, gps, gpb), (uv, ups, upb)):
                        for k in range(8):
                            P.op("pe", lambda e, sl_=sl_, ps_=ps_, k=k: e.matmul(
                                ps_[:, 0:Tw], sl_[:, k, 0:128], inv_[:, k, :], start=(k == 0), stop=(k == 7)),
                                reads=[slb, gbuf[0][k]], writes=[pb_])
                else:
                    P.op("pe", mmg, reads=[slb] + gbuf[0], writes=[gpb])
                    P.op("pe", lambda e, sl=uv, j=j, ps=ups, mmg=mmg: mmg(e, sl, j, ps), reads=[slb] + gbuf[0], writes=[upb])
                P.op("act", lambda e, gps=gps: e.activation(out=sg[:, 0:Tw], in_=gps[:, 0:Tw], func=AF.Silu), reads=[gpb], writes=[b_tA])
                P.op("dve", lambda e, ups=ups, hc=hc: e.tensor_tensor(out=hidv[:, hc, :], in0=sg[:, 0:Tw], in1=ups[:, 0:Tw], op=ALU.mult),
                     reads=[upb, b_tA], writes=[b_hid[hc]])
        nrm = psum("L")
        linear_fm(w_down, 0, D, hid, b_hid, 22, Tw, resid_add(Tw, DEBUG["ffn"], sq=(2,) + nrm))
        P.mark(f"t{ti}:final")
        rmsnorm(0, G_FIN, 2, Tw, out_f32_inplace=True, pre=nrm)
        store_tile(dst_rows, r0, Tw)

    hlf_t = sb("hlf", 4 * 512)
    hk_t = attT
    hlf = hlf_t[:, :].rearrange("p (h t) -> p h t", h=4)
    hkk = hk_t[:, :].rearrange("p (h t) -> p h t", h=4)
    b_hsc = [Buf(f"hsc{h}") for h in range(4)]
    b_hsk = [Buf(f"hsk{h}") for h in range(4)]
    e1s_t = sb("e1s", 64)
    e1s = e1s_t[:, :].rearrange("p (h r) -> p h r", h=4)
    b_e1s = Buf("e1s")

    def tail(t, start, n, dt_bytes_ratio=1):
        return t[:, start:start + n]

    NRING = 3
    S0r = [xT[:, 512 + i * 512:512 + (i + 1) * 512] for i in range(NRING)]
    S0h = [xT[:, 2048 + i * 512:2048 + (i + 1) * 512] for i in range(NRING)]
    S0rb = [hid[:, 1408 + i * 512:1408 + (i + 1) * 512] for i in range(NRING)]
    S0hb = [hid[:, 2944 + i * 512:2944 + (i + 1) * 512] for i in range(NRING)]
    Kbf = [hid[:, 4480 + i * 2048:4480 + (i + 1) * 2048] for i in range(2)]
    KTr = hid[:, 8576:8576 + 2048]
    Vring = [G[1][:, 512:512 + 2048], G[2][:, 512:512 + 2048]]
    PS_s = G[0][:, 512:512 + 512]
    KDr = G[0][:, 1024:1024 + 2048]

    NIN = 5
    S0r = S0r + [qT[:, 256:1280].bitcast(F32), gate_r[:, 256:1280].bitcast(F32)]
    S0h = S0h + [kT[:, 256:1280].bitcast(F32), gate_h[:, 256:1280].bitcast(F32)]
    b_S0 = [Buf(f"S0_{i}") for i in range(NIN)]
    b_S0b = [Buf(f"S0b_{i}") for i in range(NRING)]
    b_Kbf = [Buf("Kbf0"), Buf("Kbf1")]
    b_Vr = [Buf("Vr0"), Buf("Vr1")]
    b_KTr = Buf("KTr"); b_PSs = Buf("PSs"); b_KD = Buf("KD")

    def sample_mixers(Tw, qTv, kTv, qbv, kbv, v_v, hi_v, kd_v, kbT_v, grv, ghv, ofv, ofb):
        msr = cb[0:64, CB.sl("maskSr")].rearrange("p (h x) -> p h x", x=64)
        msh = cb[0:64, CB.sl("maskSh")]
        qds = cb[:, CB.sl("qdecS")].rearrange("p (h x) -> p h x", x=64)
        memb = cb[0:64, CB.sl("memb")]
        decS = cf[:, CF.sl("decS")]
        qdv = attT[:, 0:256].rearrange("p (h t) -> p h t", t=64)
        P.op("dve", lambda e: e.tensor_tensor(out=qdv, in0=qTv, in1=qds, op=ALU.mult), reads=b_qT + [b_const], writes=[b_attT])
        aR = attT[0:64, 256:512].rearrange("p (h t) -> p h t", t=64)
        aH = attT[0:64, 512:768].rearrange("p (h t) -> p h t", t=64)
        oR, oRb = psum("L")
        oH, oHb = psum("L")
        for (isr, av, ov, ob_) in ((True, aR, oR, oRb), (False, aH, oH, oHb)):
            for h in range(4):
                ps, pb = psum()
                if isr:
                    P.op("pe", lambda e, h=h, ps=ps: e.matmul(ps[0:64, 0:64], kTv[:, h, :], qTv[:, h, :], start=True, stop=True),
                         reads=[b_kT[h], b_qT[h]], writes=[pb])
                    P.op("dve", lambda e, h=h, ps=ps, av=av: e.tensor_tensor(out=av[:, h, :], in0=ps[0:64, 0:64], in1=msr[:, h, :], op=ALU.mult),
                         reads=[pb, b_const], writes=[b_aR])
                else:
                    P.op("pe", lambda e, h=h, ps=ps: e.matmul(ps[0:64, 0:64], kbv[:, h, :], qbv[:, h, :], start=True, stop=True),
                         reads=[b_kb[h], b_qb[h]], writes=[pb])
                    P.op("dve", lambda e, h=h, ps=ps, av=av: e.tensor_tensor(out=av[:, h, :], in0=ps[0:64, 0:64], in1=msh, op=ALU.mult),
                         reads=[pb, b_const], writes=[b_aH])
            tok = b_aR if isr else b_aH
            vv = v_v if isr else hi_v
            vb = b_v[0] if isr else b_hi[0]

            def pvs(e, av=av, ov=ov, vv=vv):
                ins = None
                for h in range(4):
                    ins = e.matmul(ov[:, h * 64:(h + 1) * 64], vv[0:64, 0, h * 128:(h + 1) * 128], av[:, h, :], start=(h == 0), stop=False)
                return ins
            P.op("pe", pvs, reads=[tok, vb], writes=[ob_])
        stg = [Kbf[0], Kbf[1], Vring[0], Vring[1]]
        b_stg = [b_Kbf[0], b_Kbf[1], b_Vr[0], b_Vr[1]]

        def s_load(r):
            k = r % NIN
            P.dma_load("sp", b_S0[k], S0r[k].rearrange("p (h e) -> p h e", h=4), sret[r].rearrange("h d e -> d h e"))
            P.dma_load("sp", b_S0[k], S0h[k].rearrange("p (h e) -> p h e", h=4), shg[r].rearrange("h d e -> d h e"))

        b_KD2 = [Buf("KD0"), Buf("KD1")]

        def kd_views(bi):
            KDrv_ = KDr[:, bi * 1024:(bi + 1) * 1024].rearrange("p (h r d) -> p h r d", h=4, r=2)[0:64]
            KDhv_ = KTr[0:64, bi * 1024:(bi + 1) * 1024].rearrange("p (h r d) -> p h r d", h=4, r=2)
            return KDrv_, KDhv_

        def kd_expand(g2):
            bi = g2 % 2
            KDrv_, KDhv_ = kd_views(bi)
            for h in range(4):
                mb_ = memb[:, 2 * g2:2 * g2 + 2].unsqueeze(2).to_broadcast([64, 2, 128])
                P.op("pool", lambda e, h=h, mb_=mb_, KDrv_=KDrv_: e.tensor_tensor(
                    out=KDrv_[:, h, :, :], in0=kd_v[0:64, h, 0, :].unsqueeze(1).to_broadcast([64, 2, 128]), in1=mb_, op=ALU.mult),
                    reads=[b_kd[h], b_const], writes=[b_KD2[bi]])
                P.op("pool", lambda e, h=h, mb_=mb_, KDhv_=KDhv_: e.tensor_tensor(
                    out=KDhv_[:, h, :, :], in0=kbT_v[0:64, h, 0, :].unsqueeze(1).to_broadcast([64, 2, 128]), in1=mb_, op=ALU.mult),
                    reads=[b_kbT[h], b_const], writes=[b_KD2[bi]])

        kd_expand(0)
        for g in range(1):
            for r in range(NREQ):
                g2, rl = r // 2, r % 2
                if rl == 0 and g2 + 1 < NREQ // 2:
                    kd_expand(g2 + 1)
                KDrv, KDhv = kd_views(g2 % 2)
                b_KD = b_KD2[g2 % 2]
                k = r % NIN
                kb = r % NRING
                if r == 0:
                    for r_ in range(NIN - 1):
                        s_load(r_)
                if r + NIN - 1 < NREQ:
                    s_load(r + NIN - 1)
                P.op("act", lambda e, k=k, kb=kb: e.activation(out=S0rb[kb], in_=S0r[k], func=AF.Copy), reads=[b_S0[k]], writes=[b_S0b[kb]])
                P.op("act", lambda e, k=k, kb=kb: e.activation(out=S0hb[kb], in_=S0h[k], func=AF.Copy), reads=[b_S0[k]], writes=[b_S0b[kb]])
                last = (r == NREQ - 1)

                def inter(e, k=kb, r=r, last=last):
                    ins = None
                    for h in range(4):
                        ins = e.matmul(oR[:, h * 64 + 4 * r:h * 64 + 4 * r + 4], S0rb[k][:, h * 128:(h + 1) * 128],
                                       qdv[:, h, 4 * r:4 * r + 4], start=False, stop=(last and h == 3))
                    for h in range(4):
                        ins = e.matmul(oH[:, h * 64 + 4 * r:h * 64 + 4 * r + 4], S0hb[k][:, h * 128:(h + 1) * 128],
                                       qbv[:, h, 4 * r:4 * r + 4], start=False, stop=(last and h == 3))
                    return ins
                P.op("pe", inter, reads=[b_S0b[kb], b_attT] + b_qb, writes=[oRb, oHb])
                uR, uRb = psum()
                uH, uHb = psum()

                def ust(e, rl=rl, uR=uR, uH=uH, KDrv=KDrv, KDhv=KDhv):
                    ins = None
                    for h in range(4):
                        ins = e.matmul(uR[:, h * 128:(h + 1) * 128], KDrv[:, h, rl, :], v_v[0:64, 0, h * 128:(h + 1) * 128], start=True, stop=True)
                    for h in range(4):
                        ins = e.matmul(uH[:, h * 128:(h + 1) * 128], KDhv[:, h, rl, :], hi_v[0:64, 0, h * 128:(h + 1) * 128], start=True, stop=True)
                    return ins
                P.op("pe", ust, reads=[b_KD, b_v[0], b_hi[0]], writes=[uRb, uHb])
                si = r % 4
                sf = stg[si].bitcast(F32)
                Snr, Snh = sf[:, 0:512], sf[:, 512:1024]
                for h in range(4):
                    P.op("dve", lambda e, k=k, uR=uR, Snr=Snr, h=h: e.scalar_tensor_tensor(
                        out=Snr[:, h * 128:(h + 1) * 128], in0=S0r[k][:, h * 128:(h + 1) * 128], scalar=GAM[h] ** 4,
                        in1=uR[:, h * 128:(h + 1) * 128], op0=ALU.mult, op1=ALU.add),
                        reads=[uRb, b_S0[k]], writes=[b_stg[si]])
                e1b = e1s[:, :, r].unsqueeze(2).to_broadcast([128, 4, 128])
                P.op("dve", lambda e, k=k, uH=uH, Snh=Snh: e.tensor_tensor(out=Snh, in0=S0h[k], in1=uH[:, :], op=ALU.add),
                     reads=[uHb, b_S0[k], b_stg[si]], writes=[b_stg[si]])
                P.op("dve", lambda e, e1b=e1b, Snh=Snh: e.tensor_tensor(out=Snh.rearrange("p (h e) -> p h e", h=4),
                                                                        in0=Snh.rearrange("p (h e) -> p h e", h=4), in1=e1b, op=ALU.mult),
                     reads=[b_stg[si], b_e1s], writes=[b_stg[si]])
                P.dma_store("sp", b_stg[si], rets[r].rearrange("h d e -> d h e"), Snr.rearrange("p (h e) -> p h e", h=4))
                P.dma_store("sp", b_stg[si], hgs[r].rearrange("h d e -> d h e"), Snh.rearrange("p (h e) -> p h e", h=4))
        for h in range(4):
            head_norm_gate(oR[:, h * 64:(h + 1) * 64], oRb, Tw, RGAIN + h, grv[:, h, :], b_gr[h], ofv[:, h, :], ofb[h])
        for h in range(4):
            head_norm_gate(oH[:, h * 64:(h + 1) * 64], oHb, Tw, HGAIN + h, ghv[:, h, :], b_gh[h], ofv[:, 4 + h, :], ofb[4 + h])

    def sample_xattn(xqv, xqb, oav, oab):
        KTrv = KTr.rearrange("p (c m) -> p c m", m=256)
        PSv = PS_s.rearrange("p (m r h l) -> p m r h l", m=2, r=16, h=4)
        ops, opb = psum("L")
        opv = ops[:, :].rearrange("p (c t) -> p c t", t=64)
        b_PSr = [Buf(f"PSr{r}") for r in range(NREQ)]
        for r in range(NREQ):
            k = r % 2
            kv = Kbf[k].rearrange("p (b n) -> p b n", n=1024)
            vv = Vring[k].rearrange("p (b n) -> p b n", n=1024)
            P.dma_load("pool", b_Kbf[k], kv, cmk[r].rearrange("(b p) n -> p b n", p=128))
            P.dma_load("pool", b_Vr[k], vv, cmv[r].rearrange("(b p) n -> p b n", p=128))
            for mb in range(2):
                ps, pb = psum()
                psb = ps[:, :].bitcast(BF16)

                def tr(e, mb=mb, kv=kv, psb=psb):
                    ins = None
                    for c in range(8):
                        ins = e.transpose(psb[:, c * 128:(c + 1) * 128], kv[:, mb, c * 128:(c + 1) * 128], identb)
                    return ins
                P.op("pe", tr, reads=[b_Kbf[k], b_const], writes=[pb])
                src = psb[:, :].rearrange("p (c m) -> p c m", m=128)
                dst = KTrv[:, :, mb * 128:(mb + 1) * 128]
                if mb == 0:
                    P.op("act", lambda e, src=src, dst=dst: e.activation(out=dst, in_=src, func=AF.Copy), reads=[pb], writes=[b_KTr])
                else:
                    P.op("dve", lambda e, src=src, dst=dst: e.tensor_copy(dst, src), reads=[pb], writes=[b_KTr])
            sps, spb = psum()
            spv = sps[:, 0:32].rearrange("p (m h l) -> p m h l", m=2, h=4)

            def sc(e, r=r, spv=spv):
                ins = None
                for mb in range(2):
                    for hh in range(4):
                        for i in range(2):
                            ins = e.matmul(spv[:, mb, hh, :], KTrv[:, 2 * hh + i, mb * 128:(mb + 1) * 128],
                                           xqv[:, 2 * hh + i, 4 * r:4 * r + 4], start=(i == 0), stop=(i == 1))
                return ins
            P.op("pe", sc, reads=[b_KTr] + list(xqb), writes=[spb])
            P.op("act", lambda e, r=r, spv=spv: e.activation(out=PSv[:, :, r, :, :], in_=spv, func=AF.Exp, scale=1.0 / 16.0),
                 reads=[spb], writes=[b_PSr[r]])

            def pvx(e, r=r, vv=vv):
                ins = None
                for hh in range(4):
                    for i in range(2):
                        c0 = hh * 256 + i * 128
                        for mb in range(2):
                            ins = e.matmul(opv[:, 2 * hh + i, 4 * r:4 * r + 4], vv[:, mb, c0:c0 + 128], PSv[:, mb, r, hh, :],
                                           start=(mb == 0), stop=(mb == 1))
                return ins
            P.op("pe", pvx, reads=[b_Vr[k], b_PSr[r]], writes=[opb])
        dps, dpb = psum()

        def den(e):
            e.matmul(dps[:, 0:256], onesb, PS_s[:, 0:256], start=True, stop=False)
            return e.matmul(dps[:, 0:256], onesb, PS_s[:, 256:512], start=False, stop=True)
        P.op("pe", den, reads=b_PSr + [b_const], writes=[dpb])
        P.op("act", lambda e: e.activation(out=rden[:, 0:256], in_=dps[:, 0:256], func=AF.Ln), reads=[dpb], writes=[b_rden])
        P.op("act", lambda e: e.activation(out=rden[:, 0:256], in_=rden[:, 0:256], func=AF.Exp, scale=-1.0), reads=[b_rden], writes=[b_rden])
        rdv = rden[:, 0:256].rearrange("p (r h l) -> p r h l", r=16, h=4)
        for hh in range(4):
            in1 = rdv[:, :, hh, :].unsqueeze(1).to_broadcast([128, 2, 16, 4])
            P.op("dve", lambda e, hh=hh, in1=in1: e.tensor_tensor(
                out=oav[:, 2 * hh:2 * hh + 2, :].rearrange("p c (r l) -> p c r l", l=4),
                in0=opv[:, 2 * hh:2 * hh + 2, :].rearrange("p c (r l) -> p c r l", l=4), in1=in1, op=ALU.mult),
                reads=[opb, b_rden], writes=[oab[2 * hh], oab[2 * hh + 1]])

    for ti in range(4):
        run_tile(ti, 512, xp, ti * 512, yp, False)
    P.dma_store("sp", b_Sr, retp.rearrange("h d e -> d h e"), S_ret[:, :].rearrange("p (h e) -> p h e", h=4))
    P.dma_store("sp", b_Sh, hgp.rearrange("h d e -> d h e"), S_hg[:, :].rearrange("p (h e) -> p h e", h=4))
    P.barrier()
    run_tile(4, NS, xs, 0, ys, True)
    assert wstate["taken"] == len(plan), (wstate["taken"], len(plan))
    P.build()
    _CACHE["marks"] = P.mark_instr
    return nc


_CACHE = {}


def kernel(x_prompt, x_sample, mem_prompt, state_ret, state_hgrn, cache_mem_k, cache_mem_v,
           g_mix, w_in, ret_gain, hg_gain, hg_lb, w_out, g_xa, g_mem, w_xq, w_xk, w_xv, w_xo,
           g_ffn, w_gate, w_up, w_down, g_final):
    f = lambda a: np.ascontiguousarray(np.asarray(a, dtype=np.float32))
    if "nc" not in _CACHE:
        _CACHE["nc"] = build_program()
        _CACHE["consts"] = make_consts()
    nc = _CACHE["nc"]
    cf, cb, cs = (a.copy() for a in _CACHE["consts"])

    def fm(v, n):
        return f(v).reshape(n, 128).T
    pv = np.concatenate([fm(g_mix[0], 8), fm(g_xa[0], 8), fm(g_mem[0], 8), fm(g_ffn[0], 8), fm(g_final, 8),
                         fm(ret_gain[0], 4), fm(hg_gain[0], 4), fm(hg_lb[0], 4), fm(hg_lb[1], 4)], axis=1)
    cf[:, CF.sl("pvec")] = pv
    shared = {"w_in": f(w_in[0]), "w_out": f(w_out[0]), "w_xq": f(w_xq[0]), "w_xk": f(w_xk[0]), "w_xv": f(w_xv[0]),
              "w_xo": f(w_xo[0]), "w_gate": f(w_gate[0]), "w_up": f(w_up[0]), "w_down": f(w_down[0]),
              "cf": cf, "cb": cb, "cs": cs}
    in_maps = []
    for c in range(NCORES):
        rs = slice(c * NREQ, (c + 1) * NREQ)
        m = dict(shared)
        m["xp"] = f(x_prompt[c]); m["xs"] = f(x_sample[rs]).reshape(NS, D); m["memp"] = f(mem_prompt[c])
        m["sret"] = f(state_ret[0, rs]); m["shg"] = f(state_hgrn[0, rs])
        m["cmk"] = f(cache_mem_k[0, rs]).reshape(NREQ, NMEM, D); m["cmv"] = f(cache_mem_v[0, rs]).reshape(NREQ, NMEM, D)
        in_maps.append(m)
    res = run_bass_kernel_spmd(nc, in_maps, core_ids=list(range(NCORES)))
    R = res.results
    y_prompt = np.stack([R[c]["yp"] for c in range(NCORES)], 0)
    y_sample = np.concatenate([R[c]["ys"].reshape(NREQ, DEC, D) for c in range(NCORES)], 0)
    ret_p = np.stack([R[c]["retp"] for c in range(NCORES)], 0)[None]
    hg_p = np.stack([R[c]["hgp"] for c in range(NCORES)], 0)[None]
    mk_p = np.stack([R[c]["mkp"].reshape(NMEM, 4, 256) for c in range(NCORES)], 0)[None]
    mv_p = np.stack([R[c]["mvp"].reshape(NMEM, 4, 256) for c in range(NCORES)], 0)[None]
    ret_s = np.concatenate([R[c]["rets"] for c in range(NCORES)], 0)[None]
    hg_s = np.concatenate([R[c]["hgs"] for c in range(NCORES)], 0)[None]
    return (y_prompt.astype(np.float32), y_sample.astype(np.float32), ret_p.astype(np.float32), hg_p.astype(np.float32),
            mk_p.astype(np.float32), mv_p.astype(np.float32), ret_s.astype(np.float32), hg_s.astype(np.float32))
```
